# Optimizing a Trainium2 kernel written in Bass

```python
import math
import jax
import jax.numpy as jnp
from jax import lax
import numpy as np

D_MODEL = 1024
BATCH = 8
SEQ = 8192
DEPTH = 2

GRID_W = 64
CTX_LEN = 256
EPS = 1e-6
LB_FLOOR = 1e-30

D_SSD = 1024
SSD_HEAD_DIM = 64
SSD_HEADS = D_SSD // SSD_HEAD_DIM
SSD_GROUPS = 2
SSD_HPG = SSD_HEADS // SSD_GROUPS
SSD_STATE = 128
SSD_CONV = 5
SSD_CHUNK = 128
SSD_XBC = D_SSD + 2 * SSD_GROUPS * SSD_STATE

D_HG = 512
HG_HEAD_DIM = 128
HG_HEADS = D_HG // HG_HEAD_DIM
HG_CHUNK = 64

D_HY = 512
HY_SHORT = 3
HY_BANDS = 16
HY_EMB = 1 + 2 * HY_BANDS
HY_FILT = 64
HY_MIN_DECAY = math.log(1e-2) / 1.5
HY_MAX_DECAY = math.log(1e-2) / 0.3

D_MIX = D_SSD + D_HG + D_HY
STATE_SIZES = (SSD_XBC, SSD_HEADS, SSD_HEADS, D_HG, D_HG, D_HG)
OUT_SIZES = (D_HG, 3 * D_HY, D_SSD, D_HG, D_HY)
IN_SIZES_ALL = STATE_SIZES + OUT_SIZES
STATE_COLS = sum(STATE_SIZES)
IN_COLS = STATE_COLS + sum(OUT_SIZES)

kernel_name = 'hybrid_ssd_hgrn2_hyena_prefix_dit'


def _offsets(sizes):
    return [int(v) for v in np.cumsum(sizes)[:-1]]


def _rms(x, w):
    xf = x.astype(jnp.float32)
    return xf * lax.rsqrt(jnp.mean(xf * xf, axis=-1, keepdims=True) + EPS) * w.astype(jnp.float32)


def _rmsnorm(x, w):
    return _rms(x, w).astype(x.dtype)


def _dwconv(x, w, b):
    width, ch = w.shape
    y = lax.conv_general_dilated(x, w.astype(x.dtype)[:, None, :], window_strides=(1,),
                                 padding=[(width // 2, width // 2)],
                                 dimension_numbers=('NWC', 'WIO', 'NWC'), feature_group_count=ch)
    return y + b.astype(x.dtype)


def _to_chunks(t, size):
    b, L = t.shape[0], t.shape[1]
    return jnp.moveaxis(t.reshape(b, L // size, size, *t.shape[2:]), 1, 0)


def _from_chunks(t):
    n, b, size = t.shape[0], t.shape[1], t.shape[2]
    return jnp.moveaxis(t, 0, 1).reshape(b, n * size, *t.shape[3:])


def _to_colmajor(t, rows):
    b = t.shape[0]
    return jnp.swapaxes(t.reshape(b, rows, GRID_W, *t.shape[2:]), 1, 2).reshape(t.shape)


def _from_colmajor(t, rows):
    b = t.shape[0]
    return jnp.swapaxes(t.reshape(b, GRID_W, rows, *t.shape[2:]), 1, 2).reshape(t.shape)


def _masked_decay(cum, mask):
    diff = cum[:, :, None] - cum[:, None]
    return jnp.where(mask, jnp.exp(jnp.where(mask, diff, 0.0)), 0.0)


def _ssd_scan(x, dt, a_neg, b_in, state, c_out=None):
    f32 = jnp.float32
    x, dt, b_in = x.astype(f32), dt.astype(f32), b_in.astype(f32)
    a = dt * a_neg.astype(f32)
    mask = jnp.tril(jnp.ones((SSD_CHUNK, SSD_CHUNK), bool))[None, :, :, None, None]

    def step(h, inp):
        if c_out is None:
            xc, dtc, ac, bc = inp
        else:
            xc, dtc, ac, bc, cc = inp
        cum = jnp.cumsum(ac, axis=1)
        total = cum[:, -1]
        xdt = xc * dtc[..., None]
        h_new = (jnp.exp(total)[..., None, None] * h
                 + jnp.einsum('bsgn,bsge,bsgep->bgepn', bc, jnp.exp(total[:, None] - cum), xdt))
        if c_out is None:
            return h_new, None
        decay = _masked_decay(cum, mask)
        scores = jnp.einsum('btgn,bsgn->btsg', cc, bc)
        y = (jnp.einsum('btsg,btsge,bsgep->btgep', scores, decay, xdt)
             + jnp.einsum('btgn,bgepn->btgep', cc, h) * jnp.exp(cum)[..., None])
        return h_new, y

    xs = (x, dt, a, b_in) + (() if c_out is None else (c_out.astype(f32),))
    h, ys = lax.scan(step, state, tuple(_to_chunks(t, SSD_CHUNK) for t in xs))
    return h, (None if c_out is None else _from_chunks(ys))


def _gla_scan(k, v, log_f, state, q=None):
    f32 = jnp.float32
    mask = jnp.tril(jnp.ones((HG_CHUNK, HG_CHUNK), bool))[None, :, :, None, None]

    def step(s, inp):
        if q is None:
            kc, vc, lc = inp
        else:
            kc, vc, lc, qc = inp
        cum = jnp.cumsum(lc, axis=1)
        total = cum[:, -1]
        s_new = (jnp.exp(total)[..., None] * s
                 + jnp.einsum('bshk,bshv->bhkv', kc * jnp.exp(total[:, None] - cum), vc))
        if q is None:
            return s_new, None
        decay = _masked_decay(cum, mask)
        attn = jnp.einsum('bthk,btshk,bshk->btsh', qc, decay, kc)
        o = (jnp.einsum('btsh,bshv->bthv', attn, vc)
             + jnp.einsum('bthk,bhkv->bthv', qc * jnp.exp(cum), s))
        return s_new, o

    xs = (k.astype(f32), v.astype(f32), log_f.astype(f32)) + (() if q is None else (q.astype(f32),))
    s, ys = lax.scan(step, state, tuple(_to_chunks(t, HG_CHUNK) for t in xs))
    return s, (None if q is None else _from_chunks(ys))


def _hgrn_forget(z, lb):
    bsz, L = z.shape[0], z.shape[1]
    zf = z.astype(jnp.float32).reshape(bsz, L, HG_HEADS, HG_HEAD_DIM)
    lb = lb.astype(jnp.float32).reshape(HG_HEADS, HG_HEAD_DIM)
    log_f = jnp.logaddexp(jnp.log(jnp.maximum(lb, LB_FLOOR)), jnp.log1p(-lb) + jax.nn.log_sigmoid(zf))
    k = (1.0 - lb) * jax.nn.sigmoid(-zf)
    return log_f, k


def _hyena_filter(L, p):
    f32 = jnp.float32
    t = jnp.linspace(0.0, 1.0, L, dtype=f32)[:, None]
    ang = ((2.0 * math.pi / L) * jnp.arange(L, dtype=f32)[:, None]
           * jnp.linspace(1e-4, HY_BANDS - 1, HY_BANDS, dtype=f32)[None, :])
    z = jnp.concatenate([t, jnp.cos(ang), -jnp.sin(ang)], axis=-1)
    freq = p['hy_filt_freq'].astype(f32)
    hdn = jnp.sin(freq * (z @ p['hy_filt_w1'].astype(f32) + p['hy_filt_b1'].astype(f32)))
    hdn = jnp.sin(freq * (hdn @ p['hy_filt_w2'].astype(f32) + p['hy_filt_b2'].astype(f32)))
    h = (hdn @ p['hy_filt_w3'].astype(f32)).reshape(L, 2, D_HY)
    deltas = jnp.abs(jnp.linspace(HY_MIN_DECAY, HY_MAX_DECAY, D_HY, dtype=f32))
    h = h * jnp.exp(-t * deltas)[:, None, :]
    kern = jnp.concatenate([h[:, 0], jnp.zeros((1, D_HY), f32), h[:0:-1, 1]], axis=0)
    return kern / (jnp.sum(jnp.abs(kern), axis=0, keepdims=True) + EPS)


def _fft_conv(u, kern):
    L = u.shape[1]
    uf = jnp.fft.rfft(u, n=2 * L, axis=1)
    kf = jnp.fft.rfft(kern, n=2 * L, axis=0)
    return jnp.fft.irfft(uf * kf, n=2 * L, axis=1)[:, :L]


def _token_mixers(u, p, lb, state_in, rows, with_output):
    f32 = jnp.float32
    bsz, L = u.shape[0], u.shape[1]
    parts = jnp.split(u, _offsets(IN_SIZES_ALL if with_output else STATE_SIZES), axis=-1)
    xbc, dtf_raw, dtb_raw, hg_i, hg_ff, hg_fb = parts[:6]
    rev = lambda t: jnp.flip(t, axis=1)

    xbc = jax.nn.silu(_dwconv(xbc, p['ssd_conv_w'], p['ssd_conv_b']))
    xs, b_in, c_out = jnp.split(xbc, [D_SSD, D_SSD + SSD_GROUPS * SSD_STATE], axis=-1)
    xs = xs.reshape(bsz, L, SSD_GROUPS, SSD_HPG, SSD_HEAD_DIM)
    b_in = b_in.reshape(bsz, L, SSD_GROUPS, SSD_STATE)
    c_out = c_out.reshape(bsz, L, SSD_GROUPS, SSD_STATE)
    a_neg = -jnp.exp(p['ssd_a_log'].astype(f32)).reshape(2, SSD_GROUPS, SSD_HPG)
    dt_f = jax.nn.softplus((dtf_raw + p['ssd_dt_bias'][0]).astype(f32)).reshape(bsz, L, SSD_GROUPS, SSD_HPG)
    dt_b = jax.nn.softplus((dtb_raw + p['ssd_dt_bias'][1]).astype(f32)).reshape(bsz, L, SSD_GROUPS, SSD_HPG)
    ssd_hf, ssd_yf = _ssd_scan(xs, dt_f, a_neg[0], b_in, state_in[0], c_out if with_output else None)
    ssd_hb, ssd_yb = _ssd_scan(rev(xs), rev(dt_b), a_neg[1], rev(b_in), state_in[1],
                               rev(c_out) if with_output else None)

    order = (lambda t: _to_colmajor(t, rows)) if rows is not None else (lambda t: t)
    lf_f, k_f = _hgrn_forget(hg_ff, lb[0])
    lf_b, k_b = _hgrn_forget(hg_fb, lb[1])
    v_hg = order(hg_i.reshape(bsz, L, HG_HEADS, HG_HEAD_DIM))
    lf_f, k_f, lf_b, k_b = (order(t) for t in (lf_f, k_f, lf_b, k_b))
    q_hg = order(parts[6].reshape(bsz, L, HG_HEADS, HG_HEAD_DIM)) if with_output else None
    hg_hf, hg_of = _gla_scan(k_f, v_hg, lf_f, state_in[2], q_hg)
    hg_hb, hg_ob = _gla_scan(rev(k_b), rev(v_hg), rev(lf_b), state_in[3],
                             rev(q_hg) if with_output else None)
    states = (ssd_hf, ssd_hb, hg_hf, hg_hb)
    if not with_output:
        return states, None
    hy_vxx, ssd_z, hg_g, hy_g = parts[7:]

    y_a = ssd_yf + rev(ssd_yb) + p['ssd_d'].astype(f32).reshape(SSD_GROUPS, SSD_HPG, 1) * xs.astype(f32)
    y_a = y_a.reshape(bsz, L, SSD_GROUPS, SSD_HPG * SSD_HEAD_DIM)
    y_a = y_a * jax.nn.silu(ssd_z.astype(f32)).reshape(bsz, L, SSD_GROUPS, SSD_HPG * SSD_HEAD_DIM)
    y_a = _rms(y_a, p['ssd_norm_w'].reshape(SSD_GROUPS, -1)).reshape(bsz, L, D_SSD)

    o = hg_of + rev(hg_ob)
    if rows is not None:
        o = _from_colmajor(o, rows)
    y_b = _rms(o, p['hg_norm_w'].reshape(HG_HEADS, HG_HEAD_DIM)).reshape(bsz, L, D_HG)
    y_b = y_b * jax.nn.silu(hg_g.astype(f32))

    hv, hx0, hx1 = jnp.split(_dwconv(hy_vxx, p['hy_conv_w'], p['hy_conv_b']), 3, axis=-1)
    w = (hx1 * hv).astype(f32)
    y_c = hx0.astype(f32) * (_fft_conv(w, _hyena_filter(L, p)) + p['hy_bias'].astype(f32) * w)
    y_c = y_c * jax.nn.silu(hy_g.astype(f32))

    return states, jnp.concatenate([y_a, y_b, y_c], axis=-1).astype(u.dtype)


def setup_inputs(seed: int = 0) -> dict:
    key = jax.random.key(seed)
    ks = jax.random.split(key, 28)
    f32 = jnp.float32
    nrm = lambda k, shape, s: s * jax.random.normal(k, shape, f32)
    dt = jnp.exp(jax.random.uniform(ks[10], (DEPTH, 2, SSD_HEADS), f32, math.log(1e-3), math.log(1e-1)))
    return {
        'x': nrm(ks[0], (BATCH, SEQ, D_MODEL), 1.0),
        'c': nrm(ks[1], (BATCH, D_MODEL), 1.0),
        'ctx': nrm(ks[2], (BATCH, CTX_LEN, D_MODEL), 1.0),
        'c_ctx': nrm(ks[3], (D_MODEL,), 1.0),
        'ada_w': nrm(ks[4], (DEPTH, D_MODEL, 3 * D_MODEL), 0.5 * D_MODEL ** -0.5),
        'ada_b': nrm(ks[5], (DEPTH, 3 * D_MODEL), 0.02),
        'norm_w': 1.0 + nrm(ks[6], (DEPTH, D_MODEL), 0.05),
        'w_in': nrm(ks[7], (DEPTH, D_MODEL, IN_COLS), D_MODEL ** -0.5),
        'ssd_conv_w': nrm(ks[8], (DEPTH, SSD_CONV, SSD_XBC), SSD_CONV ** -0.5),
        'ssd_conv_b': nrm(ks[9], (DEPTH, SSD_XBC), 0.02),
        'ssd_dt_bias': dt + jnp.log(-jnp.expm1(-dt)),
        'ssd_a_log': jnp.log(jax.random.uniform(ks[11], (DEPTH, 2, SSD_HEADS), f32, 1.0, 16.0)),
        'ssd_d': 1.0 + nrm(ks[12], (DEPTH, SSD_HEADS), 0.1),
        'ssd_norm_w': 1.0 + nrm(ks[13], (DEPTH, D_SSD), 0.05),
        'hg_lb_logits': 1.0 + nrm(ks[14], (2, DEPTH, D_HG), 0.5),
        'hg_norm_w': 1.0 + nrm(ks[15], (DEPTH, D_HG), 0.05),
        'hy_conv_w': nrm(ks[16], (DEPTH, HY_SHORT, 3 * D_HY), HY_SHORT ** -0.5),
        'hy_conv_b': nrm(ks[17], (DEPTH, 3 * D_HY), 0.02),
        'hy_filt_w1': nrm(ks[18], (DEPTH, HY_EMB, HY_FILT), HY_EMB ** -0.5),
        'hy_filt_b1': nrm(ks[19], (DEPTH, HY_FILT), 0.1),
        'hy_filt_w2': nrm(ks[20], (DEPTH, HY_FILT, HY_FILT), HY_FILT ** -0.5),
        'hy_filt_b2': nrm(ks[21], (DEPTH, HY_FILT), 0.1),
        'hy_filt_freq': 1.0 + nrm(ks[22], (DEPTH, HY_FILT), 0.1),
        'hy_filt_w3': nrm(ks[23], (DEPTH, HY_FILT, 2 * D_HY), HY_FILT ** -0.5),
        'hy_bias': nrm(ks[24], (DEPTH, D_HY), 1.0),
        'w_out': nrm(ks[25], (DEPTH, D_MIX, D_MODEL), D_MIX ** -0.5),
        'final_norm_w': 1.0 + nrm(ks[26], (D_MODEL,), 0.05),
    }


def reference(x, c, ctx, c_ctx, ada_w, ada_b, norm_w, w_in, ssd_conv_w, ssd_conv_b, ssd_dt_bias,
              ssd_a_log, ssd_d, ssd_norm_w, hg_lb_logits, hg_norm_w, hy_conv_w, hy_conv_b,
              hy_filt_w1, hy_filt_b1, hy_filt_w2, hy_filt_b2, hy_filt_freq, hy_filt_w3, hy_bias,
              w_out, final_norm_w):
    f32 = jnp.float32
    bsz, seq = x.shape[0], x.shape[1]
    rows = seq // GRID_W
    lb_p = jax.nn.softmax(hg_lb_logits.astype(f32), axis=1)
    lb_all = jnp.cumsum(lb_p, axis=1) - lb_p[:, :1]
    zero_states = ((jnp.zeros((bsz, SSD_GROUPS, SSD_HPG, SSD_HEAD_DIM, SSD_STATE), f32),) * 2
                   + (jnp.zeros((bsz, HG_HEADS, HG_HEAD_DIM, HG_HEAD_DIM), f32),) * 2)
    s_lat = jax.nn.silu(c)
    s_ctx = jax.nn.silu(c_ctx)
    xc = ctx
    for l in range(DEPTH):
        last = l == DEPTH - 1
        p = {'ssd_conv_w': ssd_conv_w[l], 'ssd_conv_b': ssd_conv_b[l], 'ssd_dt_bias': ssd_dt_bias[l],
             'ssd_a_log': ssd_a_log[l], 'ssd_d': ssd_d[l], 'ssd_norm_w': ssd_norm_w[l],
             'hg_norm_w': hg_norm_w[l], 'hy_conv_w': hy_conv_w[l], 'hy_conv_b': hy_conv_b[l],
             'hy_filt_w1': hy_filt_w1[l], 'hy_filt_b1': hy_filt_b1[l], 'hy_filt_w2': hy_filt_w2[l],
             'hy_filt_b2': hy_filt_b2[l], 'hy_filt_freq': hy_filt_freq[l], 'hy_filt_w3': hy_filt_w3[l],
             'hy_bias': hy_bias[l]}
        shift, scale, gate = jnp.split(s_lat @ ada_w[l] + ada_b[l], 3, axis=-1)
        shift_c, scale_c, gate_c = jnp.split(s_ctx @ ada_w[l] + ada_b[l], 3, axis=-1)
        hc = _rmsnorm(xc, norm_w[l]) * (1.0 + scale_c) + shift_c
        w_in_c = w_in[l][:, :STATE_COLS] if last else w_in[l]
        ctx_states, yc = _token_mixers(hc @ w_in_c, p, lb_all[:, l], zero_states, None, not last)
        h = _rmsnorm(x, norm_w[l]) * (1.0 + scale[:, None]) + shift[:, None]
        _, y = _token_mixers(h @ w_in[l], p, lb_all[:, l], ctx_states, rows, True)
        x = x + gate[:, None] * (y @ w_out[l])
        if not last:
            xc = xc + gate_c * (yc @ w_out[l])
    return _rmsnorm(x, final_norm_w)
```

```python
import numpy as np
import contextlib
import concourse.bass as bass
import concourse.mybir as mybir
from concourse.bass_utils import run_bass_kernel_spmd

F32 = mybir.dt.float32
BF16 = mybir.dt.bfloat16
AF = mybir.ActivationFunctionType
ALU = mybir.AluOpType


class Tk:
    __slots__ = ("w", "r", "multi")

    def __init__(self, multi=False):
        self.w = {}
        self.r = {}
        self.multi = multi


class FW:
    EPOCH = 30000

    def __init__(self, nc):
        self.nc = nc
        self.eng = {"pe": nc.tensor, "dve": nc.vector, "act": nc.scalar, "pool": nc.gpsimd, "sp": nc.sync}
        self.stack = contextlib.ExitStack()
        self.sem = {}
        self.cnt = {}
        self.semobj = {}
        self.waited = {e: {} for e in self.eng}
        self.nsem = 0
        for e in self.eng:
            self._newsem(e)
        self.dring = []
        for i in range(12):
            s = self._alloc_sem("dma%d" % i)
            self.dring.append([s, 0])
        self.dpos = 0
        self.ninst = 0

    def _alloc_sem(self, name):
        s = self.stack.enter_context(self.nc.semaphore(name))
        self.nsem += 1
        self.semobj[id(s)] = s
        return s

    def _newsem(self, e):
        self.sem[e] = self._alloc_sem("s_%s_%d" % (e, self.nsem))
        self.cnt[e] = 0

    def _deps(self, reads, writes):
        d = {}

        def add(dd):
            for k, v in dd.items():
                if d.get(k, 0) < v:
                    d[k] = v
        for t in reads:
            add(t.w)
        for t in writes:
            if not t.multi:
                add(t.w)
            add(t.r)
        return d

    def _wait(self, e, deps, skip_own=False):
        w = self.waited[e]
        own = id(self.sem[e])
        for k, v in deps.items():
            if skip_own and k == own:
                continue
            if w.get(k, 0) >= v:
                continue
            self.eng[e].wait_ge(self.semobj[k], v)
            w[k] = v

    def _record(self, tok, reads, writes):
        k, v = tok
        for t in reads:
            if t.r.get(k, 0) < v:
                t.r[k] = v
        for t in writes:
            if t.multi:
                if t.w.get(k, 0) < v:
                    t.w[k] = v
            else:
                t.w = {k: v}
                t.r = {}

    def op(self, e, fn, reads=(), writes=()):
        deps = self._deps(reads, writes)
        self._wait(e, deps, skip_own=(e == "pe"))
        if self.cnt[e] >= self.EPOCH:
            self._newsem(e)
        inst = fn()
        inst.then_inc(self.sem[e], 1)
        self.cnt[e] += 1
        self.ninst += 1
        self._record((id(self.sem[e]), self.cnt[e]), reads, writes)
        return inst

    def dma(self, out, in_, reads=(), writes=(), q="sp", **kw):
        deps = self._deps(reads, writes)
        slot = self.dring[self.dpos % len(self.dring)]
        self.dpos += 1
        if slot[1] > 0:
            deps[id(slot[0])] = max(deps.get(id(slot[0]), 0), slot[1])
        self._wait(q, deps)
        if slot[1] >= 16 * 1800:
            slot[0] = self._alloc_sem("dmax%d" % self.nsem)
            slot[1] = 0
        inst = self.eng[q].dma_start(out=out, in_=in_, **kw)
        slot[1] += 16
        inst.then_inc(slot[0], 16)
        self.ninst += 1
        self._record((id(slot[0]), slot[1]), reads, writes)
        return inst

    def barrier(self):
        allt = {}
        for e in self.eng:
            if self.cnt[e] > 0:
                allt[id(self.sem[e])] = self.cnt[e]
        for s, v in self.dring:
            if v > 0:
                allt[id(s)] = v
        for e in self.eng:
            self._wait(e, dict(allt))

    def finish(self):
        self.barrier()
        self.stack.close()


class Pool:
    N = 0

    def __init__(self, fw):
        self.fw = fw
        self.stack = contextlib.ExitStack()
        self.n = 0

    def sb(self, shape, dt, name=None):
        Pool.N += 1
        t = self.stack.enter_context(self.fw.nc.sbuf_tensor(name or ("sb_%d" % Pool.N), list(shape), dt))
        return t, Tk()

    def ps(self, shape, dt=F32, name=None):
        Pool.N += 1
        t = self.stack.enter_context(self.fw.nc.psum_tensor(name or ("ps_%d" % Pool.N), list(shape), dt))
        return t, Tk()

    def close(self):
        self.fw.barrier()
        self.stack.close()
import math

AX = mybir.AxisListType
I32 = mybir.dt.int32
D = 1024
KC = 8
EPS = 1e-6
GW = 64
O_XBC, O_DTF, O_DTB, O_HGI, O_HFF, O_HFB, O_HQ, O_HY, O_Z, O_HGG, O_HYG = 0, 1536, 1552, 1568, 2080, 2592, 3104, 3616, 5152, 6176, 6688
IN_COLS = 7200
HY_MIN_DECAY = math.log(1e-2) / 1.5
HY_MAX_DECAY = math.log(1e-2) / 0.3


def bc(ap, shape):
    return ap.to_broadcast(list(shape))


class Seq:
    pass


class Builder:
    def __init__(self, L, LC, depth, debug=()):
        self.L, self.LC, self.depth = L, LC, depth
        self.debug = set(debug)
        self.nc = bass.Bass("TRN2", target_bir_lowering=False)
        self.fw = FW(self.nc)
        self.inputs = {}
        self.outputs = []

    def din(self, name, shape, dt=F32):
        t = self.nc.dram_tensor(name, list(shape), dt, kind="ExternalInput").ap()
        self.inputs[name] = t
        return t

    def scr(self, name, shape, dt):
        kind = "ExternalOutput" if name in self.debug else "Internal"
        t = self.nc.dram_tensor(name, list(shape), dt, kind=kind).ap()
        if name in self.debug:
            self.outputs.append(name)
        return t, Tk(multi=True)

    def tt(self, e, out, in0, in1, op, R, W):
        en = self.fw.eng[e]
        return self.fw.op(e, lambda: en.tensor_tensor(out=out, in0=in0, in1=in1, op=op), R, W)

    def ts(self, e, out, in0, s1, s2, op0, op1, R, W):
        en = self.fw.eng[e]
        if op1 is None:
            return self.fw.op(e, lambda: en.tensor_scalar(out=out, in0=in0, scalar1=s1, scalar2=None, op0=op0), R, W)
        return self.fw.op(e, lambda: en.tensor_scalar(out=out, in0=in0, scalar1=s1, scalar2=s2, op0=op0, op1=op1), R, W)

    def stt(self, out, in0, scalar, in1, op0, op1, R, W):
        nc = self.nc
        return self.fw.op("dve", lambda: nc.vector.scalar_tensor_tensor(out=out, in0=in0, scalar=scalar, in1=in1, op0=op0, op1=op1), R, W)

    def act(self, out, in_, func, R, W, bias=None, scale=None, accum_out=None):
        nc = self.nc
        kw = {}
        if bias is not None:
            kw["bias"] = bias
        if scale is not None:
            kw["scale"] = scale
        if accum_out is not None:
            kw["accum_out"] = accum_out
            npart = accum_out.shape[0]

            def fn():
                nc.scalar.activation(out=out, in_=in_, func=func, **kw)
                return nc.scalar.copy(out=self.dummy[0:npart, 0:1], in_=accum_out)
            return self.fw.op("act", fn, R, W)
        return self.fw.op("act", lambda: nc.scalar.activation(out=out, in_=in_, func=func, **kw), R, W)

    def mm(self, out, lhsT, rhs, start, stop, R, W):
        nc = self.nc
        return self.fw.op("pe", lambda: nc.tensor.matmul(out, lhsT, rhs, start=start, stop=stop), R, W)

    def tr(self, out, in_, ident, R, W):
        nc = self.nc
        return self.fw.op("pe", lambda: nc.tensor.transpose(out, in_, ident), R, W)

    def cp(self, e, out, in_, R, W):
        if e == "act":
            nc = self.nc
            return self.fw.op("act", lambda: nc.scalar.copy(out=out, in_=in_), R, W)
        en = self.fw.eng[e]
        return self.fw.op(e, lambda: en.tensor_copy(out=out, in_=in_), R, W)

    def ms(self, e, ap, val, W):
        en = self.fw.eng[e]
        return self.fw.op(e, lambda: en.memset(ap, val), (), W)

    def load_w(self, P, dst, dk, wsrc, ncols):
        TP = Pool(self.fw)
        st = [TP.sb([128, 8, 256], F32) for _ in range(2)]
        src = wsrc.rearrange("(k p) n -> p k n", p=128)
        i = 0
        for c0 in range(0, ncols, 256):
            n = min(256, ncols - c0)
            s, sk = st[i % 2]
            self.fw.dma(s[:, :, 0:n], src[:, :, c0:c0 + n], writes=[sk])
            self.cp("pool" if i % 2 else "dve", dst[:, :, c0:c0 + n], s[:, :, 0:n], [sk], [dk])
            i += 1
        TP.close()

    def setup(self):
        L, LC, dp = self.L, self.LC, self.depth
        self.x_in = self.din("x", [L, D])
        self.ctx_in = self.din("ctx", [LC, D])
        self.c2_in = self.din("c2", [128, 8, 2])
        self.ada_w = self.din("ada_w", [dp, D, 3 * D])
        self.ada_b2 = self.din("ada_b2", [dp, 2, 3 * D])
        self.norm_w2 = self.din("norm_w2", [dp, 2, D])
        self.fnw_bc = self.din("fnw_bc", [128, D])
        self.w_in = self.din("w_in", [dp, D, IN_COLS])
        self.w_out = self.din("w_out", [dp, 2 * D, D])
        self.cw_fm = self.din("cw_fm", [dp, 128, 12, 5])
        self.cb_fm = self.din("cb_fm", [dp, 128, 12])
        self.hcw_fm = self.din("hcw_fm", [dp, 128, 12, 3])
        self.hcb_fm = self.din("hcb_fm", [dp, 128, 12])
        self.dtb_fm = self.din("dtb_fm", [dp, 32, 1])
        self.alog_fm = self.din("alog_fm", [dp, 32, 1])
        self.d_bc = self.din("d_bc", [dp, 128, 16])
        self.snw_bc = self.din("snw_bc", [dp, 128, D])
        self.lbl_fm = self.din("lbl_fm", [128, 2, dp, 4])
        self.hnw_bc = self.din("hnw_bc", [dp, 128, 512])
        self.hyb_bc = self.din("hyb_bc", [dp, 128, 512])
        self.fw1 = self.din("fw1", [dp, 33, 64])
        self.fw2 = self.din("fw2", [dp, 64, 64])
        self.fw3 = self.din("fw3", [dp, 64, 1024])
        self.fvec = self.din("fvec", [dp, 64, 3])
        self.c_id32 = self.din("c_id32", [128, 128])
        self.c_idbf = self.din("c_idbf", [128, 128], BF16)
        self.c_mask = self.din("c_mask", [2, 128, 128])
        self.c_sel = self.din("c_sel", [2, 256])
        self.c_rst = self.din("c_rst", [128, 2, 512])
        self.c_delta = self.din("c_delta", [128, 512])
        self.c_z = {}
        self.c_tneg = {}
        for Ls in sorted(set([L, LC])):
            self.c_z[Ls] = self.din("c_z%d" % Ls, [33, 2 * Ls])
            self.c_tneg[Ls] = self.din("c_tneg%d" % Ls, [128, 2 * Ls // 128])
        self.out = self.nc.dram_tensor("out", [L, D], F32, kind="ExternalOutput").ap()
        self.outk = Tk(multi=True)
        self.ink = Tk(multi=True)
        self.GP = Pool(self.fw)
        G = self.GP
        self.id32, self.id32k = G.sb([128, 128], F32)
        self.idbf, self.idbfk = G.sb([128, 128], BF16)
        self.mask, self.maskk = G.sb([128, 2, 128], F32)
        self.rst, self.rstk = G.sb([128, 2, 512], F32)
        self.gsh, self.gshk = G.sb([128, 8, 2, 2], F32)
        self.gate, self.gatek = G.sb([128, 2, D], F32)
        self.zero, self.zerok = G.sb([128, 160], BF16)
        self.dummy, self.dummyk = G.sb([128, 4], F32)
        fw = self.fw
        fw.dma(self.id32[:], self.c_id32, writes=[self.id32k])
        fw.dma(self.idbf[:], self.c_idbf, writes=[self.idbfk])
        fw.dma(self.mask[:], self.c_mask.rearrange("a s t -> s a t"), writes=[self.maskk])
        fw.dma(self.rst[:], self.c_rst, writes=[self.rstk])
        self.ms("dve", self.zero[:], 0.0, [self.zerok])
        self.seqs = []
        for nm, Ls, xin in (("c", LC, self.ctx_in), ("l", L, self.x_in)):
            s = Seq()
            s.name, s.L, s.is_lat = nm, Ls, nm == "l"
            s.T = min(512, Ls)
            s.x, s.xk = xin, self.ink
            s.hT = {"rm": self.scr("hT_" + nm, [D, Ls + 4], BF16)}
            if s.is_lat:
                s.hT["cm"] = self.scr("hTc_" + nm, [D, Ls + 4], BF16)
            else:
                s.hT["cm"] = s.hT["rm"]
            s.xbcT = self.scr("xbcT_" + nm, [1536, Ls], BF16)
            s.dta = self.scr("dta_" + nm, [32, 2, Ls], F32)
            s.ysd = [self.scr("ysd%d_%s" % (d, nm), [Ls, D], F32) for d in range(2)]
            s.od = [self.scr("od%d_%s" % (d, nm), [Ls, 512], F32) for d in range(2)]
            s.y = self.scr("y_" + nm, [Ls, 2 * D], BF16)
            s.wT = self.scr("wT_" + nm, [512, Ls + 256], BF16)
            s.hx0g = self.scr("hx0g_" + nm, [Ls, 512], F32)
            s.hbw = self.scr("hbw_" + nm, [Ls, 512], F32)
            s.x1 = self.scr("x1_" + nm, [Ls, D], F32)
            s.kmat = self.scr("kmat_" + nm, [8, 128, 2 * Ls // 128, 64], BF16)
            s.rn = self.scr("rn_" + nm, [128, 512], F32)
            self.seqs.append(s)
        self.cumsc = [self.scr("cumsc%d" % d, [16, 512], F32) for d in range(2)]
        for s in self.seqs:
            for key in (("rm", "cm") if s.is_lat else ("rm",)):
                hd, hk = s.hT[key]
                v = hd.rearrange("(k p) t -> p k t", p=128)
                fw.dma(v[:, :, 0:2], self.zero[:, 0:16].rearrange("p (k t) -> p k t", k=8), reads=[self.zerok], writes=[hk])
                fw.dma(v[:, :, s.L + 2:s.L + 4], self.zero[:, 0:16].rearrange("p (k t) -> p k t", k=8), reads=[self.zerok], writes=[hk])
            wd, wk = s.wT
            for c0 in range(0, 512, 128):
                fw.dma(wd[c0:c0 + 128, 0:127], self.zero[:, 0:127], reads=[self.zerok], writes=[wk])
                fw.dma(wd[c0:c0 + 128, 127 + s.L:s.L + 256], self.zero[:, 0:129], reads=[self.zerok], writes=[wk])

    def p_adaln(self, l):
        fw, nc = self.fw, self.nc
        P = Pool(fw)
        c2, c2k = P.sb([128, 8, 2], F32)
        wst = [P.sb([128, 8, 512], F32) for _ in range(2)]
        mod, modk = P.sb([2, 3 * D], F32)
        adab, adabk = P.sb([2, 3 * D], F32)
        nw2, nw2k = P.sb([2, D], F32)
        grow, growk = P.sb([2, 2, D], F32)
        sel, selk = P.sb([2, 256], F32)
        ps, psk = P.ps([128, 512])
        pst, pstk = P.ps([128, 512])
        fw.dma(c2[:], self.c2_in, writes=[c2k])
        fw.dma(adab[:], self.ada_b2[l], writes=[adabk])
        fw.dma(nw2[:], self.norm_w2[l], writes=[nw2k])
        fw.dma(sel[:], self.c_sel, writes=[selk])
        self.act(c2[:], c2[:], AF.Silu, [c2k], [c2k])
        wv = self.ada_w[l].rearrange("(p k) n -> p k n", k=8)
        for cb in range(6):
            w, wk = wst[cb % 2]
            fw.dma(w[:], wv[:, :, cb * 512:(cb + 1) * 512], writes=[wk])
            for k in range(8):
                self.mm(ps[0:2, :], c2[:, k, :], w[:, k, :], k == 0, k == 7, [c2k, wk], [psk])
            self.tt("dve", mod[:, cb * 512:(cb + 1) * 512], ps[0:2, :], adab[:, cb * 512:(cb + 1) * 512], ALU.add, [psk, adabk], [modk])
        self.stt(grow[:, 0, :], mod[:, D:2 * D], 1.0, nw2[:], ALU.add, ALU.mult, [modk, nw2k], [growk])
        self.cp("dve", grow[:, 1, :], mod[:, 0:D], [modk], [growk])
        for k in range(8):
            for w_ in range(2):
                o = (k * 2 + w_) * 2
                self.tr(pst[:, o:o + 2], grow[0:2, w_, k * 128:(k + 1) * 128], self.id32[0:2, 0:2], [growk, self.id32k], [pstk])
        self.cp("dve", self.gsh[:].rearrange("p k a b -> p (k a b)"), pst[:, 0:32], [pstk], [self.gshk])
        for r in range(2):
            for h in range(2):
                self.mm(ps[:, :], sel[0:2, r * 128:(r + 1) * 128], mod[0:2, 2 * D + h * 512:2 * D + (h + 1) * 512], True, True, [selk, modk], [psk])
                self.cp("dve", self.gate[:, r, h * 512:(h + 1) * 512], ps[:, :], [psk], [self.gatek])
        P.close()

    def x_tile_src(self, seq, xap, i, order):
        if order == "rm" or not seq.is_lat:
            return [(0, 128, xap[i * 128:(i + 1) * 128, :])]
        rows = seq.L // GW
        v = xap.rearrange("(r c) d -> c r d", c=GW)
        if rows >= 128:
            assert rows == 128
            return [(0, 128, v[i])]
        ncol = 128 // rows
        return [(j * rows, rows, v[i * ncol + j]) for j in range(ncol)]

    def p_norm(self, seq, xap, xk, order):
        fw, nc = self.fw, self.nc
        row = 0 if seq.is_lat else 1
        hd, hk = seq.hT[order]
        hv = hd.rearrange("(k p) t -> p k t", p=128)
        P = Pool(fw)
        xt = [P.sb([128, D], F32) for _ in range(2)]
        junk, junkk = P.sb([128, D], BF16)
        ss = [P.sb([128, 2], F32) for _ in range(2)]
        xn = [P.sb([128, D], F32) for _ in range(2)]
        tmp, tmpk = P.sb([128, 8, 128], F32)
        hs = [P.sb([128, 8, 512], BF16) for _ in range(2)]
        ps = [P.ps([128, 1024]) for _ in range(2)]
        nt = seq.L // 128
        T = seq.T
        per = T // 128
        for i in range(nt):
            x_, x_k = xt[i % 2]
            for (p0, n, src) in self.x_tile_src(seq, xap, i, order):
                fw.dma(x_[p0:p0 + n, :], src, reads=[xk], writes=[x_k])
            s_, s_k = ss[i % 2]
            self.act(junk[:], x_[:], AF.Square, [x_k], [junkk, s_k], accum_out=s_[:, 0:1])
            self.ts("dve", s_[:, 1:2], s_[:, 0:1], 1.0 / D, EPS, ALU.mult, ALU.add, [s_k], [s_k])
            self.act(s_[:, 1:2], s_[:, 1:2], AF.Sqrt, [s_k], [s_k])
            fw.op("dve", lambda: nc.vector.reciprocal(out=s_[:, 1:2], in_=s_[:, 1:2]), [s_k], [s_k])
            n_, n_k = xn[i % 2]
            self.act(n_[:], x_[:], AF.Copy, [x_k, s_k], [n_k], scale=s_[:, 1:2])
            p_, p_k = ps[i % 2]
            for k in range(8):
                self.tr(p_[:, k * 128:(k + 1) * 128], n_[:, k * 128:(k + 1) * 128], self.id32[:], [n_k, self.id32k], [p_k])
            h_, h_k = hs[(i // per) % 2]
            j = i % per
            self.tt("dve", tmp[:], p_[:].rearrange("p (k t) -> p k t", k=8), bc(self.gsh[:, :, 0, row:row + 1], [128, 8, 128]), ALU.mult, [p_k, self.gshk], [tmpk])
            self.tt("pool", h_[:, :, j * 128:(j + 1) * 128], tmp[:], bc(self.gsh[:, :, 1, row:row + 1], [128, 8, 128]), ALU.add, [tmpk, self.gshk], [h_k])
            if j == per - 1:
                t0 = (i // per) * T
                fw.dma(hv[:, :, 2 + t0:2 + t0 + T], h_[:, :, 0:T], reads=[h_k], writes=[hk])
        P.close()

    def p_ssd_front(self, seq, l):
        fw, nc = self.fw, self.nc
        P = Pool(fw)
        W, Wk = P.sb([128, 8, 1568], BF16)
        self.load_w(P, W, Wk, self.w_in[l][:, O_XBC:O_XBC + 1568], 1568)
        cw, cwk = P.sb([128, 12, 5], F32)
        cb, cbk = P.sb([128, 12], F32)
        dtb, dtbk = P.sb([32, 1], F32)
        aneg, anegk = P.sb([32, 1], F32)
        fw.dma(cw[:], self.cw_fm[l], writes=[cwk])
        fw.dma(cb[:], self.cb_fm[l], writes=[cbk])
        fw.dma(dtb[:], self.dtb_fm[l], writes=[dtbk])
        fw.dma(aneg[:], self.alog_fm[l], writes=[anegk])
        self.act(aneg[:], aneg[:], AF.Exp, [anegk], [anegk])
        self.ts("dve", aneg[:], aneg[:], -1.0, None, ALU.mult, None, [anegk], [anegk])
        T = seq.T
        hw = [P.sb([128, 8, T + 4], BF16) for _ in range(2)]
        ps = [P.ps([128, 1024]) for _ in range(2)]
        acc = [P.sb([128, T], F32) for _ in range(2)]
        xst = [P.sb([128, 12, T], BF16) for _ in range(2)]
        psd, psdk = P.ps([128, 512])
        dst = [P.sb([32, 2, T], F32) for _ in range(2)]
        hd, hk = seq.hT["rm"]
        hv = hd.rearrange("(k p) t -> p k t", p=128)
        xd, xdk = seq.xbcT
        xdv = xd.rearrange("(b p) t -> p b t", p=128)
        dd, ddk = seq.dta
        nsup = seq.L // T
        for si in range(nsup):
            t0 = si * T
            h_, h_k = hw[si % 2]
            fw.dma(h_[:], hv[:, :, t0:t0 + T + 4], reads=[hk], writes=[h_k])
            x_, x_k = xst[si % 2]
            for b in range(12):
                p_, p_k = ps[b % 2]
                for k in range(8):
                    self.mm(p_[:, 0:T], W[:, k, b * 128:(b + 1) * 128], h_[:, k, 0:T], k == 0, k == 7, [Wk, h_k], [p_k])
                for k in range(8):
                    self.mm(p_[:, T:T + 4], W[:, k, b * 128:(b + 1) * 128], h_[:, k, T:T + 4], k == 0, k == 7, [Wk, h_k], [p_k])
                a_, a_k = acc[b % 2]
                self.act(a_[:], p_[:, 0:T], AF.Identity, [p_k, cwk, cbk], [a_k], scale=cw[:, b, 0:1], bias=cb[:, b:b + 1])
                for k in range(1, 5):
                    self.stt(a_[:], p_[:, k:k + T], cw[:, b, k:k + 1], a_[:], ALU.mult, ALU.add, [p_k, cwk, a_k], [a_k])
                self.act(x_[:, b, :], a_[:], AF.Silu, [a_k], [x_k])
            fw.dma(xdv[:, :, t0:t0 + T], x_[:], reads=[x_k], writes=[xdk])
            for k in range(8):
                self.mm(psd[0:32, 0:T], W[:, k, 1536:1568], h_[:, k, 2:2 + T], k == 0, k == 7, [Wk, h_k], [psdk])
            d_, d_k = dst[si % 2]
            self.act(d_[:, 0, :], psd[0:32, 0:T], AF.Exp, [psdk, dtbk], [d_k], bias=dtb[:, 0:1])
            self.act(d_[:, 0, :], d_[:, 0, :], AF.Ln, [d_k], [d_k], bias=1.0)
            self.ts("dve", d_[:, 1, :], d_[:, 0, :], aneg[:, 0:1], None, ALU.mult, None, [d_k, anegk], [d_k])
            fw.dma(dd[:, :, t0:t0 + T], d_[:], reads=[d_k], writes=[ddk])
        P.close()

    def p_ssd_scan(self, l, d):
        fw, nc = self.fw, self.nc
        P = Pool(fw)
        S, Sk = P.sb([128, 16, 64], F32)
        Sbf, Sbfk = P.sb([128, 1024], BF16)
        self.ms("dve", S[:], 0.0, [Sk])
        self.ms("pool", Sbf[:], 0.0, [Sbfk])
        drow, drowk = P.sb([128, 16], F32)
        fw.dma(drow[:], self.d_bc[l], writes=[drowk])
        xs2 = [P.sb([128, 12, 512], BF16) for _ in range(2)]
        da2 = [P.sb([16, 2, 512], F32) for _ in range(2)]
        cumT2 = [P.sb([16, 512], F32) for _ in range(2)]
        R2 = [P.sb([128, 16, 128], F32) for _ in range(2)]
        ctok, ctokk = P.sb([128, 48], F32)
        Dm, Dmk = P.sb([128, 16, 128], F32)
        dec, deck = P.sb([128, 16, 128], F32)
        sm, smk = P.sb([128, 2, 128], F32)
        MT, MTk = P.sb([128, 16, 128], BF16)
        Btok, Btokk = P.sb([128, 256], BF16)
        xdt, xdtk = P.sb([128, 16, 64], BF16)
        xsD, xsDk = P.sb([128, 16, 64], BF16)
        xdtw, xdtwk = P.sb([128, 16, 64], BF16)
        yi, yik = P.sb([128, 16, 64], F32)
        yo2 = [P.sb([128, 1024], F32) for _ in range(2)]
        etot, etotk = P.sb([128, 16], F32)
        tot, totk = P.sb([16, 8], F32)
        pst, pstk = P.ps([128, 512])
        pss, pssk = P.ps([128, 512])
        psx, psxk = P.ps([128, 1024], BF16)
        psb, psbk = P.ps([128, 1024], BF16)
        psy, psyk = P.ps([128, 1024])
        psi, psik = P.ps([128, 1024])
        cd, cdk = self.cumsc[d]
        mk = self.mask[:, d, :]
        gi = 0
        for seq in self.seqs:
            T = seq.T
            nch = T // 128
            nsup = seq.L // T
            xd, xdk = seq.xbcT
            xdv = xd.rearrange("(b p) t -> p b t", p=128)
            dd, ddk = seq.dta
            yd, ydk = seq.ysd[d]
            sups = range(nsup) if d == 0 else range(nsup - 1, -1, -1)
            for si in sups:
                t0 = si * T
                xs, xsk = xs2[gi % 2]
                da, dak = da2[gi % 2]
                cumT, cumTk = cumT2[gi % 2]
                gi += 1
                fw.dma(xs[:, :, 0:T], xdv[:, :, t0:t0 + T], reads=[xdk], writes=[xsk])
                fw.dma(da[:, :, 0:T], dd[d * 16:(d + 1) * 16, :, t0:t0 + T], reads=[ddk], writes=[dak])
                fw.op("dve", lambda: nc.vector.tensor_tensor_scan(out=cumT[:, 0:T], data0=self.rst[0:16, 0, 0:T], data1=da[:, 1, 0:T], initial=0.0, op0=ALU.mult, op1=ALU.add), [self.rstk, dak], [cumTk])
                if d == 1:
                    c3 = cumT[:, 0:T].rearrange("p (c t) -> p c t", t=128)
                    self.cp("dve", tot[:, 0:nch], cumT[:, 127:T:128], [cumTk], [totk])
                    self.tt("dve", c3, bc(tot[:, 0:nch].unsqueeze(2), [16, nch, 128]), c3, ALU.subtract, [cumTk, totk], [cumTk])
                    self.tt("dve", cumT[:, 0:T], cumT[:, 0:T], da[:, 1, 0:T], ALU.add, [cumTk, dak], [cumTk])
                fw.dma(cd[:, 0:T], cumT[:, 0:T], reads=[cumTk], writes=[cdk])
                chs = range(nch) if d == 0 else range(nch - 1, -1, -1)
                for ci in chs:
                    c0 = ci * 128
                    R, Rk = R2[ci % 2]
                    rsrc = bass.AP(tensor=cd.tensor, offset=c0, ap=[[0, 128], [512, 16], [1, 128]])
                    fw.dma(R[:], rsrc, reads=[cdk], writes=[Rk])
                    self.tr(pst[:, 0:16], cumT[:, c0:c0 + 128], self.id32[0:16, 0:16], [cumTk, self.id32k], [pstk])
                    self.tr(pst[:, 16:32], da[:, 0, c0:c0 + 128], self.id32[0:16, 0:16], [dak, self.id32k], [pstk])
                    self.cp("act", ctok[:, 0:32], pst[:, 0:32], [pstk], [ctokk])
                    self.act(ctok[:, 32:48], ctok[:, 0:16], AF.Exp, [ctokk], [ctokk])
                    for e in range(16):
                        self.ts("dve", Dm[:, e, :], R[:, e, :], ctok[:, e:e + 1], 0.0, ALU.subtract, ALU.min, [Rk, ctokk], [Dmk])
                    self.act(dec[:], Dm[:], AF.Exp, [Dmk], [deck])
                    for g in range(2):
                        self.mm(pss[:, g * 128:(g + 1) * 128], xs[:, 8 + g, c0:c0 + 128], xs[:, 10 + g, c0:c0 + 128], True, True, [xsk], [pssk])
                    self.tt("dve", sm[:], pss[:, 0:256].rearrange("p (g t) -> p g t", g=2), bc(mk.unsqueeze(1), [128, 2, 128]), ALU.mult, [pssk, self.maskk], [smk])
                    for g in range(2):
                        self.tt("pool" if g else "dve", MT[:, g * 8:(g + 1) * 8, :], dec[:, g * 8:(g + 1) * 8, :], bc(sm[:, g:g + 1, :], [128, 8, 128]), ALU.mult, [deck, smk], [MTk])
                    for b in range(8):
                        self.tr(psx[:, b * 128:(b + 1) * 128], xs[:, b, c0:c0 + 128], self.idbf[:], [xsk, self.idbfk], [psxk])
                    for g in range(2):
                        self.tr(psb[:, g * 128:(g + 1) * 128], xs[:, 8 + g, c0:c0 + 128], self.idbf[:], [xsk, self.idbfk], [psbk])
                    self.cp("act", Btok[:], psb[:, 0:256], [psbk], [Btokk])
                    px3 = psx[:].rearrange("p (e q) -> p e q", e=16)
                    self.tt("dve", xdt[:], px3, bc(ctok[:, 16:32].unsqueeze(2), [128, 16, 64]), ALU.mult, [psxk, ctokk], [xdtk])
                    if d == 0:
                        self.tt("dve", xsD[:], px3, bc(drow[:].unsqueeze(2), [128, 16, 64]), ALU.mult, [psxk, drowk], [xsDk])
                        for g in range(2):
                            self.mm(psy[:, g * 512:(g + 1) * 512], self.idbf[:], xsD[:, g * 8:(g + 1) * 8, :].rearrange("p e q -> p (e q)"), True, False, [self.idbfk, xsDk], [psyk])
                    for ge in range(16):
                        st = (d == 1) and (ge % 8 == 0)
                        self.mm(psy[:, ge * 64:(ge + 1) * 64], MT[:, ge, :], xdt[:, ge, :], st, ge % 8 == 7, [MTk, xdtk], [psyk])
                    for g in range(2):
                        self.mm(psi[:, g * 512:(g + 1) * 512], xs[:, 10 + g, c0:c0 + 128], Sbf[:, g * 512:(g + 1) * 512], True, True, [xsk, Sbfk], [psik])
                    self.tt("dve", yi[:], psi[:].rearrange("p (e q) -> p e q", e=16), bc(ctok[:, 32:48].unsqueeze(2), [128, 16, 64]), ALU.mult, [psik, ctokk], [yik])
                    yo, yok = yo2[ci % 2]
                    self.tt("dve", yo[:], psy[:], yi[:].rearrange("p e q -> p (e q)"), ALU.add, [psyk, yik], [yok])
                    fw.dma(yd[t0 + c0:t0 + c0 + 128, :], yo[:], reads=[yok], writes=[ydk])
                    col = 127 if d == 0 else 0
                    self.tt("pool", xdtw[:], xdt[:], bc(dec[:, :, col:col + 1], [128, 16, 64]), ALU.mult, [xdtk, deck], [xdtwk])
                    for g in range(2):
                        self.mm(psi[:, g * 512:(g + 1) * 512], Btok[:, g * 128:(g + 1) * 128], xdtw[:, g * 8:(g + 1) * 8, :].rearrange("p e q -> p (e q)"), True, True, [Btokk, xdtwk], [psik])
                    self.act(etot[:], R[:, :, col], AF.Exp, [Rk], [etotk])
                    self.tt("pool", S[:], S[:], bc(etot[:].unsqueeze(2), [128, 16, 64]), ALU.mult, [Sk, etotk], [Sk])
                    self.tt("dve", S[:], S[:], psi[:].rearrange("p (e q) -> p e q", e=16), ALU.add, [Sk, psik], [Sk])
                    self.cp("act", Sbf[:], S[:].rearrange("p e q -> p (e q)"), [Sk], [Sbfk])
        P.close()

    def p_ssd_comb(self, seq, l):
        fw, nc = self.fw, self.nc
        P = Pool(fw)
        W, Wk = P.sb([128, 8, 1024], BF16)
        self.load_w(P, W, Wk, self.w_in[l][:, O_Z:O_Z + 1024], 1024)
        nw, nwk = P.sb([128, 1024], F32)
        fw.dma(nw[:], self.snw_bc[l], writes=[nwk])
        T = seq.T
        hw = [P.sb([128, 8, T], BF16) for _ in range(2)]
        yf = [P.sb([128, 1024], F32) for _ in range(2)]
        yb = [P.sb([128, 1024], F32) for _ in range(2)]
        sz, szk = P.sb([128, 1024], F32)
        junk, junkk = P.sb([128, 512], BF16)
        ss, ssk = P.sb([128, 4], F32)
        yo = [P.sb([128, 1024], BF16) for _ in range(2)]
        ps = [P.ps([128, 1024]) for _ in range(2)]
        hd, hk = seq.hT["rm"]
        hv = hd.rearrange("(k p) t -> p k t", p=128)
        yd, ydk = seq.y
        per = T // 128
        for i in range(seq.L // 128):
            h_, h_k = hw[(i // per) % 2]
            if i % per == 0:
                fw.dma(h_[:], hv[:, :, 2 + i * 128:2 + i * 128 + T], reads=[hk], writes=[h_k])
            j = i % per
            f_, f_k = yf[i % 2]
            b_, b_k = yb[i % 2]
            fw.dma(f_[:], seq.ysd[0][0][i * 128:(i + 1) * 128, :], reads=[seq.ysd[0][1]], writes=[f_k])
            fw.dma(b_[:], seq.ysd[1][0][i * 128:(i + 1) * 128, :], reads=[seq.ysd[1][1]], writes=[b_k])
            p_, p_k = ps[i % 2]
            for h in range(2):
                for k in range(8):
                    self.mm(p_[:, h * 512:(h + 1) * 512], h_[:, k, j * 128:(j + 1) * 128], W[:, k, h * 512:(h + 1) * 512], k == 0, k == 7, [h_k, Wk], [p_k])
            self.act(sz[:], p_[:], AF.Silu, [p_k], [szk])
            self.tt("pool", f_[:], f_[:], b_[:], ALU.add, [f_k, b_k], [f_k])
            self.tt("dve", f_[:], f_[:], sz[:], ALU.mult, [f_k, szk], [f_k])
            for g in range(2):
                self.act(junk[:], f_[:, g * 512:(g + 1) * 512], AF.Square, [f_k], [junkk, ssk], accum_out=ss[:, g:g + 1])
            self.ts("dve", ss[:, 2:4], ss[:, 0:2], 1.0 / 512, EPS, ALU.mult, ALU.add, [ssk], [ssk])
            self.act(ss[:, 2:4], ss[:, 2:4], AF.Sqrt, [ssk], [ssk])
            fw.op("dve", lambda: nc.vector.reciprocal(out=ss[:, 2:4], in_=ss[:, 2:4]), [ssk], [ssk])
            o_, o_k = yo[i % 2]
            for g in range(2):
                self.stt(o_[:, g * 512:(g + 1) * 512], f_[:, g * 512:(g + 1) * 512], ss[:, 2 + g:3 + g], nw[:, g * 512:(g + 1) * 512], ALU.mult, ALU.mult, [f_k, ssk, nwk], [o_k])
            fw.dma(yd[i * 128:(i + 1) * 128, 0:1024], o_[:], reads=[o_k], writes=[ydk])
            yield
        P.close()

    def p_out(self, seq, l, xap, xk, dst, dstk, final):
        fw, nc = self.fw, self.nc
        row = 0 if seq.is_lat else 1
        P = Pool(fw)
        W, Wk = P.sb([128, 16, 1024], BF16)
        st = [P.sb([128, 16, 128], F32) for _ in range(2)]
        wsrc = self.w_out[l].rearrange("(k p) n -> p k n", p=128)
        for i, c0 in enumerate(range(0, 1024, 128)):
            s, sk = st[i % 2]
            fw.dma(s[:], wsrc[:, :, c0:c0 + 128], writes=[sk])
            self.cp("pool" if i % 2 else "dve", W[:, :, c0:c0 + 128], s[:], [sk], [Wk])
        fnw, fnwk = P.sb([128, 1024], F32)
        fw.dma(fnw[:], self.fnw_bc, writes=[fnwk])
        yt = [P.sb([128, 2048], BF16) for _ in range(2)]
        yT, yTk = P.sb([128, 16, 128], BF16)
        xt = [P.sb([128, 1024], F32) for _ in range(2)]
        xo = [P.sb([128, 1024], F32) for _ in range(2)]
        junk, junkk = P.sb([128, 1024], BF16)
        ss, ssk = P.sb([128, 2], F32)
        pT, pTk = P.ps([128, 2048], BF16)
        ps = [P.ps([128, 1024]) for _ in range(2)]
        yd, ydk = seq.y
        for i in range(seq.L // 128):
            y_, y_k = yt[i % 2]
            x_, x_k = xt[i % 2]
            fw.dma(y_[:], yd[i * 128:(i + 1) * 128, :], reads=[ydk], writes=[y_k])
            fw.dma(x_[:], xap[i * 128:(i + 1) * 128, :], reads=[xk], writes=[x_k])
            for k in range(16):
                self.tr(pT[:, k * 128:(k + 1) * 128], y_[:, k * 128:(k + 1) * 128], self.idbf[:], [y_k, self.idbfk], [pTk])
            self.cp("act", yT[:].rearrange("p k t -> p (k t)"), pT[:], [pTk], [yTk])
            p_, p_k = ps[i % 2]
            for h in range(2):
                for k in range(16):
                    self.mm(p_[:, h * 512:(h + 1) * 512], yT[:, k, :], W[:, k, h * 512:(h + 1) * 512], k == 0, k == 15, [yTk, Wk], [p_k])
            o_, o_k = xo[i % 2]
            self.tt("dve", o_[:], p_[:], self.gate[:, row, :], ALU.mult, [p_k, self.gatek], [o_k])
            self.tt("pool", o_[:], o_[:], x_[:], ALU.add, [o_k, x_k], [o_k])
            if final:
                self.act(junk[:], o_[:], AF.Square, [o_k], [junkk, ssk], accum_out=ss[:, 0:1])
                self.ts("dve", ss[:, 1:2], ss[:, 0:1], 1.0 / D, EPS, ALU.mult, ALU.add, [ssk], [ssk])
                self.act(ss[:, 1:2], ss[:, 1:2], AF.Sqrt, [ssk], [ssk])
                fw.op("dve", lambda: nc.vector.reciprocal(out=ss[:, 1:2], in_=ss[:, 1:2]), [ssk], [ssk])
                self.stt(o_[:], o_[:], ss[:, 1:2], fnw[:], ALU.mult, ALU.mult, [o_k, ssk, fnwk], [o_k])
            fw.dma(dst[i * 128:(i + 1) * 128, :], o_[:], reads=[o_k], writes=[dstk])
        P.close()

    def p_gla_scan(self, l, d):
        fw, nc = self.fw, self.nc
        P = Pool(fw)
        Wf, Wfk = P.sb([128, 8, 512], BF16)
        Wq, Wqk = P.sb([128, 8, 512], BF16)
        Wv, Wvk = P.sb([128, 8, 512], BF16)
        self.load_w(P, Wf, Wfk, self.w_in[l][:, (O_HFF if d == 0 else O_HFB):(O_HFF if d == 0 else O_HFB) + 512], 512)
        self.load_w(P, Wq, Wqk, self.w_in[l][:, O_HQ:O_HQ + 512], 512)
        self.load_w(P, Wv, Wvk, self.w_in[l][:, O_HGI:O_HGI + 512], 512)
        lbl, lblk = P.sb([128, 2, self.depth, 4], F32)
        lb, lbk = P.sb([128, 8], F32)
        fw.dma(lbl[:], self.lbl_fm, writes=[lblk])
        if l == 0:
            self.ms("dve", lb[:, 0:4], 0.0, [lbk])
        else:
            self.tt("dve", lb[:, 0:4], lbl[:, d, 1, :], lbl[:, d, 0, :], ALU.subtract, [lblk], [lbk])
            self.act(lb[:, 0:4], lb[:, 0:4], AF.Sigmoid, [lbk], [lbk])
        self.ts("dve", lb[:, 4:8], lb[:, 0:4], -1.0, 1.0, ALU.mult, ALU.add, [lbk], [lbk])
        S, Sk = P.sb([128, 4, 128], F32)
        Sbf, Sbfk = P.sb([128, 4, 128], BF16)
        self.ms("dve", S[:], 0.0, [Sk])
        self.ms("pool", Sbf[:], 0.0, [Sbfk])
        hw2 = [P.sb([128, 8, 512], BF16) for _ in range(2)]
        f_, fk = P.sb([128, 512], F32)
        lf, lfk = P.sb([128, 512], F32)
        kk, kkk = P.sb([128, 512], F32)
        cum, cumk = P.sb([128, 512], F32)
        E, Ek = P.sb([128, 512], F32)
        ex, exk = P.sb([128, 4, 512], F32)
        ref, refk = P.sb([128, 16], F32)
        etot, etotk = P.sb([128, 4, 8], F32)
        qt, qtk = P.sb([128, 4, 512], BF16)
        qp, qpk = P.sb([128, 4, 512], BF16)
        kt, ktk = P.sb([128, 4, 512], BF16)
        kh, khk = P.sb([128, 4, 512], BF16)
        vtok, vtokk = P.sb([64, 512], BF16)
        khtok, khtokk = P.sb([64, 4, 128], BF16)
        am, amk = P.sb([64, 4, 64], BF16)
        amf, amfk = P.sb([64, 256], F32)
        ost2 = [P.sb([64, 512], F32) for _ in range(2)]
        psf, psfk = P.ps([128, 512])
        psq, psqk = P.ps([128, 512])
        psv, psvk = psf, psfk
        pskh, pskhk = P.ps([128, 1024], BF16)
        psa, psak = P.ps([128, 512])
        pso, psok = P.ps([128, 512])
        psS, psSk = psq, psqk
        mk = self.mask[0:64, d, 0:64]
        gi = 0
        for seq in self.seqs:
            T = seq.T
            nch = T // 64
            nsup = seq.L // T
            hd, hk = seq.hT["cm"]
            hv = hd.rearrange("(k p) t -> p k t", p=128)
            od, odk = seq.od[d]
            sups = range(nsup) if d == 0 else range(nsup - 1, -1, -1)
            for si in sups:
                t0 = si * T
                hw, hwk = hw2[gi % 2]
                gi += 1
                fw.dma(hw[:, :, 0:T], hv[:, :, 2 + t0:2 + t0 + T], reads=[hk], writes=[hwk])
                for h in range(4):
                    for k in range(8):
                        self.mm(psf[:, 0:T], Wf[:, k, h * 128:(h + 1) * 128], hw[:, k, 0:T], k == 0, k == 7, [Wfk, hwk], [psfk])
                    for k in range(8):
                        self.mm(psq[:, 0:T], Wq[:, k, h * 128:(h + 1) * 128], hw[:, k, 0:T], k == 0, k == 7, [Wqk, hwk], [psqk])
                    self.act(f_[:, 0:T], psf[:, 0:T], AF.Sigmoid, [psfk], [fk])
                    self.ts("dve", f_[:, 0:T], f_[:, 0:T], lb[:, 4 + h:5 + h], lb[:, h:h + 1], ALU.mult, ALU.add, [fk, lbk], [fk])
                    self.act(lf[:, 0:T], f_[:, 0:T], AF.Ln, [fk], [lfk])
                    self.ts("pool", kk[:, 0:T], f_[:, 0:T], -1.0, 1.0, ALU.mult, ALU.add, [fk], [kkk])
                    fw.op("dve", lambda: nc.vector.tensor_tensor_scan(out=cum[:, 0:T], data0=self.rst[:, 1, 0:T], data1=lf[:, 0:T], initial=0.0, op0=ALU.mult, op1=ALU.add), [self.rstk, lfk], [cumk])
                    c3 = cum[:, 0:T].rearrange("p (c t) -> p c t", t=64)
                    if d == 1:
                        self.cp("dve", ref[:, 8:8 + nch], cum[:, 63:T:64], [cumk], [refk])
                        self.tt("dve", c3, bc(ref[:, 8:8 + nch].unsqueeze(2), [128, nch, 64]), c3, ALU.subtract, [refk, cumk], [cumk])
                        self.tt("dve", cum[:, 0:T], cum[:, 0:T], lf[:, 0:T], ALU.add, [cumk, lfk], [cumk])
                    tcol = 63 if d == 0 else 0
                    self.cp("dve", ref[:, 0:nch], cum[:, 32:T:64], [cumk], [refk])
                    self.cp("dve", ref[:, 8:8 + nch], cum[:, tcol:T:64], [cumk], [refk])
                    E3 = E[:, 0:T].rearrange("p (c t) -> p c t", t=64)
                    self.tt("dve", E3, c3, bc(ref[:, 0:nch].unsqueeze(2), [128, nch, 64]), ALU.subtract, [cumk, refk], [Ek])
                    self.act(ex[:, 0, 0:T], E[:, 0:T], AF.Exp, [Ek], [exk])
                    self.act(ex[:, 1, 0:T], E[:, 0:T], AF.Exp, [Ek], [exk], scale=-1.0)
                    self.act(ex[:, 2, 0:T], cum[:, 0:T], AF.Exp, [cumk], [exk])
                    self.tt("dve", E3, bc(ref[:, 8:8 + nch].unsqueeze(2), [128, nch, 64]), c3, ALU.subtract, [cumk, refk], [Ek])
                    self.act(ex[:, 3, 0:T], E[:, 0:T], AF.Exp, [Ek], [exk])
                    self.act(etot[:, h, 0:nch], ref[:, 8:8 + nch], AF.Exp, [refk], [etotk])
                    self.tt("dve", qt[:, h, 0:T], psq[:, 0:T], ex[:, 0, 0:T], ALU.mult, [psqk, exk], [qtk])
                    self.tt("dve", qp[:, h, 0:T], psq[:, 0:T], ex[:, 2, 0:T], ALU.mult, [psqk, exk], [qpk])
                    self.tt("pool", kt[:, h, 0:T], kk[:, 0:T], ex[:, 1, 0:T], ALU.mult, [kkk, exk], [ktk])
                    self.tt("pool", kh[:, h, 0:T], kk[:, 0:T], ex[:, 3, 0:T], ALU.mult, [kkk, exk], [khk])
                    yield
                chs = range(nch) if d == 0 else range(nch - 1, -1, -1)
                for ci in chs:
                    c0 = ci * 64
                    for k in range(8):
                        self.mm(psv[0:64, :], hw[:, k, c0:c0 + 64], Wv[:, k, :], k == 0, k == 7, [hwk, Wvk], [psvk])
                    self.cp("act", vtok[:], psv[0:64, :], [psvk], [vtokk])
                    yield
                    for h in range(4):
                        self.tr(pskh[0:64, h * 128:(h + 1) * 128], kh[:, h, c0:c0 + 64], self.idbf[:], [khk, self.idbfk], [pskhk])
                    self.cp("act", khtok[:].rearrange("p h k -> p (h k)"), pskh[0:64, 0:512], [pskhk], [khtokk])
                    yield
                    for h in range(4):
                        self.mm(psa[0:64, h * 64:(h + 1) * 64], kt[:, h, c0:c0 + 64], qt[:, h, c0:c0 + 64], True, True, [ktk, qtk], [psak])
                    self.ts("dve", amf[:], psa[0:64, 0:256], 1.0e30, -1.0e30, ALU.min, ALU.max, [psak], [amfk])
                    self.tt("dve", am[:], amf[:].rearrange("p (h t) -> p h t", h=4), bc(mk.unsqueeze(1), [64, 4, 64]), ALU.mult, [amfk, self.maskk], [amk])
                    yield
                    for h in range(4):
                        self.mm(pso[0:64, h * 128:(h + 1) * 128], am[:, h, :], vtok[:, h * 128:(h + 1) * 128], True, False, [amk, vtokk], [psok])
                        self.mm(pso[0:64, h * 128:(h + 1) * 128], qp[:, h, c0:c0 + 64], Sbf[:, h, :], False, True, [qpk, Sbfk], [psok])
                    ost, ostk = ost2[ci % 2]
                    self.cp("act", ost[:], pso[0:64, :], [psok], [ostk])
                    fw.dma(od[t0 + c0:t0 + c0 + 64, :], ost[:], reads=[ostk], writes=[odk])
                    yield
                    for h in range(4):
                        self.mm(psS[:, h * 128:(h + 1) * 128], khtok[:, h, :], vtok[:, h * 128:(h + 1) * 128], True, True, [khtokk, vtokk], [psSk])
                    yield
                    for h in range(4):
                        self.stt(S[:, h, :], S[:, h, :], etot[:, h, ci:ci + 1], psS[:, h * 128:(h + 1) * 128], ALU.mult, ALU.add, [Sk, etotk, psSk], [Sk])
                    self.cp("act", Sbf[:], S[:], [Sk], [Sbfk])
        P.close()

    def p_gla_comb(self, seq, l):
        fw, nc = self.fw, self.nc
        P = Pool(fw)
        W, Wk = P.sb([128, 8, 512], BF16)
        self.load_w(P, W, Wk, self.w_in[l][:, O_HGG:O_HGG + 512], 512)
        nw, nwk = P.sb([128, 512], F32)
        fw.dma(nw[:], self.hnw_bc[l], writes=[nwk])
        T = seq.T
        hw = [P.sb([128, 8, T], BF16) for _ in range(2)]
        of = [P.sb([128, 512], F32) for _ in range(2)]
        ob = [P.sb([128, 512], F32) for _ in range(2)]
        sg, sgk = P.sb([128, 512], F32)
        junk, junkk = P.sb([128, 128], BF16)
        ss, ssk = P.sb([128, 8], F32)
        yo = [P.sb([128, 512], BF16) for _ in range(2)]
        ps = [P.ps([128, 512]) for _ in range(2)]
        hd, hk = seq.hT["cm"]
        hv = hd.rearrange("(k p) t -> p k t", p=128)
        yd, ydk = seq.y
        per = T // 128
        for i in range(seq.L // 128):
            h_, h_k = hw[(i // per) % 2]
            if i % per == 0:
                fw.dma(h_[:], hv[:, :, 2 + i * 128:2 + i * 128 + T], reads=[hk], writes=[h_k])
            j = i % per
            f_, f_k = of[i % 2]
            b_, b_k = ob[i % 2]
            fw.dma(f_[:], seq.od[0][0][i * 128:(i + 1) * 128, :], reads=[seq.od[0][1]], writes=[f_k])
            fw.dma(b_[:], seq.od[1][0][i * 128:(i + 1) * 128, :], reads=[seq.od[1][1]], writes=[b_k])
            p_, p_k = ps[i % 2]
            for k in range(8):
                self.mm(p_[:, :], h_[:, k, j * 128:(j + 1) * 128], W[:, k, :], k == 0, k == 7, [h_k, Wk], [p_k])
            self.act(sg[:], p_[:], AF.Silu, [p_k], [sgk])
            self.tt("pool", f_[:], f_[:], b_[:], ALU.add, [f_k, b_k], [f_k])
            for h in range(4):
                self.act(junk[:], f_[:, h * 128:(h + 1) * 128], AF.Square, [f_k], [junkk, ssk], accum_out=ss[:, h:h + 1])
            self.ts("dve", ss[:, 4:8], ss[:, 0:4], 1.0 / 128, EPS, ALU.mult, ALU.add, [ssk], [ssk])
            self.act(ss[:, 4:8], ss[:, 4:8], AF.Sqrt, [ssk], [ssk])
            fw.op("dve", lambda: nc.vector.reciprocal(out=ss[:, 4:8], in_=ss[:, 4:8]), [ssk], [ssk])
            self.tt("pool", sg[:], sg[:], nw[:], ALU.mult, [sgk, nwk], [sgk])
            o_, o_k = yo[i % 2]
            for h in range(4):
                self.stt(o_[:, h * 128:(h + 1) * 128], f_[:, h * 128:(h + 1) * 128], ss[:, 4 + h:5 + h], sg[:, h * 128:(h + 1) * 128], ALU.mult, ALU.mult, [f_k, ssk, sgk], [o_k])
            for (p0, n, dstap) in self.x_tile_src(seq, yd, i, "cm"):
                fw.dma(dstap[:, 1024:1536], o_[p0:p0 + n, :], reads=[o_k], writes=[ydk])
            yield
        P.close()

    def p_hy_filter(self, seq, l):
        fw, nc = self.fw, self.nc
        Ls = seq.L
        nb2 = 2 * Ls // 128
        NP = 2 * Ls
        P = Pool(fw)
        w1, w1k = P.sb([33, 64], F32)
        w2, w2k = P.sb([64, 64], F32)
        w3, w3k = P.sb([64, 1024], F32)
        fv, fvk = P.sb([64, 4], F32)
        fb, fbk = P.sb([64, 2], F32)
        dl, dlk = P.sb([128, 512], F32)
        tn, tnk = P.sb([128, nb2], F32)
        ones, onesk = P.sb([128, 128], F32)
        fw.dma(w1[:], self.fw1[l], writes=[w1k])
        fw.dma(w2[:], self.fw2[l], writes=[w2k])
        fw.dma(w3[:], self.fw3[l], writes=[w3k])
        fw.dma(fv[:, 0:3], self.fvec[l], writes=[fvk])
        fw.dma(dl[:], self.c_delta, writes=[dlk])
        fw.dma(tn[:], self.c_tneg[Ls], writes=[tnk])
        self.ms("dve", ones[:], 1.0, [onesk])
        self.ts("dve", fb[:], fv[:, 0:2], fv[:, 2:3], None, ALU.mult, None, [fvk], [fbk])
        CH = 512
        zt = [P.sb([33, CH], F32) for _ in range(2)]
        h1, h1k = P.sb([64, CH], F32)
        msk, mskk = P.sb([64, CH], F32)
        h2, h2k = P.sb([64, CH], F32)
        ksb, ksbk = P.sb([128, 4, 128], F32)
        dcy, dcyk = P.sb([128, 512], F32)
        kab, kabk = P.sb([128, 512], F32)
        kbf = [P.sb([128, 4, 4, 128], BF16) for _ in range(2)]
        nacc, nacck = P.sb([128, 512], F32)
        self.ms("dve", nacc[:], 0.0, [nacck])
        ps1, ps1k = P.ps([128, 512])
        ps2, ps2k = P.ps([128, 512])
        psk, pskk = P.ps([128, 512])
        kd, kdk = seq.kmat
        zsrc = self.c_z[Ls]
        TWO_PI = 2.0 * math.pi

        def sinwrap(out, pin, pink, col):
            self.act(out, pin, AF.Identity, [pink, fvk, fbk], [h1k if out is h1 else h2k], scale=fv[:, 2:3], bias=fb[:, col:col + 1])
        for ci in range(NP // CH):
            z_, z_k = zt[ci % 2]
            fw.dma(z_[:], zsrc[:, ci * CH:(ci + 1) * CH], writes=[z_k])
            self.mm(ps1[0:64, :], w1[:], z_[:], True, True, [w1k, z_k], [ps1k])
            for (hh, hhk, pp, ppk, col) in ((h1, h1k, ps1, ps1k, 0), (h2, h2k, ps2, ps2k, 1)):
                if col == 1:
                    self.mm(ps2[0:64, :], w2[:], h1[:], True, True, [w2k, h1k], [ps2k])
                self.act(hh[:], pp[0:64, :], AF.Identity, [ppk, fvk, fbk], [hhk], scale=fv[:, 2:3], bias=fb[:, col:col + 1])
                self.ts("dve", msk[:], hh[:], math.pi, -TWO_PI, ALU.is_gt, ALU.mult, [hhk], [mskk])
                self.tt("dve", hh[:], hh[:], msk[:], ALU.add, [hhk, mskk], [hhk])
                self.ts("dve", msk[:], hh[:], -math.pi, TWO_PI, ALU.is_lt, ALU.mult, [hhk], [mskk])
                self.tt("dve", hh[:], hh[:], msk[:], ALU.add, [hhk, mskk], [hhk])
                self.act(hh[:], hh[:], AF.Sin, [hhk], [hhk])
            k_, k_k = kbf[ci % 2]
            for j in range(CH // 128):
                dd = ci * (CH // 128) + j
                side = 1 if dd < nb2 // 2 else 0
                self.mm(psk[:, :], h2[:, j * 128:(j + 1) * 128], w3[:, side * 512:(side + 1) * 512], True, True, [h2k, w3k], [pskk])
                self.act(dcy[:], dl[:], AF.Exp, [dlk, tnk], [dcyk], scale=tn[:, dd:dd + 1])
                self.tt("dve", ksb[:].rearrange("p a c -> p (a c)"), psk[:, :], dcy[:], ALU.mult, [pskk, dcyk], [ksbk])
                self.cp("act", k_[:, :, j, :], ksb[:], [ksbk], [k_k])
                self.act(kab[:], ksb[:].rearrange("p a c -> p (a c)"), AF.Abs, [ksbk], [kabk])
                self.tt("pool", nacc[:], nacc[:], kab[:], ALU.add, [nacck, kabk], [nacck])
            d0 = ci * (CH // 128)
            for sg in range(8):
                fw.dma(kd[sg, :, d0:d0 + CH // 128, :], k_[:, sg // 2, :, (sg % 2) * 64:(sg % 2 + 1) * 64], reads=[k_k], writes=[kdk])
        self.mm(psk[:, :], ones[:], nacc[:], True, True, [onesk, nacck], [pskk])
        self.ts("dve", nacc[:], psk[:, :], EPS, None, ALU.add, None, [pskk], [nacck])
        fw.op("dve", lambda: nc.vector.reciprocal(out=nacc[:], in_=nacc[:]), [nacck], [nacck])
        fw.dma(seq.rn[0], nacc[:], reads=[nacck], writes=[seq.rn[1]])
        P.close()

    def p_hy_front(self, seq, l):
        fw, nc = self.fw, self.nc
        P = Pool(fw)
        W, Wk = P.sb([128, 8, 1536], BF16)
        Wg, Wgk = P.sb([128, 8, 512], BF16)
        self.load_w(P, W, Wk, self.w_in[l][:, O_HY:O_HY + 1536], 1536)
        self.load_w(P, Wg, Wgk, self.w_in[l][:, O_HYG:O_HYG + 512], 512)
        cw, cwk = P.sb([128, 12, 3], F32)
        cb, cbk = P.sb([128, 12], F32)
        hb, hbk = P.sb([128, 512], F32)
        fw.dma(cw[:], self.hcw_fm[l], writes=[cwk])
        fw.dma(cb[:], self.hcb_fm[l], writes=[cbk])
        fw.dma(hb[:], self.hyb_bc[l], writes=[hbk])
        T = seq.T
        nt = T // 128
        hw = [P.sb([128, 8, T + 2], BF16) for _ in range(2)]
        ps = [P.ps([128, 1024]) for _ in range(2)]
        cv, cvk = P.sb([128, 12, T], F32)
        wf, wfk = P.sb([128, 4, T], F32)
        wb = [P.sb([128, 4, T], BF16) for _ in range(2)]
        psg, psgk = P.ps([128, 512])
        psT, psTk = P.ps([128, 1024])
        sg, sgk = P.sb([128, 512], F32)
        o1 = [P.sb([128, 512], F32) for _ in range(2)]
        o2 = [P.sb([128, 512], F32) for _ in range(2)]
        hd, hk = seq.hT["rm"]
        hv = hd.rearrange("(k p) t -> p k t", p=128)
        wd, wdk = seq.wT
        wdv = wd.rearrange("(b p) t -> p b t", p=128)
        for si in range(seq.L // T):
            t0 = si * T
            h_, h_k = hw[si % 2]
            fw.dma(h_[:], hv[:, :, 1 + t0:1 + t0 + T + 2], reads=[hk], writes=[h_k])
            for b in range(12):
                p_, p_k = ps[b % 2]
                for k in range(8):
                    self.mm(p_[:, 0:T], W[:, k, b * 128:(b + 1) * 128], h_[:, k, 0:T], k == 0, k == 7, [Wk, h_k], [p_k])
                for k in range(8):
                    self.mm(p_[:, T:T + 2], W[:, k, b * 128:(b + 1) * 128], h_[:, k, T:T + 2], k == 0, k == 7, [Wk, h_k], [p_k])
                self.act(cv[:, b, :], p_[:, 0:T], AF.Identity, [p_k, cwk, cbk], [cvk], scale=cw[:, b, 0:1], bias=cb[:, b:b + 1])
                for k in range(1, 3):
                    self.stt(cv[:, b, :], p_[:, k:k + T], cw[:, b, k:k + 1], cv[:, b, :], ALU.mult, ALU.add, [p_k, cwk, cvk], [cvk])
            self.tt("dve", wf[:], cv[:, 8:12, :], cv[:, 0:4, :], ALU.mult, [cvk], [wfk])
            w_, w_k = wb[si % 2]
            self.cp("pool", w_[:], wf[:], [wfk], [w_k])
            fw.dma(wdv[:, :, 127 + t0:127 + t0 + T], w_[:], reads=[w_k], writes=[wdk])
            for j in range(nt):
                for b in range(4):
                    self.tr(psT[:, b * 128:(b + 1) * 128], cv[:, 4 + b, j * 128:(j + 1) * 128], self.id32[:], [cvk, self.id32k], [psTk])
                    self.tr(psT[:, 512 + b * 128:512 + (b + 1) * 128], wf[:, b, j * 128:(j + 1) * 128], self.id32[:], [wfk, self.id32k], [psTk])
                for k in range(8):
                    self.mm(psg[:, :], h_[:, k, 1 + j * 128:1 + (j + 1) * 128], Wg[:, k, :], k == 0, k == 7, [h_k, Wgk], [psgk])
                self.act(sg[:], psg[:], AF.Silu, [psgk], [sgk])
                a_, a_k = o1[j % 2]
                b_, b_k = o2[j % 2]
                self.tt("dve", a_[:], psT[:, 0:512], sg[:], ALU.mult, [psTk, sgk], [a_k])
                self.tt("dve", b_[:], psT[:, 512:1024], hb[:], ALU.mult, [psTk, hbk], [b_k])
                self.tt("pool", b_[:], b_[:], a_[:], ALU.mult, [b_k, a_k], [b_k])
                r0 = t0 + j * 128
                fw.dma(seq.hx0g[0][r0:r0 + 128, :], a_[:], reads=[a_k], writes=[seq.hx0g[1]])
                fw.dma(seq.hbw[0][r0:r0 + 128, :], b_[:], reads=[b_k], writes=[seq.hbw[1]])
        P.close()

    def p_hy_conv(self, seqs, l):
        fw, nc = self.fw, self.nc
        Lm = max(s.L for s in seqs)
        nbm = Lm // 128
        P = Pool(fw)
        self.hyP = P
        km, kmk = P.sb([128, 2 * nbm, 64], BF16)
        rn, rnk = P.sb([128, 512], F32)
        wsh = [P.sb([128, Lm + 128], BF16) for _ in range(2)]
        yg, ygk = P.sb([128, nbm, 64], BF16)
        IBm = min(nbm, 8)
        hx = [P.sb([128, IBm, 64], F32) for _ in range(2)]
        hbw = [P.sb([128, IBm, 64], F32) for _ in range(2)]
        yo = [P.sb([128, IBm, 64], BF16) for _ in range(2)]
        pys = [P.ps([128, 512]) for _ in range(3)]
        yield
        gc = 0
        for seq in seqs:
            Ls = seq.L
            nb = Ls // 128
            IB = min(nb, 8)
            fw.dma(rn[:], seq.rn[0], reads=[seq.rn[1]], writes=[rnk])
            wd, wdk = seq.wT
            yd, ydk = seq.y
            WL = Ls + 256
            for sg in range(8):
                fw.dma(km[:, 0:2 * nb, :], seq.kmat[0][sg], reads=[seq.kmat[1]], writes=[kmk])
                for c in range(64):
                    ch = sg * 64 + c
                    w_, w_k = wsh[gc % 2]
                    src = bass.AP(tensor=wd.tensor, offset=ch * WL, ap=[[1, 128], [1, Ls + 128]])
                    fw.dma(w_[:, 0:Ls + 128], src, reads=[wdk], writes=[w_k])
                    pyt, pyk = pys[gc % 3]
                    gc += 1
                    py = pyt[:, 0:nb]
                    for j in range(nb + 1):
                        self.mm(py, w_[:, j * 128:(j + 1) * 128], km[:, nb - j:2 * nb - j, c], j == 0, j == nb, [w_k, kmk], [pyk])
                        if j % 13 == 12:
                            yield
                    self.act(yg[:, 0:nb, c], py, AF.Copy, [pyk, rnk], [ygk], scale=rn[:, ch:ch + 1])
                    yield
                for ib in range(nb // IB):
                    a_, a_k = hx[ib % 2]
                    b_, b_k = hbw[ib % 2]
                    o_, o_k = yo[ib % 2]
                    r0 = ib * IB * 128
                    fw.dma(a_[:, 0:IB, :], seq.hx0g[0][r0:r0 + IB * 128, sg * 64:(sg + 1) * 64].rearrange("(i p) c -> p i c", p=128), reads=[seq.hx0g[1]], writes=[a_k])
                    fw.dma(b_[:, 0:IB, :], seq.hbw[0][r0:r0 + IB * 128, sg * 64:(sg + 1) * 64].rearrange("(i p) c -> p i c", p=128), reads=[seq.hbw[1]], writes=[b_k])
                    self.tt("dve", a_[:, 0:IB, :], a_[:, 0:IB, :], yg[:, ib * IB:(ib + 1) * IB, :], ALU.mult, [a_k, ygk], [a_k])
                    self.tt("pool", o_[:, 0:IB, :], a_[:, 0:IB, :], b_[:, 0:IB, :], ALU.add, [a_k, b_k], [o_k])
                    fw.dma(yd[r0:r0 + IB * 128, 1536 + sg * 64:1536 + (sg + 1) * 64].rearrange("(i p) c -> p i c", p=128), o_[:, 0:IB, :], reads=[o_k], writes=[ydk])
                    yield

    @staticmethod
    def _drain(g):
        if g is not None:
            for _ in g:
                pass

    def _corun(self, main_gens, bg, ratio=2):
        bg_alive = bg is not None
        for g in main_gens:
            if g is None:
                continue
            for _ in g:
                if bg_alive:
                    for _r in range(ratio):
                        try:
                            next(bg)
                        except StopIteration:
                            bg_alive = False
                            break
        if bg_alive:
            self._drain(bg)

    def build(self):
        self.setup()
        dp = self.depth
        sc, sl = self.seqs
        xc, xck = sc.x, sc.xk
        xl, xlk = sl.x, sl.xk
        for l in range(dp):
            last = l == dp - 1
            act_seqs = [sl] if last else [sc, sl]
            self.p_adaln(l)
            self.p_norm(sc, xc, xck, "rm")
            self.p_norm(sl, xl, xlk, "rm")
            self.p_norm(sl, xl, xlk, "cm")
            for s in act_seqs:
                self.p_hy_filter(s, l)
                self.p_hy_front(s, l)
            for s in (sc, sl):
                self.p_ssd_front(s, l)
            for d in range(2):
                self.p_ssd_scan(l, d)
            bg = self.p_hy_conv(act_seqs, l)
            next(bg)
            mains = [self.p_gla_scan(l, 0), self.p_gla_scan(l, 1)]
            for s in act_seqs:
                mains.append(self.p_ssd_comb(s, l))
                mains.append(self.p_gla_comb(s, l))
            self._corun(mains, bg)
            self.hyP.close()
            if not last:
                self.p_out(sc, l, xc, xck, sc.x1[0], sc.x1[1], False)
                self.p_out(sl, l, xl, xlk, sl.x1[0], sl.x1[1], False)
                xc, xck = sc.x1
                xl, xlk = sl.x1
            else:
                self.p_out(sl, l, xl, xlk, self.out, self.outk, True)
        self.GP.close()
        self.fw.finish()
        return self.nc


def _consts(L, LC):
    import ml_dtypes
    c = {}
    c["c_id32"] = np.eye(128, dtype=np.float32)
    c["c_idbf"] = np.eye(128, dtype=np.float32).astype(ml_dtypes.bfloat16)
    s = np.arange(128)[:, None]
    t = np.arange(128)[None, :]
    c["c_mask"] = np.stack([(s <= t), (s >= t)]).astype(np.float32)
    sel = np.zeros((2, 256), np.float32)
    sel[0, 0:128] = 1.0
    sel[1, 128:256] = 1.0
    c["c_sel"] = sel
    rst = np.ones((128, 2, 512), np.float32)
    rst[:, 0, 0::128] = 0.0
    rst[:, 1, 0::64] = 0.0
    c["c_rst"] = rst
    delta = np.abs(np.linspace(HY_MIN_DECAY, HY_MAX_DECAY, 512, dtype=np.float32))
    c["c_delta"] = np.ascontiguousarray(np.broadcast_to(delta[None, :], (128, 512))).astype(np.float32)
    for Ls in sorted(set([L, LC])):
        nb = Ls // 128
        dd = np.arange(2 * nb)[None, :]
        rp = np.arange(128)[:, None]
        lam = 128 * (dd - nb) + 127 - rp
        pos = np.abs(lam)
        invalid = pos >= Ls
        pos = np.where(invalid, 0, pos)
        tl = np.linspace(0.0, 1.0, Ls, dtype=np.float32)
        tt = tl[pos].astype(np.float32)
        tneg = np.where(invalid, -1.0e4, -tt).astype(np.float32)
        c["c_tneg%d" % Ls] = np.ascontiguousarray(tneg)
        posn = pos.T.reshape(-1)
        bands = np.linspace(1e-4, 15.0, 16, dtype=np.float32)
        ang = (np.float32(2.0 * math.pi / Ls) * posn.astype(np.float32)[:, None] * bands[None, :]).astype(np.float32)
        z = np.concatenate([tl[posn][:, None], np.cos(ang.astype(np.float64)), -np.sin(ang.astype(np.float64))], axis=1)
        c["c_z%d" % Ls] = np.ascontiguousarray(z.T).astype(np.float32)
    return c


def _layout_inputs(inp, b, L, LC, depth):
    f = lambda a: np.ascontiguousarray(np.asarray(a, dtype=np.float32))
    m = {}
    m["x"] = f(inp["x"][b])
    m["ctx"] = f(inp["ctx"][b])
    c2 = np.stack([np.asarray(inp["c"][b]).reshape(128, 8), np.asarray(inp["c_ctx"]).reshape(128, 8)], axis=-1)
    m["c2"] = f(c2)
    m["ada_w"] = f(inp["ada_w"])
    m["ada_b2"] = f(np.broadcast_to(np.asarray(inp["ada_b"])[:, None, :], (depth, 2, 3 * D)))
    m["norm_w2"] = f(np.broadcast_to(np.asarray(inp["norm_w"])[:, None, :], (depth, 2, D)))
    m["fnw_bc"] = f(np.broadcast_to(np.asarray(inp["final_norm_w"])[None, :], (128, D)))
    m["w_in"] = f(inp["w_in"])
    m["w_out"] = f(inp["w_out"])
    cw = np.asarray(inp["ssd_conv_w"])
    m["cw_fm"] = f(cw.reshape(depth, 5, 12, 128).transpose(0, 3, 2, 1))
    m["cb_fm"] = f(np.asarray(inp["ssd_conv_b"]).reshape(depth, 12, 128).transpose(0, 2, 1))
    hcw = np.asarray(inp["hy_conv_w"])
    m["hcw_fm"] = f(hcw.reshape(depth, 3, 12, 128).transpose(0, 3, 2, 1))
    m["hcb_fm"] = f(np.asarray(inp["hy_conv_b"]).reshape(depth, 12, 128).transpose(0, 2, 1))
    m["dtb_fm"] = f(np.asarray(inp["ssd_dt_bias"]).reshape(depth, 32, 1))
    m["alog_fm"] = f(np.asarray(inp["ssd_a_log"]).reshape(depth, 32, 1))
    m["d_bc"] = f(np.broadcast_to(np.asarray(inp["ssd_d"])[:, None, :], (depth, 128, 16)))
    m["snw_bc"] = f(np.broadcast_to(np.asarray(inp["ssd_norm_w"])[:, None, :], (depth, 128, D)))
    lbl = np.asarray(inp["hg_lb_logits"])
    m["lbl_fm"] = f(lbl.reshape(2, depth, 4, 128).transpose(3, 0, 1, 2))
    m["hnw_bc"] = f(np.broadcast_to(np.asarray(inp["hg_norm_w"])[:, None, :], (depth, 128, 512)))
    m["hyb_bc"] = f(np.broadcast_to(np.asarray(inp["hy_bias"])[:, None, :], (depth, 128, 512)))
    m["fw1"] = f(inp["hy_filt_w1"])
    m["fw2"] = f(inp["hy_filt_w2"])
    m["fw3"] = f(inp["hy_filt_w3"])
    m["fvec"] = f(np.stack([np.asarray(inp["hy_filt_b1"]), np.asarray(inp["hy_filt_b2"]), np.asarray(inp["hy_filt_freq"])], axis=-1))
    return m


_CACHE = {}


def run(inputs, L, LC, depth, nb, debug=()):
    key = (L, LC, depth, tuple(debug))
    if key not in _CACHE:
        bld = Builder(L, LC, depth, debug)
        bld.build()
        _CACHE[key] = bld
    bld = _CACHE[key]
    consts = _consts(L, LC)
    in_maps = []
    for b in range(nb):
        m = _layout_inputs(inputs, b, L, LC, depth)
        m.update(consts)
        in_maps.append({k: m[k] for k in bld.inputs})
    res = run_bass_kernel_spmd(bld.nc, in_maps, core_ids=list(range(nb)))
    return res, bld


def kernel(**inputs):
    res, bld = run(inputs, 8192, 256, 2, 8)
    out = np.stack([np.asarray(r["out"], dtype=np.float32) for r in res.results], axis=0)
    return out
```

```python
import numpy as np
import contextlib
import concourse.bass as bass
import concourse.mybir as mybir
from concourse.bass_utils import run_bass_kernel_spmd

F32 = mybir.dt.float32
BF16 = mybir.dt.bfloat16
AF = mybir.ActivationFunctionType
ALU = mybir.AluOpType


class Tk:
    __slots__ = ("w", "r", "multi")

    def __init__(self, multi=False):
        self.w = {}
        self.r = {}
        self.multi = multi


class FW:
    EPOCH = 30000

    def __init__(self, nc):
        self.nc = nc
        self.eng = {"pe": nc.tensor, "dve": nc.vector, "act": nc.scalar, "pool": nc.gpsimd, "sp": nc.sync}
        self.stack = contextlib.ExitStack()
        self.sem = {}
        self.cnt = {}
        self.semobj = {}
        self.waited = {e: {} for e in self.eng}
        self.nsem = 0
        for e in self.eng:
            self._newsem(e)
        self.dring = []
        for i in range(12):
            s = self._alloc_sem("dma%d" % i)
            self.dring.append([s, 0])
        self.dpos = 0
        self.ninst = 0

    def _alloc_sem(self, name):
        s = self.stack.enter_context(self.nc.semaphore(name))
        self.nsem += 1
        self.semobj[id(s)] = s
        return s

    def _newsem(self, e):
        self.sem[e] = self._alloc_sem("s_%s_%d" % (e, self.nsem))
        self.cnt[e] = 0

    def _deps(self, reads, writes):
        d = {}

        def add(dd):
            for k, v in dd.items():
                if d.get(k, 0) < v:
                    d[k] = v
        for t in reads:
            add(t.w)
        for t in writes:
            if not t.multi:
                add(t.w)
            add(t.r)
        return d

    def _wait(self, e, deps, skip_own=False):
        w = self.waited[e]
        own = id(self.sem[e])
        for k, v in deps.items():
            if skip_own and k == own:
                continue
            if w.get(k, 0) >= v:
                continue
            self.eng[e].wait_ge(self.semobj[k], v)
            w[k] = v

    def _record(self, tok, reads, writes):
        k, v = tok
        for t in reads:
            if t.r.get(k, 0) < v:
                t.r[k] = v
        for t in writes:
            if t.multi:
                if t.w.get(k, 0) < v:
                    t.w[k] = v
            else:
                t.w = {k: v}
                t.r = {}

    def op(self, e, fn, reads=(), writes=()):
        deps = self._deps(reads, writes)
        self._wait(e, deps, skip_own=(e == "pe"))
        if self.cnt[e] >= self.EPOCH:
            self._newsem(e)
        inst = fn()
        inst.then_inc(self.sem[e], 1)
        self.cnt[e] += 1
        self.ninst += 1
        self._record((id(self.sem[e]), self.cnt[e]), reads, writes)
        return inst

    def dma(self, out, in_, reads=(), writes=(), q="sp", **kw):
        deps = self._deps(reads, writes)
        slot = self.dring[self.dpos % len(self.dring)]
        self.dpos += 1
        if slot[1] > 0:
            deps[id(slot[0])] = max(deps.get(id(slot[0]), 0), slot[1])
        self._wait(q, deps)
        if slot[1] >= 16 * 1800:
            slot[0] = self._alloc_sem("dmax%d" % self.nsem)
            slot[1] = 0
        inst = self.eng[q].dma_start(out=out, in_=in_, **kw)
        slot[1] += 16
        inst.then_inc(slot[0], 16)
        self.ninst += 1
        self._record((id(slot[0]), slot[1]), reads, writes)
        return inst

    def barrier(self):
        allt = {}
        for e in self.eng:
            if self.cnt[e] > 0:
                allt[id(self.sem[e])] = self.cnt[e]
        for s, v in self.dring:
            if v > 0:
                allt[id(s)] = v
        for e in self.eng:
            self._wait(e, dict(allt))

    def finish(self):
        self.barrier()
        self.stack.close()


class Pool:
    N = 0

    def __init__(self, fw):
        self.fw = fw
        self.stack = contextlib.ExitStack()
        self.n = 0

    def sb(self, shape, dt, name=None):
        Pool.N += 1
        t = self.stack.enter_context(self.fw.nc.sbuf_tensor(name or ("sb_%d" % Pool.N), list(shape), dt))
        return t, Tk()

    def ps(self, shape, dt=F32, name=None):
        Pool.N += 1
        t = self.stack.enter_context(self.fw.nc.psum_tensor(name or ("ps_%d" % Pool.N), list(shape), dt))
        return t, Tk()

    def close(self):
        self.fw.barrier()
        self.stack.close()
import math

AX = mybir.AxisListType
I32 = mybir.dt.int32
D = 1024
KC = 8
EPS = 1e-6
GW = 64
O_XBC, O_DTF, O_DTB, O_HGI, O_HFF, O_HFB, O_HQ, O_HY, O_Z, O_HGG, O_HYG = 0, 1536, 1552, 1568, 2080, 2592, 3104, 3616, 5152, 6176, 6688
IN_COLS = 7200
HY_MIN_DECAY = math.log(1e-2) / 1.5
HY_MAX_DECAY = math.log(1e-2) / 0.3


def bc(ap, shape):
    return ap.to_broadcast(list(shape))


class Seq:
    pass


class Builder:
    def __init__(self, L, LC, depth, debug=()):
        self.L, self.LC, self.depth = L, LC, depth
        self.debug = set(debug)
        self.nc = bass.Bass("TRN2", target_bir_lowering=False)
        self.fw = FW(self.nc)
        self.inputs = {}
        self.outputs = []

    def din(self, name, shape, dt=F32):
        t = self.nc.dram_tensor(name, list(shape), dt, kind="ExternalInput").ap()
        self.inputs[name] = t
        return t

    def scr(self, name, shape, dt):
        kind = "ExternalOutput" if name in self.debug else "Internal"
        t = self.nc.dram_tensor(name, list(shape), dt, kind=kind).ap()
        if name in self.debug:
            self.outputs.append(name)
        return t, Tk(multi=True)

    def tt(self, e, out, in0, in1, op, R, W):
        en = self.fw.eng[e]
        return self.fw.op(e, lambda: en.tensor_tensor(out=out, in0=in0, in1=in1, op=op), R, W)

    def ts(self, e, out, in0, s1, s2, op0, op1, R, W):
        en = self.fw.eng[e]
        if op1 is None:
            return self.fw.op(e, lambda: en.tensor_scalar(out=out, in0=in0, scalar1=s1, scalar2=None, op0=op0), R, W)
        return self.fw.op(e, lambda: en.tensor_scalar(out=out, in0=in0, scalar1=s1, scalar2=s2, op0=op0, op1=op1), R, W)

    def stt(self, out, in0, scalar, in1, op0, op1, R, W):
        nc = self.nc
        return self.fw.op("dve", lambda: nc.vector.scalar_tensor_tensor(out=out, in0=in0, scalar=scalar, in1=in1, op0=op0, op1=op1), R, W)

    def act(self, out, in_, func, R, W, bias=None, scale=None, accum_out=None):
        nc = self.nc
        kw = {}
        if bias is not None:
            kw["bias"] = bias
        if scale is not None:
            kw["scale"] = scale
        if accum_out is not None:
            kw["accum_out"] = accum_out
            npart = accum_out.shape[0]

            def fn():
                nc.scalar.activation(out=out, in_=in_, func=func, **kw)
                return nc.scalar.copy(out=self.dummy[0:npart, 0:1], in_=accum_out)
            return self.fw.op("act", fn, R, W)
        return self.fw.op("act", lambda: nc.scalar.activation(out=out, in_=in_, func=func, **kw), R, W)

    def mm(self, out, lhsT, rhs, start, stop, R, W):
        nc = self.nc
        return self.fw.op("pe", lambda: nc.tensor.matmul(out, lhsT, rhs, start=start, stop=stop), R, W)

    def tr(self, out, in_, ident, R, W):
        nc = self.nc
        return self.fw.op("pe", lambda: nc.tensor.transpose(out, in_, ident), R, W)

    def cp(self, e, out, in_, R, W):
        if e == "act":
            nc = self.nc
            return self.fw.op("act", lambda: nc.scalar.copy(out=out, in_=in_), R, W)
        en = self.fw.eng[e]
        return self.fw.op(e, lambda: en.tensor_copy(out=out, in_=in_), R, W)

    def ms(self, e, ap, val, W):
        en = self.fw.eng[e]
        return self.fw.op(e, lambda: en.memset(ap, val), (), W)

    def load_w(self, P, dst, dk, wsrc, ncols):
        TP = Pool(self.fw)
        st = [TP.sb([128, 8, 256], F32) for _ in range(2)]
        src = wsrc.rearrange("(k p) n -> p k n", p=128)
        i = 0
        for c0 in range(0, ncols, 256):
            n = min(256, ncols - c0)
            s, sk = st[i % 2]
            self.fw.dma(s[:, :, 0:n], src[:, :, c0:c0 + n], writes=[sk])
            self.cp("pool" if i % 2 else "dve", dst[:, :, c0:c0 + n], s[:, :, 0:n], [sk], [dk])
            i += 1
        TP.close()

    def setup(self):
        L, LC, dp = self.L, self.LC, self.depth
        self.x_in = self.din("x", [L, D])
        self.ctx_in = self.din("ctx", [LC, D])
        self.c2_in = self.din("c2", [128, 8, 2])
        self.ada_w = self.din("ada_w", [dp, D, 3 * D])
        self.ada_b2 = self.din("ada_b2", [dp, 2, 3 * D])
        self.norm_w2 = self.din("norm_w2", [dp, 2, D])
        self.fnw_bc = self.din("fnw_bc", [128, D])
        self.w_in = self.din("w_in", [dp, D, IN_COLS])
        self.w_out = self.din("w_out", [dp, 2 * D, D])
        self.cw_fm = self.din("cw_fm", [dp, 128, 12, 5])
        self.cb_fm = self.din("cb_fm", [dp, 128, 12])
        self.hcw_fm = self.din("hcw_fm", [dp, 128, 12, 3])
        self.hcb_fm = self.din("hcb_fm", [dp, 128, 12])
        self.dtb_fm = self.din("dtb_fm", [dp, 32, 1])
        self.alog_fm = self.din("alog_fm", [dp, 32, 1])
        self.d_bc = self.din("d_bc", [dp, 128, 16])
        self.snw_bc = self.din("snw_bc", [dp, 128, D])
        self.lbl_fm = self.din("lbl_fm", [128, 2, dp, 4])
        self.hnw_bc = self.din("hnw_bc", [dp, 128, 512])
        self.hyb_bc = self.din("hyb_bc", [dp, 128, 512])
        self.fw1 = self.din("fw1", [dp, 33, 64])
        self.fw2 = self.din("fw2", [dp, 64, 64])
        self.fw3 = self.din("fw3", [dp, 64, 1024])
        self.fvec = self.din("fvec", [dp, 64, 3])
        self.c_id32 = self.din("c_id32", [128, 128])
        self.c_idbf = self.din("c_idbf", [128, 128], BF16)
        self.c_mask = self.din("c_mask", [2, 128, 128])
        self.c_sel = self.din("c_sel", [2, 256])
        self.c_rst = self.din("c_rst", [128, 2, 512])
        self.c_delta = self.din("c_delta", [128, 512])
        self.c_z = {}
        self.c_tneg = {}
        for Ls in sorted(set([L, LC])):
            self.c_z[Ls] = self.din("c_z%d" % Ls, [33, 2 * Ls])
            self.c_tneg[Ls] = self.din("c_tneg%d" % Ls, [128, 2 * Ls // 128])
        self.out = self.nc.dram_tensor("out", [L, D], F32, kind="ExternalOutput").ap()
        self.outk = Tk(multi=True)
        self.ink = Tk(multi=True)
        self.GP = Pool(self.fw)
        G = self.GP
        self.id32, self.id32k = G.sb([128, 128], F32)
        self.idbf, self.idbfk = G.sb([128, 128], BF16)
        self.mask, self.maskk = G.sb([128, 2, 128], F32)
        self.rst, self.rstk = G.sb([128, 2, 512], F32)
        self.gsh, self.gshk = G.sb([128, 8, 2, 2], F32)
        self.gate, self.gatek = G.sb([128, 2, D], F32)
        self.zero, self.zerok = G.sb([128, 160], BF16)
        self.dummy, self.dummyk = G.sb([128, 4], F32)
        fw = self.fw
        fw.dma(self.id32[:], self.c_id32, writes=[self.id32k])
        fw.dma(self.idbf[:], self.c_idbf, writes=[self.idbfk])
        fw.dma(self.mask[:], self.c_mask.rearrange("a s t -> s a t"), writes=[self.maskk])
        fw.dma(self.rst[:], self.c_rst, writes=[self.rstk])
        self.ms("dve", self.zero[:], 0.0, [self.zerok])
        self.seqs = []
        for nm, Ls, xin in (("c", LC, self.ctx_in), ("l", L, self.x_in)):
            s = Seq()
            s.name, s.L, s.is_lat = nm, Ls, nm == "l"
            s.T = min(512, Ls)
            s.x, s.xk = xin, self.ink
            s.hT = {"rm": self.scr("hT_" + nm, [D, Ls + 4], BF16)}
            if s.is_lat:
                s.hT["cm"] = self.scr("hTc_" + nm, [D, Ls + 4], BF16)
            else:
                s.hT["cm"] = s.hT["rm"]
            s.xbcT = self.scr("xbcT_" + nm, [1536, Ls], BF16)
            s.dta = self.scr("dta_" + nm, [32, 2, Ls], F32)
            s.ysd = [self.scr("ysd%d_%s" % (d, nm), [Ls, D], F32) for d in range(2)]
            s.od = [self.scr("od%d_%s" % (d, nm), [Ls, 512], F32) for d in range(2)]
            s.y = self.scr("y_" + nm, [Ls, 2 * D], BF16)
            s.wT = self.scr("wT_" + nm, [512, Ls + 256], BF16)
            s.hx0g = self.scr("hx0g_" + nm, [Ls, 512], F32)
            s.hbw = self.scr("hbw_" + nm, [Ls, 512], F32)
            s.x1 = self.scr("x1_" + nm, [Ls, D], F32)
            s.kmat = self.scr("kmat_" + nm, [8, 128, 2 * Ls // 128, 64], BF16)
            s.rn = self.scr("rn_" + nm, [128, 512], F32)
            self.seqs.append(s)
        self.cumsc2 = [[self.scr("cumsc%d_%d" % (d, j), [16, 512], F32) for j in range(2)] for d in range(2)]
        for s in self.seqs:
            for key in (("rm", "cm") if s.is_lat else ("rm",)):
                hd, hk = s.hT[key]
                v = hd.rearrange("(k p) t -> p k t", p=128)
                fw.dma(v[:, :, 0:2], self.zero[:, 0:16].rearrange("p (k t) -> p k t", k=8), reads=[self.zerok], writes=[hk])
                fw.dma(v[:, :, s.L + 2:s.L + 4], self.zero[:, 0:16].rearrange("p (k t) -> p k t", k=8), reads=[self.zerok], writes=[hk])
            wd, wk = s.wT
            for c0 in range(0, 512, 128):
                fw.dma(wd[c0:c0 + 128, 0:127], self.zero[:, 0:127], reads=[self.zerok], writes=[wk])
                fw.dma(wd[c0:c0 + 128, 127 + s.L:s.L + 256], self.zero[:, 0:129], reads=[self.zerok], writes=[wk])

    def p_adaln(self, l):
        fw, nc = self.fw, self.nc
        P = Pool(fw)
        c2, c2k = P.sb([128, 8, 2], F32)
        wst = [P.sb([128, 8, 512], F32) for _ in range(2)]
        mod, modk = P.sb([2, 3 * D], F32)
        adab, adabk = P.sb([2, 3 * D], F32)
        nw2, nw2k = P.sb([2, D], F32)
        grow, growk = P.sb([2, 2, D], F32)
        sel, selk = P.sb([2, 256], F32)
        ps, psk = P.ps([128, 512])
        pst, pstk = P.ps([128, 512])
        fw.dma(c2[:], self.c2_in, writes=[c2k])
        fw.dma(adab[:], self.ada_b2[l], writes=[adabk])
        fw.dma(nw2[:], self.norm_w2[l], writes=[nw2k])
        fw.dma(sel[:], self.c_sel, writes=[selk])
        self.act(c2[:], c2[:], AF.Silu, [c2k], [c2k])
        wv = self.ada_w[l].rearrange("(p k) n -> p k n", k=8)
        for cb in range(6):
            w, wk = wst[cb % 2]
            fw.dma(w[:], wv[:, :, cb * 512:(cb + 1) * 512], writes=[wk])
            for k in range(8):
                self.mm(ps[0:2, :], c2[:, k, :], w[:, k, :], k == 0, k == 7, [c2k, wk], [psk])
            self.tt("dve", mod[:, cb * 512:(cb + 1) * 512], ps[0:2, :], adab[:, cb * 512:(cb + 1) * 512], ALU.add, [psk, adabk], [modk])
        self.stt(grow[:, 0, :], mod[:, D:2 * D], 1.0, nw2[:], ALU.add, ALU.mult, [modk, nw2k], [growk])
        self.cp("dve", grow[:, 1, :], mod[:, 0:D], [modk], [growk])
        for k in range(8):
            for w_ in range(2):
                o = (k * 2 + w_) * 2
                self.tr(pst[:, o:o + 2], grow[0:2, w_, k * 128:(k + 1) * 128], self.id32[0:2, 0:2], [growk, self.id32k], [pstk])
        self.cp("dve", self.gsh[:].rearrange("p k a b -> p (k a b)"), pst[:, 0:32], [pstk], [self.gshk])
        for r in range(2):
            for h in range(2):
                self.mm(ps[:, :], sel[0:2, r * 128:(r + 1) * 128], mod[0:2, 2 * D + h * 512:2 * D + (h + 1) * 512], True, True, [selk, modk], [psk])
                self.cp("dve", self.gate[:, r, h * 512:(h + 1) * 512], ps[:, :], [psk], [self.gatek])
        P.close()

    def x_tile_src(self, seq, xap, i, order):
        if order == "rm" or not seq.is_lat:
            return [(0, 128, xap[i * 128:(i + 1) * 128, :])]
        rows = seq.L // GW
        v = xap.rearrange("(r c) d -> c r d", c=GW)
        if rows >= 128:
            assert rows == 128
            return [(0, 128, v[i])]
        ncol = 128 // rows
        return [(j * rows, rows, v[i * ncol + j]) for j in range(ncol)]

    def p_norm(self, seq, xap, xk, order):
        fw, nc = self.fw, self.nc
        row = 0 if seq.is_lat else 1
        hd, hk = seq.hT[order]
        hv = hd.rearrange("(k p) t -> p k t", p=128)
        P = Pool(fw)
        xt = [P.sb([128, D], F32) for _ in range(2)]
        junk, junkk = P.sb([128, D], BF16)
        ss = [P.sb([128, 2], F32) for _ in range(2)]
        xn = [P.sb([128, D], F32) for _ in range(2)]
        tmp, tmpk = P.sb([128, 8, 128], F32)
        hs = [P.sb([128, 8, 512], BF16) for _ in range(2)]
        ps = [P.ps([128, 1024]) for _ in range(2)]
        nt = seq.L // 128
        T = seq.T
        per = T // 128
        def load(i):
            x_, x_k = xt[i % 2]
            for (p0, n, src) in self.x_tile_src(seq, xap, i, order):
                fw.dma(x_[p0:p0 + n, :], src, reads=[xk], writes=[x_k])
        load(0)
        for i in range(nt):
            x_, x_k = xt[i % 2]
            if i + 1 < nt:
                load(i + 1)
            s_, s_k = ss[i % 2]
            self.act(junk[:], x_[:], AF.Square, [x_k], [junkk, s_k], accum_out=s_[:, 0:1])
            self.ts("dve", s_[:, 1:2], s_[:, 0:1], 1.0 / D, EPS, ALU.mult, ALU.add, [s_k], [s_k])
            self.act(s_[:, 1:2], s_[:, 1:2], AF.Sqrt, [s_k], [s_k])
            fw.op("dve", lambda: nc.vector.reciprocal(out=s_[:, 1:2], in_=s_[:, 1:2]), [s_k], [s_k])
            n_, n_k = xn[i % 2]
            self.act(n_[:], x_[:], AF.Copy, [x_k, s_k], [n_k], scale=s_[:, 1:2])
            p_, p_k = ps[i % 2]
            for k in range(8):
                self.tr(p_[:, k * 128:(k + 1) * 128], n_[:, k * 128:(k + 1) * 128], self.id32[:], [n_k, self.id32k], [p_k])
            h_, h_k = hs[(i // per) % 2]
            j = i % per
            self.tt("dve", tmp[:], p_[:].rearrange("p (k t) -> p k t", k=8), bc(self.gsh[:, :, 0, row:row + 1], [128, 8, 128]), ALU.mult, [p_k, self.gshk], [tmpk])
            self.tt("pool", h_[:, :, j * 128:(j + 1) * 128], tmp[:], bc(self.gsh[:, :, 1, row:row + 1], [128, 8, 128]), ALU.add, [tmpk, self.gshk], [h_k])
            if j == per - 1:
                t0 = (i // per) * T
                fw.dma(hv[:, :, 2 + t0:2 + t0 + T], h_[:, :, 0:T], reads=[h_k], writes=[hk])
        P.close()

    def p_ssd_front(self, seq, l):
        fw, nc = self.fw, self.nc
        P = Pool(fw)
        W, Wk = P.sb([128, 8, 1568], BF16)
        self.load_w(P, W, Wk, self.w_in[l][:, O_XBC:O_XBC + 1568], 1568)
        cw, cwk = P.sb([128, 12, 5], F32)
        cb, cbk = P.sb([128, 12], F32)
        dtb, dtbk = P.sb([32, 1], F32)
        aneg, anegk = P.sb([32, 1], F32)
        fw.dma(cw[:], self.cw_fm[l], writes=[cwk])
        fw.dma(cb[:], self.cb_fm[l], writes=[cbk])
        fw.dma(dtb[:], self.dtb_fm[l], writes=[dtbk])
        fw.dma(aneg[:], self.alog_fm[l], writes=[anegk])
        self.act(aneg[:], aneg[:], AF.Exp, [anegk], [anegk])
        self.ts("dve", aneg[:], aneg[:], -1.0, None, ALU.mult, None, [anegk], [anegk])
        T = seq.T
        hw = [P.sb([128, 8, T + 4], BF16) for _ in range(2)]
        ps = [P.ps([128, 1024]) for _ in range(2)]
        acc = [P.sb([128, T], F32) for _ in range(2)]
        xst = [P.sb([128, 12, T], BF16) for _ in range(2)]
        psd, psdk = P.ps([128, 512])
        dst = [P.sb([32, 2, T], F32) for _ in range(2)]
        hd, hk = seq.hT["rm"]
        hv = hd.rearrange("(k p) t -> p k t", p=128)
        xd, xdk = seq.xbcT
        xdv = xd.rearrange("(b p) t -> p b t", p=128)
        dd, ddk = seq.dta
        nsup = seq.L // T
        def load(si):
            h_, h_k = hw[si % 2]
            fw.dma(h_[:], hv[:, :, si * T:si * T + T + 4], reads=[hk], writes=[h_k])
        load(0)
        for si in range(nsup):
            t0 = si * T
            h_, h_k = hw[si % 2]
            if si + 1 < nsup:
                load(si + 1)
            x_, x_k = xst[si % 2]
            for b in range(12):
                p_, p_k = ps[b % 2]
                for k in range(8):
                    self.mm(p_[:, 0:T], W[:, k, b * 128:(b + 1) * 128], h_[:, k, 0:T], k == 0, k == 7, [Wk, h_k], [p_k])
                for k in range(8):
                    self.mm(p_[:, T:T + 4], W[:, k, b * 128:(b + 1) * 128], h_[:, k, T:T + 4], k == 0, k == 7, [Wk, h_k], [p_k])
                a_, a_k = acc[b % 2]
                self.act(a_[:], p_[:, 0:T], AF.Identity, [p_k, cwk, cbk], [a_k], scale=cw[:, b, 0:1], bias=cb[:, b:b + 1])
                for k in range(1, 5):
                    self.stt(a_[:], p_[:, k:k + T], cw[:, b, k:k + 1], a_[:], ALU.mult, ALU.add, [p_k, cwk, a_k], [a_k])
                self.act(x_[:, b, :], a_[:], AF.Silu, [a_k], [x_k])
            fw.dma(xdv[:, :, t0:t0 + T], x_[:], reads=[x_k], writes=[xdk])
            for k in range(8):
                self.mm(psd[0:32, 0:T], W[:, k, 1536:1568], h_[:, k, 2:2 + T], k == 0, k == 7, [Wk, h_k], [psdk])
            d_, d_k = dst[si % 2]
            self.act(d_[:, 0, :], psd[0:32, 0:T], AF.Exp, [psdk, dtbk], [d_k], bias=dtb[:, 0:1])
            self.act(d_[:, 0, :], d_[:, 0, :], AF.Ln, [d_k], [d_k], bias=1.0)
            self.ts("dve", d_[:, 1, :], d_[:, 0, :], aneg[:, 0:1], None, ALU.mult, None, [d_k, anegk], [d_k])
            fw.dma(dd[:, :, t0:t0 + T], d_[:], reads=[d_k], writes=[ddk])
        P.close()

    def p_ssd_scan(self, l, d):
        fw, nc = self.fw, self.nc
        P = Pool(fw)
        self._pools.append(P)
        TS = 256
        S, Sk = P.sb([128, 16, 64], F32)
        Sbf, Sbfk = P.sb([128, 1024], BF16)
        self.ms("dve", S[:], 0.0, [Sk])
        self.ms("pool", Sbf[:], 0.0, [Sbfk])
        drow, drowk = P.sb([128, 16], F32)
        fw.dma(drow[:], self.d_bc[l], writes=[drowk])
        xs2 = [P.sb([128, 12, TS], BF16) for _ in range(2)]
        da2 = [P.sb([16, 2, TS], F32) for _ in range(2)]
        cumT2 = [P.sb([16, TS], F32) for _ in range(2)]
        R4 = [P.sb([128, 16, 128], F32) for _ in range(4)]
        ctok, ctokk = P.sb([128, 48], F32)
        Dm, Dmk = P.sb([128, 16, 128], F32)
        dec, deck = P.sb([128, 16, 128], BF16)
        sm, smk = P.sb([128, 2, 128], F32)
        MT, MTk = P.sb([128, 16, 128], BF16)
        Btok, Btokk = P.sb([128, 256], BF16)
        xdt, xdtk = P.sb([128, 16, 64], BF16)
        xsD, xsDk = P.sb([128, 16, 64], BF16)
        xdtw, xdtwk = P.sb([128, 16, 64], BF16)
        yi, yik = P.sb([128, 8, 64], F32)
        yo2 = [P.sb([128, 1024], F32) for _ in range(2)]
        etot, etotk = P.sb([128, 16], F32)
        tot, totk = P.sb([16, 8], F32)
        pm, pmk = P.ps([128, 512])
        psx, psxk = P.ps([128, 1024], BF16)
        psy, psyk = P.ps([128, 512])
        psi, psik = P.ps([128, 512])
        pst = pm[:, 0:32]
        pss = pm[:, 32:288]
        yield
        mk = self.mask[:, d, :]
        gc = 0
        plan = []
        for seq in self.seqs:
            T = min(TS, seq.L)
            nsup = seq.L // T
            sups = list(range(nsup) if d == 0 else range(nsup - 1, -1, -1))
            plan.extend((seq, si) for si in sups)
        Rb = {}

        def prologue(pi):
            seq, si = plan[pi]
            T = min(TS, seq.L)
            nch = T // 128
            t0 = si * T
            xd, xdk = seq.xbcT
            xdv = xd.rearrange("(b p) t -> p b t", p=128)
            dd, ddk = seq.dta
            xs, xsk = xs2[pi % 2]
            da, dak = da2[pi % 2]
            cumT, cumTk = cumT2[pi % 2]
            cd, cdk = self.cumsc2[d][pi % 2]
            fw.dma(xs[:, :, 0:T], xdv[:, :, t0:t0 + T], reads=[xdk], writes=[xsk])
            fw.dma(da[:, :, 0:T], dd[d * 16:(d + 1) * 16, :, t0:t0 + T], reads=[ddk], writes=[dak])
            fw.op("dve", lambda: nc.vector.tensor_tensor_scan(out=cumT[:, 0:T], data0=self.rst[0:16, 0, 0:T], data1=da[:, 1, 0:T], initial=0.0, op0=ALU.mult, op1=ALU.add), [self.rstk, dak], [cumTk])
            if d == 1:
                c3 = cumT[:, 0:T].rearrange("p (c t) -> p c t", t=128)
                self.cp("dve", tot[:, 0:nch], cumT[:, 127:T:128], [cumTk], [totk])
                self.tt("dve", c3, bc(tot[:, 0:nch].unsqueeze(2), [16, nch, 128]), c3, ALU.subtract, [cumTk, totk], [cumTk])
                self.tt("dve", cumT[:, 0:T], cumT[:, 0:T], da[:, 1, 0:T], ALU.add, [cumTk, dak], [cumTk])
            fw.dma(cd[:, 0:T], cumT[:, 0:T], reads=[cumTk], writes=[cdk])
            for ci in range(nch):
                R, Rk = R4[(pi % 2) * 2 + ci]
                rsrc = bass.AP(tensor=cd.tensor, offset=ci * 128, ap=[[0, 128], [512, 16], [1, 128]])
                fw.dma(R[:], rsrc, reads=[cdk], writes=[Rk])
        prologue(0)
        for pi in range(len(plan)):
            seq, si = plan[pi]
            if True:
                T = min(TS, seq.L)
                nch = T // 128
                yd, ydk = seq.ysd[d]
                t0 = si * T
                xs, xsk = xs2[pi % 2]
                da, dak = da2[pi % 2]
                cumT, cumTk = cumT2[pi % 2]
                if pi + 1 < len(plan):
                    prologue(pi + 1)
                yield
                chs = range(nch) if d == 0 else range(nch - 1, -1, -1)
                for ci in chs:
                    c0 = ci * 128
                    R, Rk = R4[(pi % 2) * 2 + ci]
                    yo, yok = yo2[gc % 2]
                    gc += 1
                    self.tr(pst[:, 0:16], cumT[:, c0:c0 + 128], self.id32[0:16, 0:16], [cumTk, self.id32k], [pmk])
                    self.tr(pst[:, 16:32], da[:, 0, c0:c0 + 128], self.id32[0:16, 0:16], [dak, self.id32k], [pmk])
                    for b in range(8):
                        self.tr(psx[:, b * 128:(b + 1) * 128], xs[:, b, c0:c0 + 128], self.idbf[:], [xsk, self.idbfk], [psxk])
                    yield
                    self.cp("act", ctok[:, 0:32], pst[:, 0:32], [pmk], [ctokk])
                    self.act(ctok[:, 32:48], ctok[:, 0:16], AF.Exp, [ctokk], [ctokk])
                    for g in range(2):
                        self.mm(pss[:, g * 128:(g + 1) * 128], xs[:, 8 + g, c0:c0 + 128], xs[:, 10 + g, c0:c0 + 128], True, True, [xsk], [pmk])
                    yield
                    self.tt("dve", Dm[:], R[:], bc(ctok[:, 0:16].unsqueeze(2), [128, 16, 128]), ALU.subtract, [Rk, ctokk], [Dmk])
                    self.ts("dve", Dm[:], Dm[:], 0.0, None, ALU.min, None, [Dmk], [Dmk])
                    self.tt("dve", sm[:], pss.rearrange("p (g t) -> p g t", g=2), bc(mk.unsqueeze(1), [128, 2, 128]), ALU.mult, [pmk, self.maskk], [smk])
                    px3 = psx[:].rearrange("p (e q) -> p e q", e=16)
                    self.tt("dve", xdt[:], px3, bc(ctok[:, 16:32].unsqueeze(2), [128, 16, 64]), ALU.mult, [psxk, ctokk], [xdtk])
                    if d == 0:
                        self.tt("dve", xsD[:], px3, bc(drow[:].unsqueeze(2), [128, 16, 64]), ALU.mult, [psxk, drowk], [xsDk])
                    yield
                    self.act(dec[:], Dm[:], AF.Exp, [Dmk], [deck])
                    for g in range(2):
                        self.tr(psx[:, g * 128:(g + 1) * 128], xs[:, 8 + g, c0:c0 + 128], self.idbf[:], [xsk, self.idbfk], [psxk])
                    self.cp("act", Btok[:], psx[:, 0:256], [psxk], [Btokk])
                    yield
                    for g in range(2):
                        self.tt("pool" if g else "dve", MT[:, g * 8:(g + 1) * 8, :], dec[:, g * 8:(g + 1) * 8, :], bc(sm[:, g:g + 1, :], [128, 8, 128]), ALU.mult, [deck, smk], [MTk])
                    col = 127 if d == 0 else 0
                    self.tt("pool", xdtw[:], xdt[:], bc(dec[:, :, col:col + 1], [128, 16, 64]), ALU.mult, [xdtk, deck], [xdtwk])
                    self.act(etot[:], R[:, :, col], AF.Exp, [Rk], [etotk])
                    yield
                    for g in range(2):
                        if d == 0:
                            self.mm(psy[:, :], self.idbf[:], xsD[:, g * 8:(g + 1) * 8, :].rearrange("p e q -> p (e q)"), True, False, [self.idbfk, xsDk], [psyk])
                        for e in range(8):
                            ge = g * 8 + e
                            self.mm(psy[:, e * 64:(e + 1) * 64], MT[:, ge, :], xdt[:, ge, :], (d == 1) and e == 0, e == 7, [MTk, xdtk], [psyk])
                        self.mm(psi[:, :], xs[:, 10 + g, c0:c0 + 128], Sbf[:, g * 512:(g + 1) * 512], True, True, [xsk, Sbfk], [psik])
                        yield
                        self.tt("dve", yi[:], psi[:].rearrange("p (e q) -> p e q", e=8), bc(ctok[:, 32 + g * 8:40 + g * 8].unsqueeze(2), [128, 8, 64]), ALU.mult, [psik, ctokk], [yik])
                        self.tt("dve", yo[:, g * 512:(g + 1) * 512], psy[:], yi[:].rearrange("p e q -> p (e q)"), ALU.add, [psyk, yik], [yok])
                        yield
                    fw.dma(yd[t0 + c0:t0 + c0 + 128, :], yo[:], reads=[yok], writes=[ydk])
                    for g in range(2):
                        self.mm(psi[:, :], Btok[:, g * 128:(g + 1) * 128], xdtw[:, g * 8:(g + 1) * 8, :].rearrange("p e q -> p (e q)"), True, True, [Btokk, xdtwk], [psik])
                        self.tt("pool", S[:, g * 8:(g + 1) * 8, :], S[:, g * 8:(g + 1) * 8, :], bc(etot[:, g * 8:(g + 1) * 8].unsqueeze(2), [128, 8, 64]), ALU.mult, [Sk, etotk], [Sk])
                        yield
                        self.tt("dve", S[:, g * 8:(g + 1) * 8, :], S[:, g * 8:(g + 1) * 8, :], psi[:].rearrange("p (e q) -> p e q", e=8), ALU.add, [Sk, psik], [Sk])
                    self.cp("act", Sbf[:], S[:].rearrange("p e q -> p (e q)"), [Sk], [Sbfk])
                    yield

    def p_ssd_comb(self, seq, l):
        fw, nc = self.fw, self.nc
        P = Pool(fw)
        W, Wk = P.sb([128, 8, 1024], BF16)
        self.load_w(P, W, Wk, self.w_in[l][:, O_Z:O_Z + 1024], 1024)
        nw, nwk = P.sb([128, 1024], F32)
        fw.dma(nw[:], self.snw_bc[l], writes=[nwk])
        T = seq.T
        hw = [P.sb([128, 8, T], BF16) for _ in range(2)]
        yf = [P.sb([128, 1024], F32) for _ in range(2)]
        yb = [P.sb([128, 1024], F32) for _ in range(2)]
        sz, szk = P.sb([128, 1024], F32)
        junk, junkk = P.sb([128, 512], BF16)
        ss, ssk = P.sb([128, 4], F32)
        yo = [P.sb([128, 1024], BF16) for _ in range(2)]
        ps = [P.ps([128, 1024]) for _ in range(2)]
        hd, hk = seq.hT["rm"]
        hv = hd.rearrange("(k p) t -> p k t", p=128)
        yd, ydk = seq.y
        per = T // 128
        ntile = seq.L // 128

        def load(i):
            h_, h_k = hw[(i // per) % 2]
            if i % per == 0:
                fw.dma(h_[:], hv[:, :, 2 + i * 128:2 + i * 128 + T], reads=[hk], writes=[h_k])
            f_, f_k = yf[i % 2]
            b_, b_k = yb[i % 2]
            fw.dma(f_[:], seq.ysd[0][0][i * 128:(i + 1) * 128, :], reads=[seq.ysd[0][1]], writes=[f_k])
            fw.dma(b_[:], seq.ysd[1][0][i * 128:(i + 1) * 128, :], reads=[seq.ysd[1][1]], writes=[b_k])
        load(0)
        for i in range(ntile):
            h_, h_k = hw[(i // per) % 2]
            j = i % per
            f_, f_k = yf[i % 2]
            b_, b_k = yb[i % 2]
            if i + 1 < ntile:
                load(i + 1)
            p_, p_k = ps[i % 2]
            for h in range(2):
                for k in range(8):
                    self.mm(p_[:, h * 512:(h + 1) * 512], h_[:, k, j * 128:(j + 1) * 128], W[:, k, h * 512:(h + 1) * 512], k == 0, k == 7, [h_k, Wk], [p_k])
            self.act(sz[:], p_[:], AF.Silu, [p_k], [szk])
            self.tt("pool", f_[:], f_[:], b_[:], ALU.add, [f_k, b_k], [f_k])
            self.tt("dve", f_[:], f_[:], sz[:], ALU.mult, [f_k, szk], [f_k])
            for g in range(2):
                self.act(junk[:], f_[:, g * 512:(g + 1) * 512], AF.Square, [f_k], [junkk, ssk], accum_out=ss[:, g:g + 1])
            self.ts("dve", ss[:, 2:4], ss[:, 0:2], 1.0 / 512, EPS, ALU.mult, ALU.add, [ssk], [ssk])
            self.act(ss[:, 2:4], ss[:, 2:4], AF.Sqrt, [ssk], [ssk])
            fw.op("dve", lambda: nc.vector.reciprocal(out=ss[:, 2:4], in_=ss[:, 2:4]), [ssk], [ssk])
            o_, o_k = yo[i % 2]
            for g in range(2):
                self.stt(o_[:, g * 512:(g + 1) * 512], f_[:, g * 512:(g + 1) * 512], ss[:, 2 + g:3 + g], nw[:, g * 512:(g + 1) * 512], ALU.mult, ALU.mult, [f_k, ssk, nwk], [o_k])
            fw.dma(yd[i * 128:(i + 1) * 128, 0:1024], o_[:], reads=[o_k], writes=[ydk])
            yield
        P.close()

    def p_out(self, seq, l, xap, xk, dst, dstk, final):
        fw, nc = self.fw, self.nc
        row = 0 if seq.is_lat else 1
        P = Pool(fw)
        W, Wk = P.sb([128, 16, 1024], BF16)
        st = [P.sb([128, 16, 128], F32) for _ in range(2)]
        wsrc = self.w_out[l].rearrange("(k p) n -> p k n", p=128)
        for i, c0 in enumerate(range(0, 1024, 128)):
            s, sk = st[i % 2]
            fw.dma(s[:], wsrc[:, :, c0:c0 + 128], writes=[sk])
            self.cp("pool" if i % 2 else "dve", W[:, :, c0:c0 + 128], s[:], [sk], [Wk])
        fnw, fnwk = P.sb([128, 1024], F32)
        fw.dma(fnw[:], self.fnw_bc, writes=[fnwk])
        yt = [P.sb([128, 2048], BF16) for _ in range(2)]
        yT, yTk = P.sb([128, 16, 128], BF16)
        xt = [P.sb([128, 1024], F32) for _ in range(2)]
        xo = [P.sb([128, 1024], F32) for _ in range(2)]
        junk, junkk = P.sb([128, 1024], BF16)
        ss, ssk = P.sb([128, 2], F32)
        pT, pTk = P.ps([128, 2048], BF16)
        ps = [P.ps([128, 1024]) for _ in range(2)]
        yd, ydk = seq.y
        ntile = seq.L // 128

        def load(i):
            y_, y_k = yt[i % 2]
            x_, x_k = xt[i % 2]
            fw.dma(y_[:], yd[i * 128:(i + 1) * 128, :], reads=[ydk], writes=[y_k])
            fw.dma(x_[:], xap[i * 128:(i + 1) * 128, :], reads=[xk], writes=[x_k])
        load(0)
        for i in range(ntile):
            y_, y_k = yt[i % 2]
            x_, x_k = xt[i % 2]
            if i + 1 < ntile:
                load(i + 1)
            for k in range(16):
                self.tr(pT[:, k * 128:(k + 1) * 128], y_[:, k * 128:(k + 1) * 128], self.idbf[:], [y_k, self.idbfk], [pTk])
            self.cp("act", yT[:].rearrange("p k t -> p (k t)"), pT[:], [pTk], [yTk])
            p_, p_k = ps[i % 2]
            for h in range(2):
                for k in range(16):
                    self.mm(p_[:, h * 512:(h + 1) * 512], yT[:, k, :], W[:, k, h * 512:(h + 1) * 512], k == 0, k == 15, [yTk, Wk], [p_k])
            o_, o_k = xo[i % 2]
            self.tt("dve", o_[:], p_[:], self.gate[:, row, :], ALU.mult, [p_k, self.gatek], [o_k])
            self.tt("pool", o_[:], o_[:], x_[:], ALU.add, [o_k, x_k], [o_k])
            if final:
                self.act(junk[:], o_[:], AF.Square, [o_k], [junkk, ssk], accum_out=ss[:, 0:1])
                self.ts("dve", ss[:, 1:2], ss[:, 0:1], 1.0 / D, EPS, ALU.mult, ALU.add, [ssk], [ssk])
                self.act(ss[:, 1:2], ss[:, 1:2], AF.Sqrt, [ssk], [ssk])
                fw.op("dve", lambda: nc.vector.reciprocal(out=ss[:, 1:2], in_=ss[:, 1:2]), [ssk], [ssk])
                self.stt(o_[:], o_[:], ss[:, 1:2], fnw[:], ALU.mult, ALU.mult, [o_k, ssk, fnwk], [o_k])
            fw.dma(dst[i * 128:(i + 1) * 128, :], o_[:], reads=[o_k], writes=[dstk])
        P.close()

    def p_gla_scan(self, l, d):
        fw, nc = self.fw, self.nc
        P = Pool(fw)
        self._pools.append(P)
        Wf, Wfk = P.sb([128, 8, 512], BF16)
        (Wq, Wqk), (Wv, Wvk) = self._gla_shared
        self.load_w(P, Wf, Wfk, self.w_in[l][:, (O_HFF if d == 0 else O_HFB):(O_HFF if d == 0 else O_HFB) + 512], 512)
        lbl, lblk = P.sb([128, 2, self.depth, 4], F32)
        lb, lbk = P.sb([128, 8], F32)
        fw.dma(lbl[:], self.lbl_fm, writes=[lblk])
        if l == 0:
            self.ms("dve", lb[:, 0:4], 0.0, [lbk])
        else:
            self.tt("dve", lb[:, 0:4], lbl[:, d, 1, :], lbl[:, d, 0, :], ALU.subtract, [lblk], [lbk])
            self.act(lb[:, 0:4], lb[:, 0:4], AF.Sigmoid, [lbk], [lbk])
        self.ts("dve", lb[:, 4:8], lb[:, 0:4], -1.0, 1.0, ALU.mult, ALU.add, [lbk], [lbk])
        S, Sk = P.sb([128, 4, 128], F32)
        Sbf, Sbfk = P.sb([128, 4, 128], BF16)
        self.ms("dve", S[:], 0.0, [Sk])
        self.ms("pool", Sbf[:], 0.0, [Sbfk])
        hw2 = [P.sb([128, 8, 512], BF16) for _ in range(1)]
        f_, fk = P.sb([128, 512], F32)
        lf, lfk = P.sb([128, 512], F32)
        kk, kkk = P.sb([128, 512], F32)
        cum, cumk = P.sb([128, 512], F32)
        E, Ek = P.sb([128, 512], F32)
        ex, exk = P.sb([128, 4, 512], F32)
        ref, refk = P.sb([128, 16], F32)
        etot, etotk = P.sb([128, 4, 8], F32)
        qt, qtk = P.sb([128, 4, 512], BF16)
        qp, qpk = P.sb([128, 4, 512], BF16)
        kt, ktk = P.sb([128, 4, 512], BF16)
        kh, khk = P.sb([128, 4, 512], BF16)
        vtok, vtokk = P.sb([64, 512], BF16)
        khtok, khtokk = P.sb([64, 4, 128], BF16)
        am, amk = P.sb([64, 4, 64], BF16)
        amf, amfk = P.sb([64, 256], F32)
        ost2 = [P.sb([64, 512], F32) for _ in range(2)]
        psf, psfk = P.ps([128, 512])
        psq, psqk = P.ps([128, 512])
        psv, psvk = psf, psfk
        pskh, pskhk = P.ps([128, 1024], BF16)
        psa, psak = psq, psqk
        pso, psok = P.ps([128, 512])
        yield
        psS, psSk = psq, psqk
        mk = self.mask[0:64, d, 0:64]
        gi = 0
        for seq in self.seqs:
            T = seq.T
            nch = T // 64
            nsup = seq.L // T
            hd, hk = seq.hT["cm"]
            hv = hd.rearrange("(k p) t -> p k t", p=128)
            od, odk = seq.od[d]
            sups = range(nsup) if d == 0 else range(nsup - 1, -1, -1)
            for si in sups:
                t0 = si * T
                hw, hwk = hw2[0]
                gi += 1
                fw.dma(hw[:, :, 0:T], hv[:, :, 2 + t0:2 + t0 + T], reads=[hk], writes=[hwk])
                for h in range(4):
                    for k in range(8):
                        self.mm(psf[:, 0:T], Wf[:, k, h * 128:(h + 1) * 128], hw[:, k, 0:T], k == 0, k == 7, [Wfk, hwk], [psfk])
                    for k in range(8):
                        self.mm(psq[:, 0:T], Wq[:, k, h * 128:(h + 1) * 128], hw[:, k, 0:T], k == 0, k == 7, [Wqk, hwk], [psqk])
                    self.act(f_[:, 0:T], psf[:, 0:T], AF.Sigmoid, [psfk], [fk])
                    self.ts("dve", f_[:, 0:T], f_[:, 0:T], lb[:, 4 + h:5 + h], lb[:, h:h + 1], ALU.mult, ALU.add, [fk, lbk], [fk])
                    self.act(lf[:, 0:T], f_[:, 0:T], AF.Ln, [fk], [lfk])
                    self.ts("pool", kk[:, 0:T], f_[:, 0:T], -1.0, 1.0, ALU.mult, ALU.add, [fk], [kkk])
                    fw.op("dve", lambda: nc.vector.tensor_tensor_scan(out=cum[:, 0:T], data0=self.rst[:, 1, 0:T], data1=lf[:, 0:T], initial=0.0, op0=ALU.mult, op1=ALU.add), [self.rstk, lfk], [cumk])
                    c3 = cum[:, 0:T].rearrange("p (c t) -> p c t", t=64)
                    if d == 1:
                        self.cp("dve", ref[:, 8:8 + nch], cum[:, 63:T:64], [cumk], [refk])
                        self.tt("dve", c3, bc(ref[:, 8:8 + nch].unsqueeze(2), [128, nch, 64]), c3, ALU.subtract, [refk, cumk], [cumk])
                        self.tt("dve", cum[:, 0:T], cum[:, 0:T], lf[:, 0:T], ALU.add, [cumk, lfk], [cumk])
                    tcol = 63 if d == 0 else 0
                    self.cp("dve", ref[:, 0:nch], cum[:, 32:T:64], [cumk], [refk])
                    self.cp("dve", ref[:, 8:8 + nch], cum[:, tcol:T:64], [cumk], [refk])
                    E3 = E[:, 0:T].rearrange("p (c t) -> p c t", t=64)
                    self.tt("dve", E3, c3, bc(ref[:, 0:nch].unsqueeze(2), [128, nch, 64]), ALU.subtract, [cumk, refk], [Ek])
                    self.act(ex[:, 0, 0:T], E[:, 0:T], AF.Exp, [Ek], [exk])
                    self.act(ex[:, 1, 0:T], E[:, 0:T], AF.Exp, [Ek], [exk], scale=-1.0)
                    self.act(ex[:, 2, 0:T], cum[:, 0:T], AF.Exp, [cumk], [exk])
                    self.tt("dve", E3, bc(ref[:, 8:8 + nch].unsqueeze(2), [128, nch, 64]), c3, ALU.subtract, [cumk, refk], [Ek])
                    self.act(ex[:, 3, 0:T], E[:, 0:T], AF.Exp, [Ek], [exk])
                    self.act(etot[:, h, 0:nch], ref[:, 8:8 + nch], AF.Exp, [refk], [etotk])
                    self.tt("dve", qt[:, h, 0:T], psq[:, 0:T], ex[:, 0, 0:T], ALU.mult, [psqk, exk], [qtk])
                    self.tt("dve", qp[:, h, 0:T], psq[:, 0:T], ex[:, 2, 0:T], ALU.mult, [psqk, exk], [qpk])
                    self.tt("pool", kt[:, h, 0:T], kk[:, 0:T], ex[:, 1, 0:T], ALU.mult, [kkk, exk], [ktk])
                    self.tt("pool", kh[:, h, 0:T], kk[:, 0:T], ex[:, 3, 0:T], ALU.mult, [kkk, exk], [khk])
                    yield
                chs = range(nch) if d == 0 else range(nch - 1, -1, -1)
                for ci in chs:
                    c0 = ci * 64
                    for k in range(8):
                        self.mm(psv[0:64, :], hw[:, k, c0:c0 + 64], Wv[:, k, :], k == 0, k == 7, [hwk, Wvk], [psvk])
                    self.cp("act", vtok[:], psv[0:64, :], [psvk], [vtokk])
                    yield
                    for h in range(4):
                        self.tr(pskh[0:64, h * 128:(h + 1) * 128], kh[:, h, c0:c0 + 64], self.idbf[:], [khk, self.idbfk], [pskhk])
                    self.cp("act", khtok[:].rearrange("p h k -> p (h k)"), pskh[0:64, 0:512], [pskhk], [khtokk])
                    yield
                    for h in range(4):
                        self.mm(psa[0:64, h * 64:(h + 1) * 64], kt[:, h, c0:c0 + 64], qt[:, h, c0:c0 + 64], True, True, [ktk, qtk], [psak])
                    self.ts("dve", amf[:], psa[0:64, 0:256], 1.0e30, -1.0e30, ALU.min, ALU.max, [psak], [amfk])
                    self.tt("dve", am[:], amf[:].rearrange("p (h t) -> p h t", h=4), bc(mk.unsqueeze(1), [64, 4, 64]), ALU.mult, [amfk, self.maskk], [amk])
                    yield
                    for h in range(4):
                        self.mm(pso[0:64, h * 128:(h + 1) * 128], am[:, h, :], vtok[:, h * 128:(h + 1) * 128], True, False, [amk, vtokk], [psok])
                        self.mm(pso[0:64, h * 128:(h + 1) * 128], qp[:, h, c0:c0 + 64], Sbf[:, h, :], False, True, [qpk, Sbfk], [psok])
                    ost, ostk = ost2[ci % 2]
                    self.cp("act", ost[:], pso[0:64, :], [psok], [ostk])
                    fw.dma(od[t0 + c0:t0 + c0 + 64, :], ost[:], reads=[ostk], writes=[odk])
                    yield
                    for h in range(4):
                        self.mm(psS[:, h * 128:(h + 1) * 128], khtok[:, h, :], vtok[:, h * 128:(h + 1) * 128], True, True, [khtokk, vtokk], [psSk])
                    yield
                    for h in range(4):
                        self.stt(S[:, h, :], S[:, h, :], etot[:, h, ci:ci + 1], psS[:, h * 128:(h + 1) * 128], ALU.mult, ALU.add, [Sk, etotk, psSk], [Sk])
                    self.cp("act", Sbf[:], S[:], [Sk], [Sbfk])
                    yield

    def p_gla_comb(self, seq, l):
        fw, nc = self.fw, self.nc
        P = Pool(fw)
        W, Wk = P.sb([128, 8, 512], BF16)
        self.load_w(P, W, Wk, self.w_in[l][:, O_HGG:O_HGG + 512], 512)
        nw, nwk = P.sb([128, 512], F32)
        fw.dma(nw[:], self.hnw_bc[l], writes=[nwk])
        T = seq.T
        hw = [P.sb([128, 8, T], BF16) for _ in range(2)]
        of = [P.sb([128, 512], F32) for _ in range(2)]
        ob = [P.sb([128, 512], F32) for _ in range(2)]
        sg, sgk = P.sb([128, 512], F32)
        junk, junkk = P.sb([128, 128], BF16)
        ss, ssk = P.sb([128, 8], F32)
        yo = [P.sb([128, 512], BF16) for _ in range(2)]
        ps = [P.ps([128, 512]) for _ in range(2)]
        hd, hk = seq.hT["cm"]
        hv = hd.rearrange("(k p) t -> p k t", p=128)
        yd, ydk = seq.y
        per = T // 128
        ntile = seq.L // 128

        def load(i):
            h_, h_k = hw[(i // per) % 2]
            if i % per == 0:
                fw.dma(h_[:], hv[:, :, 2 + i * 128:2 + i * 128 + T], reads=[hk], writes=[h_k])
            f_, f_k = of[i % 2]
            b_, b_k = ob[i % 2]
            fw.dma(f_[:], seq.od[0][0][i * 128:(i + 1) * 128, :], reads=[seq.od[0][1]], writes=[f_k])
            fw.dma(b_[:], seq.od[1][0][i * 128:(i + 1) * 128, :], reads=[seq.od[1][1]], writes=[b_k])
        load(0)
        for i in range(ntile):
            h_, h_k = hw[(i // per) % 2]
            j = i % per
            f_, f_k = of[i % 2]
            b_, b_k = ob[i % 2]
            if i + 1 < ntile:
                load(i + 1)
            p_, p_k = ps[i % 2]
            for k in range(8):
                self.mm(p_[:, :], h_[:, k, j * 128:(j + 1) * 128], W[:, k, :], k == 0, k == 7, [h_k, Wk], [p_k])
            self.act(sg[:], p_[:], AF.Silu, [p_k], [sgk])
            self.tt("pool", f_[:], f_[:], b_[:], ALU.add, [f_k, b_k], [f_k])
            for h in range(4):
                self.act(junk[:], f_[:, h * 128:(h + 1) * 128], AF.Square, [f_k], [junkk, ssk], accum_out=ss[:, h:h + 1])
            self.ts("dve", ss[:, 4:8], ss[:, 0:4], 1.0 / 128, EPS, ALU.mult, ALU.add, [ssk], [ssk])
            self.act(ss[:, 4:8], ss[:, 4:8], AF.Sqrt, [ssk], [ssk])
            fw.op("dve", lambda: nc.vector.reciprocal(out=ss[:, 4:8], in_=ss[:, 4:8]), [ssk], [ssk])
            self.tt("pool", sg[:], sg[:], nw[:], ALU.mult, [sgk, nwk], [sgk])
            o_, o_k = yo[i % 2]
            for h in range(4):
                self.stt(o_[:, h * 128:(h + 1) * 128], f_[:, h * 128:(h + 1) * 128], ss[:, 4 + h:5 + h], sg[:, h * 128:(h + 1) * 128], ALU.mult, ALU.mult, [f_k, ssk, sgk], [o_k])
            for (p0, n, dstap) in self.x_tile_src(seq, yd, i, "cm"):
                fw.dma(dstap[:, 1024:1536], o_[p0:p0 + n, :], reads=[o_k], writes=[ydk])
            yield
        P.close()

    def p_hy_filter(self, seq, l):
        fw, nc = self.fw, self.nc
        Ls = seq.L
        nb2 = 2 * Ls // 128
        NP = 2 * Ls
        P = Pool(fw)
        w1, w1k = P.sb([33, 64], F32)
        w2, w2k = P.sb([64, 64], F32)
        w3, w3k = P.sb([64, 1024], F32)
        fv, fvk = P.sb([64, 4], F32)
        fb, fbk = P.sb([64, 2], F32)
        dl, dlk = P.sb([128, 512], F32)
        tn, tnk = P.sb([128, nb2], F32)
        ones, onesk = P.sb([128, 128], F32)
        fw.dma(w1[:], self.fw1[l], writes=[w1k])
        fw.dma(w2[:], self.fw2[l], writes=[w2k])
        fw.dma(w3[:], self.fw3[l], writes=[w3k])
        fw.dma(fv[:, 0:3], self.fvec[l], writes=[fvk])
        fw.dma(dl[:], self.c_delta, writes=[dlk])
        fw.dma(tn[:], self.c_tneg[Ls], writes=[tnk])
        self.ms("dve", ones[:], 1.0, [onesk])
        self.ts("dve", fb[:], fv[:, 0:2], fv[:, 2:3], None, ALU.mult, None, [fvk], [fbk])
        CH = 512
        zt = [P.sb([33, CH], F32) for _ in range(2)]
        h1, h1k = P.sb([64, CH], F32)
        msk, mskk = P.sb([64, CH], F32)
        h2, h2k = P.sb([64, CH], F32)
        ksb, ksbk = P.sb([128, 4, 128], F32)
        dcy, dcyk = P.sb([128, 512], F32)
        kab, kabk = P.sb([128, 512], F32)
        kbf = [P.sb([128, 4, 4, 128], BF16) for _ in range(2)]
        nacc, nacck = P.sb([128, 512], F32)
        self.ms("dve", nacc[:], 0.0, [nacck])
        ps1, ps1k = P.ps([128, 512])
        ps2, ps2k = P.ps([128, 512])
        psk, pskk = P.ps([128, 512])
        kd, kdk = seq.kmat
        zsrc = self.c_z[Ls]
        TWO_PI = 2.0 * math.pi

        def sinwrap(out, pin, pink, col):
            self.act(out, pin, AF.Identity, [pink, fvk, fbk], [h1k if out is h1 else h2k], scale=fv[:, 2:3], bias=fb[:, col:col + 1])
        for ci in range(NP // CH):
            z_, z_k = zt[ci % 2]
            fw.dma(z_[:], zsrc[:, ci * CH:(ci + 1) * CH], writes=[z_k])
            self.mm(ps1[0:64, :], w1[:], z_[:], True, True, [w1k, z_k], [ps1k])
            for (hh, hhk, pp, ppk, col) in ((h1, h1k, ps1, ps1k, 0), (h2, h2k, ps2, ps2k, 1)):
                if col == 1:
                    self.mm(ps2[0:64, :], w2[:], h1[:], True, True, [w2k, h1k], [ps2k])
                self.act(hh[:], pp[0:64, :], AF.Identity, [ppk, fvk, fbk], [hhk], scale=fv[:, 2:3], bias=fb[:, col:col + 1])
                self.ts("dve", msk[:], hh[:], math.pi, -TWO_PI, ALU.is_gt, ALU.mult, [hhk], [mskk])
                self.tt("dve", hh[:], hh[:], msk[:], ALU.add, [hhk, mskk], [hhk])
                self.ts("dve", msk[:], hh[:], -math.pi, TWO_PI, ALU.is_lt, ALU.mult, [hhk], [mskk])
                self.tt("dve", hh[:], hh[:], msk[:], ALU.add, [hhk, mskk], [hhk])
                self.act(hh[:], hh[:], AF.Sin, [hhk], [hhk])
            k_, k_k = kbf[ci % 2]
            for j in range(CH // 128):
                dd = ci * (CH // 128) + j
                side = 1 if dd < nb2 // 2 else 0
                self.mm(psk[:, :], h2[:, j * 128:(j + 1) * 128], w3[:, side * 512:(side + 1) * 512], True, True, [h2k, w3k], [pskk])
                self.act(dcy[:], dl[:], AF.Exp, [dlk, tnk], [dcyk], scale=tn[:, dd:dd + 1])
                self.tt("dve", ksb[:].rearrange("p a c -> p (a c)"), psk[:, :], dcy[:], ALU.mult, [pskk, dcyk], [ksbk])
                self.cp("act", k_[:, :, j, :], ksb[:], [ksbk], [k_k])
                self.act(kab[:], ksb[:].rearrange("p a c -> p (a c)"), AF.Abs, [ksbk], [kabk])
                self.tt("pool", nacc[:], nacc[:], kab[:], ALU.add, [nacck, kabk], [nacck])
            d0 = ci * (CH // 128)
            for sg in range(8):
                fw.dma(kd[sg, :, d0:d0 + CH // 128, :], k_[:, sg // 2, :, (sg % 2) * 64:(sg % 2 + 1) * 64], reads=[k_k], writes=[kdk])
        self.mm(psk[:, :], ones[:], nacc[:], True, True, [onesk, nacck], [pskk])
        self.ts("dve", nacc[:], psk[:, :], EPS, None, ALU.add, None, [pskk], [nacck])
        fw.op("dve", lambda: nc.vector.reciprocal(out=nacc[:], in_=nacc[:]), [nacck], [nacck])
        fw.dma(seq.rn[0], nacc[:], reads=[nacck], writes=[seq.rn[1]])
        P.close()

    def p_hy_front(self, seq, l):
        fw, nc = self.fw, self.nc
        P = Pool(fw)
        W, Wk = P.sb([128, 8, 1536], BF16)
        Wg, Wgk = P.sb([128, 8, 512], BF16)
        self.load_w(P, W, Wk, self.w_in[l][:, O_HY:O_HY + 1536], 1536)
        self.load_w(P, Wg, Wgk, self.w_in[l][:, O_HYG:O_HYG + 512], 512)
        cw, cwk = P.sb([128, 12, 3], F32)
        cb, cbk = P.sb([128, 12], F32)
        hb, hbk = P.sb([128, 512], F32)
        fw.dma(cw[:], self.hcw_fm[l], writes=[cwk])
        fw.dma(cb[:], self.hcb_fm[l], writes=[cbk])
        fw.dma(hb[:], self.hyb_bc[l], writes=[hbk])
        T = seq.T
        nt = T // 128
        hw = [P.sb([128, 8, T + 2], BF16) for _ in range(2)]
        ps = [P.ps([128, 1024]) for _ in range(2)]
        cv, cvk = P.sb([128, 12, T], F32)
        wf, wfk = P.sb([128, 4, T], F32)
        wb = [P.sb([128, 4, T], BF16) for _ in range(2)]
        psg, psgk = P.ps([128, 512])
        psT, psTk = P.ps([128, 1024])
        sg, sgk = P.sb([128, 512], F32)
        o1 = [P.sb([128, 512], F32) for _ in range(2)]
        o2 = [P.sb([128, 512], F32) for _ in range(2)]
        hd, hk = seq.hT["rm"]
        hv = hd.rearrange("(k p) t -> p k t", p=128)
        wd, wdk = seq.wT
        wdv = wd.rearrange("(b p) t -> p b t", p=128)
        def load(si):
            h_, h_k = hw[si % 2]
            fw.dma(h_[:], hv[:, :, 1 + si * T:1 + si * T + T + 2], reads=[hk], writes=[h_k])
        load(0)
        for si in range(seq.L // T):
            t0 = si * T
            h_, h_k = hw[si % 2]
            if si + 1 < seq.L // T:
                load(si + 1)
            for b in range(12):
                p_, p_k = ps[b % 2]
                for k in range(8):
                    self.mm(p_[:, 0:T], W[:, k, b * 128:(b + 1) * 128], h_[:, k, 0:T], k == 0, k == 7, [Wk, h_k], [p_k])
                for k in range(8):
                    self.mm(p_[:, T:T + 2], W[:, k, b * 128:(b + 1) * 128], h_[:, k, T:T + 2], k == 0, k == 7, [Wk, h_k], [p_k])
                self.act(cv[:, b, :], p_[:, 0:T], AF.Identity, [p_k, cwk, cbk], [cvk], scale=cw[:, b, 0:1], bias=cb[:, b:b + 1])
                for k in range(1, 3):
                    self.stt(cv[:, b, :], p_[:, k:k + T], cw[:, b, k:k + 1], cv[:, b, :], ALU.mult, ALU.add, [p_k, cwk, cvk], [cvk])
            self.tt("dve", wf[:], cv[:, 8:12, :], cv[:, 0:4, :], ALU.mult, [cvk], [wfk])
            w_, w_k = wb[si % 2]
            self.cp("pool", w_[:], wf[:], [wfk], [w_k])
            fw.dma(wdv[:, :, 127 + t0:127 + t0 + T], w_[:], reads=[w_k], writes=[wdk])
            for j in range(nt):
                for b in range(4):
                    self.tr(psT[:, b * 128:(b + 1) * 128], cv[:, 4 + b, j * 128:(j + 1) * 128], self.id32[:], [cvk, self.id32k], [psTk])
                    self.tr(psT[:, 512 + b * 128:512 + (b + 1) * 128], wf[:, b, j * 128:(j + 1) * 128], self.id32[:], [wfk, self.id32k], [psTk])
                for k in range(8):
                    self.mm(psg[:, :], h_[:, k, 1 + j * 128:1 + (j + 1) * 128], Wg[:, k, :], k == 0, k == 7, [h_k, Wgk], [psgk])
                self.act(sg[:], psg[:], AF.Silu, [psgk], [sgk])
                a_, a_k = o1[j % 2]
                b_, b_k = o2[j % 2]
                self.tt("dve", a_[:], psT[:, 0:512], sg[:], ALU.mult, [psTk, sgk], [a_k])
                self.tt("dve", b_[:], psT[:, 512:1024], hb[:], ALU.mult, [psTk, hbk], [b_k])
                self.tt("pool", b_[:], b_[:], a_[:], ALU.mult, [b_k, a_k], [b_k])
                r0 = t0 + j * 128
                fw.dma(seq.hx0g[0][r0:r0 + 128, :], a_[:], reads=[a_k], writes=[seq.hx0g[1]])
                fw.dma(seq.hbw[0][r0:r0 + 128, :], b_[:], reads=[b_k], writes=[seq.hbw[1]])
        P.close()

    def p_hy_conv(self, seqs, l):
        fw, nc = self.fw, self.nc
        Lm = max(s.L for s in seqs)
        nbm = Lm // 128
        P = Pool(fw)
        self.hyP = P
        km, kmk = P.sb([128, 2 * nbm, 64], BF16)
        rn, rnk = P.sb([128, 512], F32)
        wsh = [P.sb([128, Lm + 128], BF16) for _ in range(2)]
        yg, ygk = P.sb([128, nbm, 64], BF16)
        IBm = min(nbm, 8)
        hx = [P.sb([128, IBm, 64], F32) for _ in range(2)]
        hbw = [P.sb([128, IBm, 64], F32) for _ in range(2)]
        yo = [P.sb([128, IBm, 64], BF16) for _ in range(2)]
        pys = [P.ps([128, 512]) for _ in range(3)]
        yield
        gc = 0
        for seq in seqs:
            Ls = seq.L
            nb = Ls // 128
            IB = min(nb, 8)
            fw.dma(rn[:], seq.rn[0], reads=[seq.rn[1]], writes=[rnk])
            wd, wdk = seq.wT
            yd, ydk = seq.y
            WL = Ls + 256
            for sg in range(8):
                fw.dma(km[:, 0:2 * nb, :], seq.kmat[0][sg], reads=[seq.kmat[1]], writes=[kmk])
                for c in range(64):
                    ch = sg * 64 + c
                    w_, w_k = wsh[gc % 2]
                    src = bass.AP(tensor=wd.tensor, offset=ch * WL, ap=[[1, 128], [1, Ls + 128]])
                    fw.dma(w_[:, 0:Ls + 128], src, reads=[wdk], writes=[w_k])
                    pyt, pyk = pys[gc % 3]
                    gc += 1
                    py = pyt[:, 0:nb]
                    for j in range(nb + 1):
                        self.mm(py, w_[:, j * 128:(j + 1) * 128], km[:, nb - j:2 * nb - j, c], j == 0, j == nb, [w_k, kmk], [pyk])
                        if j % 13 == 12:
                            yield
                    self.act(yg[:, 0:nb, c], py, AF.Copy, [pyk, rnk], [ygk], scale=rn[:, ch:ch + 1])
                    yield
                for ib in range(nb // IB):
                    a_, a_k = hx[ib % 2]
                    b_, b_k = hbw[ib % 2]
                    o_, o_k = yo[ib % 2]
                    r0 = ib * IB * 128
                    fw.dma(a_[:, 0:IB, :], seq.hx0g[0][r0:r0 + IB * 128, sg * 64:(sg + 1) * 64].rearrange("(i p) c -> p i c", p=128), reads=[seq.hx0g[1]], writes=[a_k])
                    fw.dma(b_[:, 0:IB, :], seq.hbw[0][r0:r0 + IB * 128, sg * 64:(sg + 1) * 64].rearrange("(i p) c -> p i c", p=128), reads=[seq.hbw[1]], writes=[b_k])
                    self.tt("dve", a_[:, 0:IB, :], a_[:, 0:IB, :], yg[:, ib * IB:(ib + 1) * IB, :], ALU.mult, [a_k, ygk], [a_k])
                    self.tt("pool", o_[:, 0:IB, :], a_[:, 0:IB, :], b_[:, 0:IB, :], ALU.add, [a_k, b_k], [o_k])
                    fw.dma(yd[r0:r0 + IB * 128, 1536 + sg * 64:1536 + (sg + 1) * 64].rearrange("(i p) c -> p i c", p=128), o_[:, 0:IB, :], reads=[o_k], writes=[ydk])
                    yield

    @staticmethod
    def _drain(g):
        if g is not None:
            for _ in g:
                pass

    def _corun(self, main_gens, bg, ratio=2):
        bg_alive = bg is not None
        for g in main_gens:
            if g is None:
                continue
            for _ in g:
                if bg_alive:
                    for _r in range(ratio):
                        try:
                            next(bg)
                        except StopIteration:
                            bg_alive = False
                            break
        if bg_alive:
            self._drain(bg)

    def _close_pools(self):
        while self._pools:
            self._pools.pop().close()

    def build(self):
        self.setup()
        self._pools = []
        dp = self.depth
        sc, sl = self.seqs
        xc, xck = sc.x, sc.xk
        xl, xlk = sl.x, sl.xk
        for l in range(dp):
            last = l == dp - 1
            act_seqs = [sl] if last else [sc, sl]
            self.p_adaln(l)
            self.p_norm(sc, xc, xck, "rm")
            self.p_norm(sl, xl, xlk, "rm")
            self.p_norm(sl, xl, xlk, "cm")
            for s in act_seqs:
                self.p_hy_filter(s, l)
                self.p_hy_front(s, l)
            for s in (sc, sl):
                self.p_ssd_front(s, l)
            g1 = self.p_ssd_scan(l, 1)
            next(g1)
            g0 = self.p_ssd_scan(l, 0)
            self._corun([g0], g1, ratio=1)
            self._close_pools()
            SP_ = Pool(self.fw)
            Wq = SP_.sb([128, 8, 512], BF16)
            Wv = SP_.sb([128, 8, 512], BF16)
            self.load_w(SP_, Wq[0], Wq[1], self.w_in[l][:, O_HQ:O_HQ + 512], 512)
            self.load_w(SP_, Wv[0], Wv[1], self.w_in[l][:, O_HGI:O_HGI + 512], 512)
            self._gla_shared = (Wq, Wv)
            g1 = self.p_gla_scan(l, 1)
            next(g1)
            g0 = self.p_gla_scan(l, 0)
            self._corun([g0], g1, ratio=1)
            self._close_pools()
            SP_.close()
            for s in act_seqs:
                self._drain(self.p_ssd_comb(s, l))
                self._drain(self.p_gla_comb(s, l))
            bg = self.p_hy_conv(act_seqs, l)
            self._drain(bg)
            self.hyP.close()
            if not last:
                self.p_out(sc, l, xc, xck, sc.x1[0], sc.x1[1], False)
                self.p_out(sl, l, xl, xlk, sl.x1[0], sl.x1[1], False)
                xc, xck = sc.x1
                xl, xlk = sl.x1
            else:
                self.p_out(sl, l, xl, xlk, self.out, self.outk, True)
        self.GP.close()
        self.fw.finish()
        return self.nc


def _consts(L, LC):
    import ml_dtypes
    c = {}
    c["c_id32"] = np.eye(128, dtype=np.float32)
    c["c_idbf"] = np.eye(128, dtype=np.float32).astype(ml_dtypes.bfloat16)
    s = np.arange(128)[:, None]
    t = np.arange(128)[None, :]
    c["c_mask"] = np.stack([(s <= t), (s >= t)]).astype(np.float32)
    sel = np.zeros((2, 256), np.float32)
    sel[0, 0:128] = 1.0
    sel[1, 128:256] = 1.0
    c["c_sel"] = sel
    rst = np.ones((128, 2, 512), np.float32)
    rst[:, 0, 0::128] = 0.0
    rst[:, 1, 0::64] = 0.0
    c["c_rst"] = rst
    delta = np.abs(np.linspace(HY_MIN_DECAY, HY_MAX_DECAY, 512, dtype=np.float32))
    c["c_delta"] = np.ascontiguousarray(np.broadcast_to(delta[None, :], (128, 512))).astype(np.float32)
    for Ls in sorted(set([L, LC])):
        nb = Ls // 128
        dd = np.arange(2 * nb)[None, :]
        rp = np.arange(128)[:, None]
        lam = 128 * (dd - nb) + 127 - rp
        pos = np.abs(lam)
        invalid = pos >= Ls
        pos = np.where(invalid, 0, pos)
        tl = np.linspace(0.0, 1.0, Ls, dtype=np.float32)
        tt = tl[pos].astype(np.float32)
        tneg = np.where(invalid, -1.0e4, -tt).astype(np.float32)
        c["c_tneg%d" % Ls] = np.ascontiguousarray(tneg)
        posn = pos.T.reshape(-1)
        bands = np.linspace(1e-4, 15.0, 16, dtype=np.float32)
        ang = (np.float32(2.0 * math.pi / Ls) * posn.astype(np.float32)[:, None] * bands[None, :]).astype(np.float32)
        z = np.concatenate([tl[posn][:, None], np.cos(ang.astype(np.float64)), -np.sin(ang.astype(np.float64))], axis=1)
        c["c_z%d" % Ls] = np.ascontiguousarray(z.T).astype(np.float32)
    return c


def _layout_inputs(inp, b, L, LC, depth):
    f = lambda a: np.ascontiguousarray(np.asarray(a, dtype=np.float32))
    m = {}
    m["x"] = f(inp["x"][b])
    m["ctx"] = f(inp["ctx"][b])
    c2 = np.stack([np.asarray(inp["c"][b]).reshape(128, 8), np.asarray(inp["c_ctx"]).reshape(128, 8)], axis=-1)
    m["c2"] = f(c2)
    m["ada_w"] = f(inp["ada_w"])
    m["ada_b2"] = f(np.broadcast_to(np.asarray(inp["ada_b"])[:, None, :], (depth, 2, 3 * D)))
    m["norm_w2"] = f(np.broadcast_to(np.asarray(inp["norm_w"])[:, None, :], (depth, 2, D)))
    m["fnw_bc"] = f(np.broadcast_to(np.asarray(inp["final_norm_w"])[None, :], (128, D)))
    m["w_in"] = f(inp["w_in"])
    m["w_out"] = f(inp["w_out"])
    cw = np.asarray(inp["ssd_conv_w"])
    m["cw_fm"] = f(cw.reshape(depth, 5, 12, 128).transpose(0, 3, 2, 1))
    m["cb_fm"] = f(np.asarray(inp["ssd_conv_b"]).reshape(depth, 12, 128).transpose(0, 2, 1))
    hcw = np.asarray(inp["hy_conv_w"])
    m["hcw_fm"] = f(hcw.reshape(depth, 3, 12, 128).transpose(0, 3, 2, 1))
    m["hcb_fm"] = f(np.asarray(inp["hy_conv_b"]).reshape(depth, 12, 128).transpose(0, 2, 1))
    m["dtb_fm"] = f(np.asarray(inp["ssd_dt_bias"]).reshape(depth, 32, 1))
    m["alog_fm"] = f(np.asarray(inp["ssd_a_log"]).reshape(depth, 32, 1))
    m["d_bc"] = f(np.broadcast_to(np.asarray(inp["ssd_d"])[:, None, :], (depth, 128, 16)))
    m["snw_bc"] = f(np.broadcast_to(np.asarray(inp["ssd_norm_w"])[:, None, :], (depth, 128, D)))
    lbl = np.asarray(inp["hg_lb_logits"])
    m["lbl_fm"] = f(lbl.reshape(2, depth, 4, 128).transpose(3, 0, 1, 2))
    m["hnw_bc"] = f(np.broadcast_to(np.asarray(inp["hg_norm_w"])[:, None, :], (depth, 128, 512)))
    m["hyb_bc"] = f(np.broadcast_to(np.asarray(inp["hy_bias"])[:, None, :], (depth, 128, 512)))
    m["fw1"] = f(inp["hy_filt_w1"])
    m["fw2"] = f(inp["hy_filt_w2"])
    m["fw3"] = f(inp["hy_filt_w3"])
    m["fvec"] = f(np.stack([np.asarray(inp["hy_filt_b1"]), np.asarray(inp["hy_filt_b2"]), np.asarray(inp["hy_filt_freq"])], axis=-1))
    return m


_CACHE = {}


def run(inputs, L, LC, depth, nb, debug=()):
    key = (L, LC, depth, tuple(debug))
    if key not in _CACHE:
        bld = Builder(L, LC, depth, debug)
        bld.build()
        _CACHE[key] = bld
    bld = _CACHE[key]
    consts = _consts(L, LC)
    in_maps = []
    for b in range(nb):
        m = _layout_inputs(inputs, b, L, LC, depth)
        m.update(consts)
        in_maps.append({k: m[k] for k in bld.inputs})
    res = run_bass_kernel_spmd(bld.nc, in_maps, core_ids=list(range(nb)))
    return res, bld


def kernel(**inputs):
    res, bld = run(inputs, 8192, 256, 2, 8)
    out = np.stack([np.asarray(r["out"], dtype=np.float32) for r in res.results], axis=0)
    return out
```

```python
import numpy as np
import contextlib
import concourse.bass as bass
import concourse.mybir as mybir
from concourse.bass_utils import run_bass_kernel_spmd

F32 = mybir.dt.float32
BF16 = mybir.dt.bfloat16
AF = mybir.ActivationFunctionType
ALU = mybir.AluOpType


class Tk:
    __slots__ = ("w", "r", "multi")

    def __init__(self, multi=False):
        self.w = {}
        self.r = {}
        self.multi = multi


class FW:
    EPOCH = 30000

    def __init__(self, nc):
        self.nc = nc
        self.eng = {"pe": nc.tensor, "dve": nc.vector, "act": nc.scalar, "pool": nc.gpsimd, "sp": nc.sync}
        self.stack = contextlib.ExitStack()
        self.sem = {}
        self.cnt = {}
        self.semobj = {}
        self.waited = {e: {} for e in self.eng}
        self.nsem = 0
        for e in self.eng:
            self._newsem(e)
        self.dring = []
        for i in range(12):
            s = self._alloc_sem("dma%d" % i)
            self.dring.append([s, 0])
        self.dpos = 0
        self.ninst = 0

    def _alloc_sem(self, name):
        s = self.stack.enter_context(self.nc.semaphore(name))
        self.nsem += 1
        self.semobj[id(s)] = s
        return s

    def _newsem(self, e):
        self.sem[e] = self._alloc_sem("s_%s_%d" % (e, self.nsem))
        self.cnt[e] = 0

    def _deps(self, reads, writes):
        d = {}

        def add(dd):
            for k, v in dd.items():
                if d.get(k, 0) < v:
                    d[k] = v
        for t in reads:
            add(t.w)
        for t in writes:
            if not t.multi:
                add(t.w)
            add(t.r)
        return d

    def _wait(self, e, deps, skip_own=False):
        w = self.waited[e]
        own = id(self.sem[e])
        for k, v in deps.items():
            if skip_own and k == own:
                continue
            if w.get(k, 0) >= v:
                continue
            self.eng[e].wait_ge(self.semobj[k], v)
            w[k] = v

    def _record(self, tok, reads, writes):
        k, v = tok
        for t in reads:
            if t.r.get(k, 0) < v:
                t.r[k] = v
        for t in writes:
            if t.multi:
                if t.w.get(k, 0) < v:
                    t.w[k] = v
            else:
                t.w = {k: v}
                t.r = {}

    def op(self, e, fn, reads=(), writes=()):
        deps = self._deps(reads, writes)
        self._wait(e, deps, skip_own=(e == "pe"))
        if self.cnt[e] >= self.EPOCH:
            self._newsem(e)
        inst = fn()
        inst.then_inc(self.sem[e], 1)
        self.cnt[e] += 1
        self.ninst += 1
        self._record((id(self.sem[e]), self.cnt[e]), reads, writes)
        return inst

    def dma(self, out, in_, reads=(), writes=(), q="sp", **kw):
        deps = self._deps(reads, writes)
        slot = self.dring[self.dpos % len(self.dring)]
        self.dpos += 1
        if slot[1] > 0:
            deps[id(slot[0])] = max(deps.get(id(slot[0]), 0), slot[1])
        self._wait(q, deps)
        if slot[1] >= 16 * 1800:
            slot[0] = self._alloc_sem("dmax%d" % self.nsem)
            slot[1] = 0
        inst = self.eng[q].dma_start(out=out, in_=in_, **kw)
        slot[1] += 16
        inst.then_inc(slot[0], 16)
        self.ninst += 1
        self._record((id(slot[0]), slot[1]), reads, writes)
        return inst

    def barrier(self):
        allt = {}
        for e in self.eng:
            if self.cnt[e] > 0:
                allt[id(self.sem[e])] = self.cnt[e]
        for s, v in self.dring:
            if v > 0:
                allt[id(s)] = v
        for e in self.eng:
            self._wait(e, dict(allt))

    def finish(self):
        self.barrier()
        self.stack.close()


class Pool:
    N = 0

    def __init__(self, fw):
        self.fw = fw
        self.stack = contextlib.ExitStack()
        self.n = 0

    def sb(self, shape, dt, name=None):
        Pool.N += 1
        t = self.stack.enter_context(self.fw.nc.sbuf_tensor(name or ("sb_%d" % Pool.N), list(shape), dt))
        return t, Tk()

    def ps(self, shape, dt=F32, name=None):
        Pool.N += 1
        t = self.stack.enter_context(self.fw.nc.psum_tensor(name or ("ps_%d" % Pool.N), list(shape), dt))
        return t, Tk()

    def close(self):
        self.fw.barrier()
        self.stack.close()
import math

AX = mybir.AxisListType
I32 = mybir.dt.int32
D = 1024
KC = 8
EPS = 1e-6
GW = 64
O_XBC, O_DTF, O_DTB, O_HGI, O_HFF, O_HFB, O_HQ, O_HY, O_Z, O_HGG, O_HYG = 0, 1536, 1552, 1568, 2080, 2592, 3104, 3616, 5152, 6176, 6688
IN_COLS = 7200
HY_MIN_DECAY = math.log(1e-2) / 1.5
HY_MAX_DECAY = math.log(1e-2) / 0.3


def bc(ap, shape):
    return ap.to_broadcast(list(shape))


class Seq:
    pass


class Builder:
    def __init__(self, L, LC, depth, debug=()):
        self.L, self.LC, self.depth = L, LC, depth
        self.debug = set(debug)
        self.nc = bass.Bass("TRN2", target_bir_lowering=False)
        self.fw = FW(self.nc)
        self.inputs = {}
        self.outputs = []

    def din(self, name, shape, dt=F32):
        t = self.nc.dram_tensor(name, list(shape), dt, kind="ExternalInput").ap()
        self.inputs[name] = t
        return t

    def scr(self, name, shape, dt):
        kind = "ExternalOutput" if name in self.debug else "Internal"
        t = self.nc.dram_tensor(name, list(shape), dt, kind=kind).ap()
        if name in self.debug:
            self.outputs.append(name)
        return t, Tk(multi=True)

    def tt(self, e, out, in0, in1, op, R, W):
        en = self.fw.eng[e]
        return self.fw.op(e, lambda: en.tensor_tensor(out=out, in0=in0, in1=in1, op=op), R, W)

    def ts(self, e, out, in0, s1, s2, op0, op1, R, W):
        en = self.fw.eng[e]
        if op1 is None:
            return self.fw.op(e, lambda: en.tensor_scalar(out=out, in0=in0, scalar1=s1, scalar2=None, op0=op0), R, W)
        return self.fw.op(e, lambda: en.tensor_scalar(out=out, in0=in0, scalar1=s1, scalar2=s2, op0=op0, op1=op1), R, W)

    def stt(self, out, in0, scalar, in1, op0, op1, R, W):
        nc = self.nc
        return self.fw.op("dve", lambda: nc.vector.scalar_tensor_tensor(out=out, in0=in0, scalar=scalar, in1=in1, op0=op0, op1=op1), R, W)

    def act(self, out, in_, func, R, W, bias=None, scale=None, accum_out=None):
        nc = self.nc
        kw = {}
        if bias is not None:
            kw["bias"] = bias
        if scale is not None:
            kw["scale"] = scale
        if accum_out is not None:
            kw["accum_out"] = accum_out
            npart = accum_out.shape[0]

            def fn():
                nc.scalar.activation(out=out, in_=in_, func=func, **kw)
                return nc.scalar.copy(out=self.dummy[0:npart, 0:1], in_=accum_out)
            return self.fw.op("act", fn, R, W)
        return self.fw.op("act", lambda: nc.scalar.activation(out=out, in_=in_, func=func, **kw), R, W)

    def mm(self, out, lhsT, rhs, start, stop, R, W):
        nc = self.nc
        return self.fw.op("pe", lambda: nc.tensor.matmul(out, lhsT, rhs, start=start, stop=stop), R, W)

    def tr(self, out, in_, ident, R, W):
        nc = self.nc
        return self.fw.op("pe", lambda: nc.tensor.transpose(out, in_, ident), R, W)

    def cp(self, e, out, in_, R, W):
        if e == "act":
            nc = self.nc
            return self.fw.op("act", lambda: nc.scalar.copy(out=out, in_=in_), R, W)
        en = self.fw.eng[e]
        return self.fw.op(e, lambda: en.tensor_copy(out=out, in_=in_), R, W)

    def ms(self, e, ap, val, W):
        en = self.fw.eng[e]
        return self.fw.op(e, lambda: en.memset(ap, val), (), W)

    def load_w(self, P, dst, dk, wsrc, ncols):
        TP = Pool(self.fw)
        st = [TP.sb([128, 8, 256], F32) for _ in range(2)]
        src = wsrc.rearrange("(k p) n -> p k n", p=128)
        i = 0
        for c0 in range(0, ncols, 256):
            n = min(256, ncols - c0)
            s, sk = st[i % 2]
            self.fw.dma(s[:, :, 0:n], src[:, :, c0:c0 + n], writes=[sk])
            self.cp("pool" if i % 2 else "dve", dst[:, :, c0:c0 + n], s[:, :, 0:n], [sk], [dk])
            i += 1
        TP.close()

    def setup(self):
        L, LC, dp = self.L, self.LC, self.depth
        self.x_in = self.din("x", [L, D])
        self.ctx_in = self.din("ctx", [LC, D])
        self.c2_in = self.din("c2", [128, 8, 2])
        self.ada_w = self.din("ada_w", [dp, D, 3 * D])
        self.ada_b2 = self.din("ada_b2", [dp, 2, 3 * D])
        self.norm_w2 = self.din("norm_w2", [dp, 2, D])
        self.fnw_bc = self.din("fnw_bc", [128, D])
        self.w_in = self.din("w_in", [dp, D, IN_COLS])
        self.w_out = self.din("w_out", [dp, 2 * D, D])
        self.cw_fm = self.din("cw_fm", [dp, 128, 12, 5])
        self.cb_fm = self.din("cb_fm", [dp, 128, 12])
        self.hcw_fm = self.din("hcw_fm", [dp, 128, 12, 3])
        self.hcb_fm = self.din("hcb_fm", [dp, 128, 12])
        self.dtb_fm = self.din("dtb_fm", [dp, 32, 1])
        self.alog_fm = self.din("alog_fm", [dp, 32, 1])
        self.d_bc = self.din("d_bc", [dp, 128, 16])
        self.snw_bc = self.din("snw_bc", [dp, 128, D])
        self.lbl_fm = self.din("lbl_fm", [128, 2, dp, 4])
        self.hnw_bc = self.din("hnw_bc", [dp, 128, 512])
        self.hyb_bc = self.din("hyb_bc", [dp, 128, 512])
        self.fw1 = self.din("fw1", [dp, 33, 64])
        self.fw2 = self.din("fw2", [dp, 64, 64])
        self.fw3 = self.din("fw3", [dp, 64, 1024])
        self.fvec = self.din("fvec", [dp, 64, 3])
        self.c_id32 = self.din("c_id32", [128, 128])
        self.c_idbf = self.din("c_idbf", [128, 128], BF16)
        self.c_mask = self.din("c_mask", [2, 128, 128])
        self.c_sel = self.din("c_sel", [2, 256])
        self.c_rst = self.din("c_rst", [128, 2, 512])
        self.c_delta = self.din("c_delta", [128, 512])
        self.c_z = {}
        self.c_tneg = {}
        for Ls in sorted(set([L, LC])):
            self.c_z[Ls] = self.din("c_z%d" % Ls, [33, 2 * Ls])
            self.c_tneg[Ls] = self.din("c_tneg%d" % Ls, [128, 2 * Ls // 128])
        self.out = self.nc.dram_tensor("out", [L, D], F32, kind="ExternalOutput").ap()
        self.outk = Tk(multi=True)
        self.ink = Tk(multi=True)
        self.GP = Pool(self.fw)
        G = self.GP
        self.id32, self.id32k = G.sb([128, 128], F32)
        self.idbf, self.idbfk = G.sb([128, 128], BF16)
        self.mask, self.maskk = G.sb([128, 2, 128], F32)
        self.rst, self.rstk = G.sb([128, 2, 512], F32)
        self.gsh, self.gshk = G.sb([128, 8, 2, 2], F32)
        self.gate, self.gatek = G.sb([128, 2, D], F32)
        self.zero, self.zerok = G.sb([128, 160], BF16)
        self.dummy, self.dummyk = G.sb([128, 4], F32)
        fw = self.fw
        fw.dma(self.id32[:], self.c_id32, writes=[self.id32k])
        fw.dma(self.idbf[:], self.c_idbf, writes=[self.idbfk])
        fw.dma(self.mask[:], self.c_mask.rearrange("a s t -> s a t"), writes=[self.maskk])
        fw.dma(self.rst[:], self.c_rst, writes=[self.rstk])
        self.ms("dve", self.zero[:], 0.0, [self.zerok])
        self.seqs = []
        for nm, Ls, xin in (("c", LC, self.ctx_in), ("l", L, self.x_in)):
            s = Seq()
            s.name, s.L, s.is_lat = nm, Ls, nm == "l"
            s.T = min(512, Ls)
            s.x, s.xk = xin, self.ink
            s.hT = {"rm": self.scr("hT_" + nm, [D, Ls + 4], BF16)}
            if s.is_lat:
                s.hT["cm"] = self.scr("hTc_" + nm, [D, Ls + 4], BF16)
            else:
                s.hT["cm"] = s.hT["rm"]
            s.xbcT = self.scr("xbcT_" + nm, [1536, Ls], BF16)
            s.dta = self.scr("dta_" + nm, [32, 2, Ls], F32)
            s.ysd = [self.scr("ysd%d_%s" % (d, nm), [Ls, D], F32) for d in range(2)]
            s.od = [self.scr("od%d_%s" % (d, nm), [Ls, 512], F32) for d in range(2)]
            s.y = self.scr("y_" + nm, [Ls, 2 * D], BF16)
            s.wT = self.scr("wT_" + nm, [512, Ls + 256], BF16)
            s.hx0g = self.scr("hx0g_" + nm, [Ls, 512], F32)
            s.hbw = self.scr("hbw_" + nm, [Ls, 512], F32)
            s.x1 = self.scr("x1_" + nm, [Ls, D], F32)
            s.kmat = self.scr("kmat_" + nm, [8, 128, 64, 2 * Ls // 128], BF16)
            s.rn = self.scr("rn_" + nm, [128, 512], F32)
            self.seqs.append(s)
        self.cumsc2 = [[self.scr("cumsc%d_%d" % (d, j), [16, 512], F32) for j in range(2)] for d in range(2)]
        for s in self.seqs:
            for key in (("rm", "cm") if s.is_lat else ("rm",)):
                hd, hk = s.hT[key]
                v = hd.rearrange("(k p) t -> p k t", p=128)
                fw.dma(v[:, :, 0:2], self.zero[:, 0:16].rearrange("p (k t) -> p k t", k=8), reads=[self.zerok], writes=[hk])
                fw.dma(v[:, :, s.L + 2:s.L + 4], self.zero[:, 0:16].rearrange("p (k t) -> p k t", k=8), reads=[self.zerok], writes=[hk])
            wd, wk = s.wT
            for c0 in range(0, 512, 128):
                fw.dma(wd[c0:c0 + 128, 0:127], self.zero[:, 0:127], reads=[self.zerok], writes=[wk])
                fw.dma(wd[c0:c0 + 128, 127 + s.L:s.L + 256], self.zero[:, 0:129], reads=[self.zerok], writes=[wk])

    def p_adaln(self, l):
        fw, nc = self.fw, self.nc
        P = Pool(fw)
        c2, c2k = P.sb([128, 8, 2], F32)
        wst = [P.sb([128, 8, 512], F32) for _ in range(2)]
        mod, modk = P.sb([2, 3 * D], F32)
        adab, adabk = P.sb([2, 3 * D], F32)
        nw2, nw2k = P.sb([2, D], F32)
        grow, growk = P.sb([2, 2, D], F32)
        sel, selk = P.sb([2, 256], F32)
        ps, psk = P.ps([128, 512])
        pst, pstk = P.ps([128, 512])
        fw.dma(c2[:], self.c2_in, writes=[c2k])
        fw.dma(adab[:], self.ada_b2[l], writes=[adabk])
        fw.dma(nw2[:], self.norm_w2[l], writes=[nw2k])
        fw.dma(sel[:], self.c_sel, writes=[selk])
        self.act(c2[:], c2[:], AF.Silu, [c2k], [c2k])
        wv = self.ada_w[l].rearrange("(p k) n -> p k n", k=8)
        for cb in range(6):
            w, wk = wst[cb % 2]
            fw.dma(w[:], wv[:, :, cb * 512:(cb + 1) * 512], writes=[wk])
            for k in range(8):
                self.mm(ps[0:2, :], c2[:, k, :], w[:, k, :], k == 0, k == 7, [c2k, wk], [psk])
            self.tt("dve", mod[:, cb * 512:(cb + 1) * 512], ps[0:2, :], adab[:, cb * 512:(cb + 1) * 512], ALU.add, [psk, adabk], [modk])
        self.stt(grow[:, 0, :], mod[:, D:2 * D], 1.0, nw2[:], ALU.add, ALU.mult, [modk, nw2k], [growk])
        self.cp("dve", grow[:, 1, :], mod[:, 0:D], [modk], [growk])
        for k in range(8):
            for w_ in range(2):
                o = (k * 2 + w_) * 2
                self.tr(pst[:, o:o + 2], grow[0:2, w_, k * 128:(k + 1) * 128], self.id32[0:2, 0:2], [growk, self.id32k], [pstk])
        self.cp("dve", self.gsh[:].rearrange("p k a b -> p (k a b)"), pst[:, 0:32], [pstk], [self.gshk])
        for r in range(2):
            for h in range(2):
                self.mm(ps[:, :], sel[0:2, r * 128:(r + 1) * 128], mod[0:2, 2 * D + h * 512:2 * D + (h + 1) * 512], True, True, [selk, modk], [psk])
                self.cp("dve", self.gate[:, r, h * 512:(h + 1) * 512], ps[:, :], [psk], [self.gatek])
        P.close()

    def x_tile_src(self, seq, xap, i, order):
        if order == "rm" or not seq.is_lat:
            return [(0, 128, xap[i * 128:(i + 1) * 128, :])]
        rows = seq.L // GW
        v = xap.rearrange("(r c) d -> c r d", c=GW)
        if rows >= 128:
            assert rows == 128
            return [(0, 128, v[i])]
        ncol = 128 // rows
        return [(j * rows, rows, v[i * ncol + j]) for j in range(ncol)]

    def p_norm(self, seq, xap, xk, order):
        fw, nc = self.fw, self.nc
        row = 0 if seq.is_lat else 1
        hd, hk = seq.hT[order]
        hv = hd.rearrange("(k p) t -> p k t", p=128)
        P = Pool(fw)
        xt = [P.sb([128, D], F32) for _ in range(2)]
        junk, junkk = P.sb([128, D], BF16)
        ss = [P.sb([128, 2], F32) for _ in range(2)]
        xn = [P.sb([128, D], F32) for _ in range(2)]
        tmp, tmpk = P.sb([128, 8, 128], F32)
        hs = [P.sb([128, 8, 512], BF16) for _ in range(2)]
        ps = [P.ps([128, 1024]) for _ in range(2)]
        nt = seq.L // 128
        T = seq.T
        per = T // 128
        def load(i):
            x_, x_k = xt[i % 2]
            for (p0, n, src) in self.x_tile_src(seq, xap, i, order):
                fw.dma(x_[p0:p0 + n, :], src, reads=[xk], writes=[x_k])
        load(0)
        for i in range(nt):
            x_, x_k = xt[i % 2]
            if i + 1 < nt:
                load(i + 1)
            s_, s_k = ss[i % 2]
            self.act(junk[:], x_[:], AF.Square, [x_k], [junkk, s_k], accum_out=s_[:, 0:1])
            self.ts("dve", s_[:, 1:2], s_[:, 0:1], 1.0 / D, EPS, ALU.mult, ALU.add, [s_k], [s_k])
            self.act(s_[:, 1:2], s_[:, 1:2], AF.Sqrt, [s_k], [s_k])
            fw.op("dve", lambda: nc.vector.reciprocal(out=s_[:, 1:2], in_=s_[:, 1:2]), [s_k], [s_k])
            n_, n_k = xn[i % 2]
            self.act(n_[:], x_[:], AF.Copy, [x_k, s_k], [n_k], scale=s_[:, 1:2])
            p_, p_k = ps[i % 2]
            for k in range(8):
                self.tr(p_[:, k * 128:(k + 1) * 128], n_[:, k * 128:(k + 1) * 128], self.id32[:], [n_k, self.id32k], [p_k])
            h_, h_k = hs[(i // per) % 2]
            j = i % per
            self.tt("dve", tmp[:], p_[:].rearrange("p (k t) -> p k t", k=8), bc(self.gsh[:, :, 0, row:row + 1], [128, 8, 128]), ALU.mult, [p_k, self.gshk], [tmpk])
            self.tt("pool", h_[:, :, j * 128:(j + 1) * 128], tmp[:], bc(self.gsh[:, :, 1, row:row + 1], [128, 8, 128]), ALU.add, [tmpk, self.gshk], [h_k])
            if j == per - 1:
                t0 = (i // per) * T
                fw.dma(hv[:, :, 2 + t0:2 + t0 + T], h_[:, :, 0:T], reads=[h_k], writes=[hk])
        P.close()

    def p_ssd_front(self, seq, l):
        fw, nc = self.fw, self.nc
        P = Pool(fw)
        W, Wk = P.sb([128, 8, 1568], BF16)
        self.load_w(P, W, Wk, self.w_in[l][:, O_XBC:O_XBC + 1568], 1568)
        cw, cwk = P.sb([128, 12, 5], F32)
        cb, cbk = P.sb([128, 12], F32)
        dtb, dtbk = P.sb([32, 1], F32)
        aneg, anegk = P.sb([32, 1], F32)
        fw.dma(cw[:], self.cw_fm[l], writes=[cwk])
        fw.dma(cb[:], self.cb_fm[l], writes=[cbk])
        fw.dma(dtb[:], self.dtb_fm[l], writes=[dtbk])
        fw.dma(aneg[:], self.alog_fm[l], writes=[anegk])
        self.act(aneg[:], aneg[:], AF.Exp, [anegk], [anegk])
        self.ts("dve", aneg[:], aneg[:], -1.0, None, ALU.mult, None, [anegk], [anegk])
        T = seq.T
        hw = [P.sb([128, 8, T + 4], BF16) for _ in range(2)]
        ps = [P.ps([128, 1024]) for _ in range(2)]
        acc = [P.sb([128, T], F32) for _ in range(2)]
        xst = [P.sb([128, 12, T], BF16) for _ in range(2)]
        psd, psdk = P.ps([128, 512])
        dst = [P.sb([32, 2, T], F32) for _ in range(2)]
        hd, hk = seq.hT["rm"]
        hv = hd.rearrange("(k p) t -> p k t", p=128)
        xd, xdk = seq.xbcT
        xdv = xd.rearrange("(b p) t -> p b t", p=128)
        dd, ddk = seq.dta
        nsup = seq.L // T
        def load(si):
            h_, h_k = hw[si % 2]
            fw.dma(h_[:], hv[:, :, si * T:si * T + T + 4], reads=[hk], writes=[h_k])
        load(0)
        for si in range(nsup):
            t0 = si * T
            h_, h_k = hw[si % 2]
            if si + 1 < nsup:
                load(si + 1)
            x_, x_k = xst[si % 2]
            for b in range(12):
                p_, p_k = ps[b % 2]
                for k in range(8):
                    self.mm(p_[:, 0:T], W[:, k, b * 128:(b + 1) * 128], h_[:, k, 0:T], k == 0, k == 7, [Wk, h_k], [p_k])
                for k in range(8):
                    self.mm(p_[:, T:T + 4], W[:, k, b * 128:(b + 1) * 128], h_[:, k, T:T + 4], k == 0, k == 7, [Wk, h_k], [p_k])
                a_, a_k = acc[b % 2]
                self.act(a_[:], p_[:, 0:T], AF.Identity, [p_k, cwk, cbk], [a_k], scale=cw[:, b, 0:1], bias=cb[:, b:b + 1])
                for k in range(1, 5):
                    self.stt(a_[:], p_[:, k:k + T], cw[:, b, k:k + 1], a_[:], ALU.mult, ALU.add, [p_k, cwk, a_k], [a_k])
                self.act(x_[:, b, :], a_[:], AF.Silu, [a_k], [x_k])
            fw.dma(xdv[:, :, t0:t0 + T], x_[:], reads=[x_k], writes=[xdk])
            for k in range(8):
                self.mm(psd[0:32, 0:T], W[:, k, 1536:1568], h_[:, k, 2:2 + T], k == 0, k == 7, [Wk, h_k], [psdk])
            d_, d_k = dst[si % 2]
            self.act(d_[:, 0, :], psd[0:32, 0:T], AF.Exp, [psdk, dtbk], [d_k], bias=dtb[:, 0:1])
            self.act(d_[:, 0, :], d_[:, 0, :], AF.Ln, [d_k], [d_k], bias=1.0)
            self.ts("dve", d_[:, 1, :], d_[:, 0, :], aneg[:, 0:1], None, ALU.mult, None, [d_k, anegk], [d_k])
            fw.dma(dd[:, :, t0:t0 + T], d_[:], reads=[d_k], writes=[ddk])
        P.close()

    def p_ssd_scan(self, l, d):
        fw, nc = self.fw, self.nc
        P = Pool(fw)
        self._pools.append(P)
        TS = 256
        S, Sk = P.sb([128, 16, 64], F32)
        Sbf, Sbfk = P.sb([128, 1024], BF16)
        self.ms("dve", S[:], 0.0, [Sk])
        self.ms("pool", Sbf[:], 0.0, [Sbfk])
        drow, drowk = P.sb([128, 16], F32)
        fw.dma(drow[:], self.d_bc[l], writes=[drowk])
        xs2 = [P.sb([128, 12, TS], BF16) for _ in range(2)]
        da2 = [P.sb([16, 2, TS], F32) for _ in range(2)]
        cumT2 = [P.sb([16, TS], F32) for _ in range(2)]
        R4 = [P.sb([128, 16, 128], F32) for _ in range(4)]
        ctok, ctokk = P.sb([128, 48], F32)
        Dm, Dmk = P.sb([128, 16, 128], F32)
        dec, deck = P.sb([128, 16, 128], BF16)
        sm, smk = P.sb([128, 2, 128], F32)
        MT, MTk = P.sb([128, 16, 128], BF16)
        Btok, Btokk = P.sb([128, 256], BF16)
        xdt, xdtk = P.sb([128, 16, 64], BF16)
        xsD, xsDk = P.sb([128, 16, 64], BF16)
        xdtw, xdtwk = P.sb([128, 16, 64], BF16)
        yi, yik = P.sb([128, 8, 64], F32)
        yo2 = [P.sb([128, 1024], F32) for _ in range(2)]
        etot, etotk = P.sb([128, 16], F32)
        tot, totk = P.sb([16, 8], F32)
        pm, pmk = P.ps([128, 512])
        psx, psxk = P.ps([128, 1024], BF16)
        psy, psyk = P.ps([128, 512])
        psi, psik = P.ps([128, 512])
        pst = pm[:, 0:32]
        pss = pm[:, 32:288]
        yield
        mk = self.mask[:, d, :]
        gc = 0
        plan = []
        for seq in self.seqs:
            T = min(TS, seq.L)
            nsup = seq.L // T
            sups = list(range(nsup) if d == 0 else range(nsup - 1, -1, -1))
            plan.extend((seq, si) for si in sups)
        Rb = {}

        def prologue(pi):
            seq, si = plan[pi]
            T = min(TS, seq.L)
            nch = T // 128
            t0 = si * T
            xd, xdk = seq.xbcT
            xdv = xd.rearrange("(b p) t -> p b t", p=128)
            dd, ddk = seq.dta
            xs, xsk = xs2[pi % 2]
            da, dak = da2[pi % 2]
            cumT, cumTk = cumT2[pi % 2]
            cd, cdk = self.cumsc2[d][pi % 2]
            fw.dma(xs[:, :, 0:T], xdv[:, :, t0:t0 + T], reads=[xdk], writes=[xsk])
            fw.dma(da[:, :, 0:T], dd[d * 16:(d + 1) * 16, :, t0:t0 + T], reads=[ddk], writes=[dak])
            fw.op("dve", lambda: nc.vector.tensor_tensor_scan(out=cumT[:, 0:T], data0=self.rst[0:16, 0, 0:T], data1=da[:, 1, 0:T], initial=0.0, op0=ALU.mult, op1=ALU.add), [self.rstk, dak], [cumTk])
            if d == 1:
                c3 = cumT[:, 0:T].rearrange("p (c t) -> p c t", t=128)
                self.cp("dve", tot[:, 0:nch], cumT[:, 127:T:128], [cumTk], [totk])
                self.tt("dve", c3, bc(tot[:, 0:nch].unsqueeze(2), [16, nch, 128]), c3, ALU.subtract, [cumTk, totk], [cumTk])
                self.tt("dve", cumT[:, 0:T], cumT[:, 0:T], da[:, 1, 0:T], ALU.add, [cumTk, dak], [cumTk])
            fw.dma(cd[:, 0:T], cumT[:, 0:T], reads=[cumTk], writes=[cdk])
            for ci in range(nch):
                R, Rk = R4[(pi % 2) * 2 + ci]
                rsrc = bass.AP(tensor=cd.tensor, offset=ci * 128, ap=[[0, 128], [512, 16], [1, 128]])
                fw.dma(R[:], rsrc, reads=[cdk], writes=[Rk])
        prologue(0)
        for pi in range(len(plan)):
            seq, si = plan[pi]
            if True:
                T = min(TS, seq.L)
                nch = T // 128
                yd, ydk = seq.ysd[d]
                t0 = si * T
                xs, xsk = xs2[pi % 2]
                da, dak = da2[pi % 2]
                cumT, cumTk = cumT2[pi % 2]
                if pi + 1 < len(plan):
                    prologue(pi + 1)
                yield
                chs = range(nch) if d == 0 else range(nch - 1, -1, -1)
                for ci in chs:
                    c0 = ci * 128
                    R, Rk = R4[(pi % 2) * 2 + ci]
                    yo, yok = yo2[gc % 2]
                    gc += 1
                    self.tr(pst[:, 0:16], cumT[:, c0:c0 + 128], self.id32[0:16, 0:16], [cumTk, self.id32k], [pmk])
                    self.tr(pst[:, 16:32], da[:, 0, c0:c0 + 128], self.id32[0:16, 0:16], [dak, self.id32k], [pmk])
                    for b in range(8):
                        self.tr(psx[:, b * 128:(b + 1) * 128], xs[:, b, c0:c0 + 128], self.idbf[:], [xsk, self.idbfk], [psxk])
                    yield
                    self.cp("act", ctok[:, 0:32], pst[:, 0:32], [pmk], [ctokk])
                    self.act(ctok[:, 32:48], ctok[:, 0:16], AF.Exp, [ctokk], [ctokk])
                    for g in range(2):
                        self.mm(pss[:, g * 128:(g + 1) * 128], xs[:, 8 + g, c0:c0 + 128], xs[:, 10 + g, c0:c0 + 128], True, True, [xsk], [pmk])
                    yield
                    self.tt("dve", Dm[:], R[:], bc(ctok[:, 0:16].unsqueeze(2), [128, 16, 128]), ALU.subtract, [Rk, ctokk], [Dmk])
                    self.ts("dve", Dm[:], Dm[:], 0.0, None, ALU.min, None, [Dmk], [Dmk])
                    self.tt("dve", sm[:], pss.rearrange("p (g t) -> p g t", g=2), bc(mk.unsqueeze(1), [128, 2, 128]), ALU.mult, [pmk, self.maskk], [smk])
                    px3 = psx[:].rearrange("p (e q) -> p e q", e=16)
                    self.tt("dve", xdt[:], px3, bc(ctok[:, 16:32].unsqueeze(2), [128, 16, 64]), ALU.mult, [psxk, ctokk], [xdtk])
                    if d == 0:
                        self.tt("dve", xsD[:], px3, bc(drow[:].unsqueeze(2), [128, 16, 64]), ALU.mult, [psxk, drowk], [xsDk])
                    yield
                    self.act(dec[:], Dm[:], AF.Exp, [Dmk], [deck])
                    for g in range(2):
                        self.tr(psx[:, g * 128:(g + 1) * 128], xs[:, 8 + g, c0:c0 + 128], self.idbf[:], [xsk, self.idbfk], [psxk])
                    self.cp("act", Btok[:], psx[:, 0:256], [psxk], [Btokk])
                    yield
                    for g in range(2):
                        self.tt("pool" if g else "dve", MT[:, g * 8:(g + 1) * 8, :], dec[:, g * 8:(g + 1) * 8, :], bc(sm[:, g:g + 1, :], [128, 8, 128]), ALU.mult, [deck, smk], [MTk])
                    col = 127 if d == 0 else 0
                    self.tt("pool", xdtw[:], xdt[:], bc(dec[:, :, col:col + 1], [128, 16, 64]), ALU.mult, [xdtk, deck], [xdtwk])
                    self.act(etot[:], R[:, :, col], AF.Exp, [Rk], [etotk])
                    yield
                    for g in range(2):
                        if d == 0:
                            self.mm(psy[:, :], self.idbf[:], xsD[:, g * 8:(g + 1) * 8, :].rearrange("p e q -> p (e q)"), True, False, [self.idbfk, xsDk], [psyk])
                        for e in range(8):
                            ge = g * 8 + e
                            self.mm(psy[:, e * 64:(e + 1) * 64], MT[:, ge, :], xdt[:, ge, :], (d == 1) and e == 0, e == 7, [MTk, xdtk], [psyk])
                        self.mm(psi[:, :], xs[:, 10 + g, c0:c0 + 128], Sbf[:, g * 512:(g + 1) * 512], True, True, [xsk, Sbfk], [psik])
                        yield
                        self.tt("dve", yi[:], psi[:].rearrange("p (e q) -> p e q", e=8), bc(ctok[:, 32 + g * 8:40 + g * 8].unsqueeze(2), [128, 8, 64]), ALU.mult, [psik, ctokk], [yik])
                        self.tt("dve", yo[:, g * 512:(g + 1) * 512], psy[:], yi[:].rearrange("p e q -> p (e q)"), ALU.add, [psyk, yik], [yok])
                        yield
                    fw.dma(yd[t0 + c0:t0 + c0 + 128, :], yo[:], reads=[yok], writes=[ydk])
                    for g in range(2):
                        self.mm(psi[:, :], Btok[:, g * 128:(g + 1) * 128], xdtw[:, g * 8:(g + 1) * 8, :].rearrange("p e q -> p (e q)"), True, True, [Btokk, xdtwk], [psik])
                        self.tt("pool", S[:, g * 8:(g + 1) * 8, :], S[:, g * 8:(g + 1) * 8, :], bc(etot[:, g * 8:(g + 1) * 8].unsqueeze(2), [128, 8, 64]), ALU.mult, [Sk, etotk], [Sk])
                        yield
                        self.tt("dve", S[:, g * 8:(g + 1) * 8, :], S[:, g * 8:(g + 1) * 8, :], psi[:].rearrange("p (e q) -> p e q", e=8), ALU.add, [Sk, psik], [Sk])
                    self.cp("act", Sbf[:], S[:].rearrange("p e q -> p (e q)"), [Sk], [Sbfk])
                    yield

    def p_ssd_comb(self, seq, l):
        fw, nc = self.fw, self.nc
        P = Pool(fw)
        W, Wk = P.sb([128, 8, 1024], BF16)
        self.load_w(P, W, Wk, self.w_in[l][:, O_Z:O_Z + 1024], 1024)
        nw, nwk = P.sb([128, 1024], F32)
        fw.dma(nw[:], self.snw_bc[l], writes=[nwk])
        T = seq.T
        hw = [P.sb([128, 8, T], BF16) for _ in range(2)]
        yf = [P.sb([128, 1024], F32) for _ in range(2)]
        yb = [P.sb([128, 1024], F32) for _ in range(2)]
        sz, szk = P.sb([128, 1024], F32)
        junk, junkk = P.sb([128, 512], BF16)
        ss, ssk = P.sb([128, 4], F32)
        yo = [P.sb([128, 1024], BF16) for _ in range(2)]
        ps = [P.ps([128, 1024]) for _ in range(2)]
        hd, hk = seq.hT["rm"]
        hv = hd.rearrange("(k p) t -> p k t", p=128)
        yd, ydk = seq.y
        per = T // 128
        ntile = seq.L // 128

        def load(i):
            h_, h_k = hw[(i // per) % 2]
            if i % per == 0:
                fw.dma(h_[:], hv[:, :, 2 + i * 128:2 + i * 128 + T], reads=[hk], writes=[h_k])
            f_, f_k = yf[i % 2]
            b_, b_k = yb[i % 2]
            fw.dma(f_[:], seq.ysd[0][0][i * 128:(i + 1) * 128, :], reads=[seq.ysd[0][1]], writes=[f_k])
            fw.dma(b_[:], seq.ysd[1][0][i * 128:(i + 1) * 128, :], reads=[seq.ysd[1][1]], writes=[b_k])
        load(0)
        for i in range(ntile):
            h_, h_k = hw[(i // per) % 2]
            j = i % per
            f_, f_k = yf[i % 2]
            b_, b_k = yb[i % 2]
            if i + 1 < ntile:
                load(i + 1)
            p_, p_k = ps[i % 2]
            for h in range(2):
                for k in range(8):
                    self.mm(p_[:, h * 512:(h + 1) * 512], h_[:, k, j * 128:(j + 1) * 128], W[:, k, h * 512:(h + 1) * 512], k == 0, k == 7, [h_k, Wk], [p_k])
            self.act(sz[:], p_[:], AF.Silu, [p_k], [szk])
            self.tt("pool", f_[:], f_[:], b_[:], ALU.add, [f_k, b_k], [f_k])
            self.tt("dve", f_[:], f_[:], sz[:], ALU.mult, [f_k, szk], [f_k])
            for g in range(2):
                self.act(junk[:], f_[:, g * 512:(g + 1) * 512], AF.Square, [f_k], [junkk, ssk], accum_out=ss[:, g:g + 1])
            self.ts("dve", ss[:, 2:4], ss[:, 0:2], 1.0 / 512, EPS, ALU.mult, ALU.add, [ssk], [ssk])
            self.act(ss[:, 2:4], ss[:, 2:4], AF.Sqrt, [ssk], [ssk])
            fw.op("dve", lambda: nc.vector.reciprocal(out=ss[:, 2:4], in_=ss[:, 2:4]), [ssk], [ssk])
            o_, o_k = yo[i % 2]
            for g in range(2):
                self.stt(o_[:, g * 512:(g + 1) * 512], f_[:, g * 512:(g + 1) * 512], ss[:, 2 + g:3 + g], nw[:, g * 512:(g + 1) * 512], ALU.mult, ALU.mult, [f_k, ssk, nwk], [o_k])
            fw.dma(yd[i * 128:(i + 1) * 128, 0:1024], o_[:], reads=[o_k], writes=[ydk])
            yield
        P.close()

    def p_out(self, seq, l, xap, xk, dst, dstk, final):
        fw, nc = self.fw, self.nc
        row = 0 if seq.is_lat else 1
        P = Pool(fw)
        W, Wk = P.sb([128, 16, 1024], BF16)
        st = [P.sb([128, 16, 128], F32) for _ in range(2)]
        wsrc = self.w_out[l].rearrange("(k p) n -> p k n", p=128)
        for i, c0 in enumerate(range(0, 1024, 128)):
            s, sk = st[i % 2]
            fw.dma(s[:], wsrc[:, :, c0:c0 + 128], writes=[sk])
            self.cp("pool" if i % 2 else "dve", W[:, :, c0:c0 + 128], s[:], [sk], [Wk])
        fnw, fnwk = P.sb([128, 1024], F32)
        fw.dma(fnw[:], self.fnw_bc, writes=[fnwk])
        yt = [P.sb([128, 2048], BF16) for _ in range(2)]
        yT, yTk = P.sb([128, 16, 128], BF16)
        xt = [P.sb([128, 1024], F32) for _ in range(2)]
        xo = [P.sb([128, 1024], F32) for _ in range(2)]
        junk, junkk = P.sb([128, 1024], BF16)
        ss, ssk = P.sb([128, 2], F32)
        pT, pTk = P.ps([128, 2048], BF16)
        ps = [P.ps([128, 1024]) for _ in range(2)]
        yd, ydk = seq.y
        ntile = seq.L // 128

        def load(i):
            y_, y_k = yt[i % 2]
            x_, x_k = xt[i % 2]
            fw.dma(y_[:], yd[i * 128:(i + 1) * 128, :], reads=[ydk], writes=[y_k])
            fw.dma(x_[:], xap[i * 128:(i + 1) * 128, :], reads=[xk], writes=[x_k])
        load(0)
        for i in range(ntile):
            y_, y_k = yt[i % 2]
            x_, x_k = xt[i % 2]
            if i + 1 < ntile:
                load(i + 1)
            for k in range(16):
                self.tr(pT[:, k * 128:(k + 1) * 128], y_[:, k * 128:(k + 1) * 128], self.idbf[:], [y_k, self.idbfk], [pTk])
            self.cp("act", yT[:].rearrange("p k t -> p (k t)"), pT[:], [pTk], [yTk])
            p_, p_k = ps[i % 2]
            for h in range(2):
                for k in range(16):
                    self.mm(p_[:, h * 512:(h + 1) * 512], yT[:, k, :], W[:, k, h * 512:(h + 1) * 512], k == 0, k == 15, [yTk, Wk], [p_k])
            o_, o_k = xo[i % 2]
            self.tt("dve", o_[:], p_[:], self.gate[:, row, :], ALU.mult, [p_k, self.gatek], [o_k])
            self.tt("pool", o_[:], o_[:], x_[:], ALU.add, [o_k, x_k], [o_k])
            if final:
                self.act(junk[:], o_[:], AF.Square, [o_k], [junkk, ssk], accum_out=ss[:, 0:1])
                self.ts("dve", ss[:, 1:2], ss[:, 0:1], 1.0 / D, EPS, ALU.mult, ALU.add, [ssk], [ssk])
                self.act(ss[:, 1:2], ss[:, 1:2], AF.Sqrt, [ssk], [ssk])
                fw.op("dve", lambda: nc.vector.reciprocal(out=ss[:, 1:2], in_=ss[:, 1:2]), [ssk], [ssk])
                self.stt(o_[:], o_[:], ss[:, 1:2], fnw[:], ALU.mult, ALU.mult, [o_k, ssk, fnwk], [o_k])
            fw.dma(dst[i * 128:(i + 1) * 128, :], o_[:], reads=[o_k], writes=[dstk])
        P.close()

    def p_gla_scan(self, l, d):
        fw, nc = self.fw, self.nc
        P = Pool(fw)
        self._pools.append(P)
        Wf, Wfk = P.sb([128, 8, 512], BF16)
        (Wq, Wqk), (Wv, Wvk) = self._gla_shared
        self.load_w(P, Wf, Wfk, self.w_in[l][:, (O_HFF if d == 0 else O_HFB):(O_HFF if d == 0 else O_HFB) + 512], 512)
        lbl, lblk = P.sb([128, 2, self.depth, 4], F32)
        lb, lbk = P.sb([128, 8], F32)
        fw.dma(lbl[:], self.lbl_fm, writes=[lblk])
        if l == 0:
            self.ms("dve", lb[:, 0:4], 0.0, [lbk])
        else:
            self.tt("dve", lb[:, 0:4], lbl[:, d, 1, :], lbl[:, d, 0, :], ALU.subtract, [lblk], [lbk])
            self.act(lb[:, 0:4], lb[:, 0:4], AF.Sigmoid, [lbk], [lbk])
        self.ts("dve", lb[:, 4:8], lb[:, 0:4], -1.0, 1.0, ALU.mult, ALU.add, [lbk], [lbk])
        S, Sk = P.sb([128, 4, 128], F32)
        Sbf, Sbfk = P.sb([128, 4, 128], BF16)
        self.ms("dve", S[:], 0.0, [Sk])
        self.ms("pool", Sbf[:], 0.0, [Sbfk])
        hw2 = [P.sb([128, 8, 512], BF16) for _ in range(1)]
        f_, fk = P.sb([128, 512], F32)
        lf, lfk = P.sb([128, 512], F32)
        kk, kkk = P.sb([128, 512], F32)
        cum, cumk = P.sb([128, 512], F32)
        E, Ek = P.sb([128, 512], F32)
        ex, exk = P.sb([128, 4, 512], F32)
        ref, refk = P.sb([128, 16], F32)
        etot, etotk = P.sb([128, 4, 8], F32)
        qt, qtk = P.sb([128, 4, 512], BF16)
        qp, qpk = P.sb([128, 4, 512], BF16)
        kt, ktk = P.sb([128, 4, 512], BF16)
        kh, khk = P.sb([128, 4, 512], BF16)
        vtok, vtokk = P.sb([64, 4, 512], BF16)
        khtok, khtokk = P.sb([64, 4, 512], BF16)
        am, amk = P.sb([64, 4, 256], BF16)
        amf, amfk = P.sb([64, 256], F32)
        ost2 = [P.sb([64, 512], F32) for _ in range(2)]
        psf, psfk = P.ps([128, 512])
        psq, psqk = P.ps([128, 512])
        psv, psvk = psf, psfk
        pskh, pskhk = P.ps([128, 1024], BF16)
        psa, psak = psq, psqk
        pso, psok = P.ps([128, 512])
        yield
        psS, psSk = psq, psqk
        mk = self.mask[0:64, d, 0:64]
        gi = 0
        for seq in self.seqs:
            T = seq.T
            nch = T // 64
            nsup = seq.L // T
            hd, hk = seq.hT["cm"]
            hv = hd.rearrange("(k p) t -> p k t", p=128)
            od, odk = seq.od[d]
            sups = range(nsup) if d == 0 else range(nsup - 1, -1, -1)
            for si in sups:
                t0 = si * T
                hw, hwk = hw2[0]
                gi += 1
                fw.dma(hw[:, :, 0:T], hv[:, :, 2 + t0:2 + t0 + T], reads=[hk], writes=[hwk])
                for h in range(4):
                    for k in range(8):
                        self.mm(psf[:, 0:T], Wf[:, k, h * 128:(h + 1) * 128], hw[:, k, 0:T], k == 0, k == 7, [Wfk, hwk], [psfk])
                    for k in range(8):
                        self.mm(psq[:, 0:T], Wq[:, k, h * 128:(h + 1) * 128], hw[:, k, 0:T], k == 0, k == 7, [Wqk, hwk], [psqk])
                    self.act(f_[:, 0:T], psf[:, 0:T], AF.Sigmoid, [psfk], [fk])
                    self.ts("dve", f_[:, 0:T], f_[:, 0:T], lb[:, 4 + h:5 + h], lb[:, h:h + 1], ALU.mult, ALU.add, [fk, lbk], [fk])
                    self.act(lf[:, 0:T], f_[:, 0:T], AF.Ln, [fk], [lfk])
                    self.ts("pool", kk[:, 0:T], f_[:, 0:T], -1.0, 1.0, ALU.mult, ALU.add, [fk], [kkk])
                    fw.op("dve", lambda: nc.vector.tensor_tensor_scan(out=cum[:, 0:T], data0=self.rst[:, 1, 0:T], data1=lf[:, 0:T], initial=0.0, op0=ALU.mult, op1=ALU.add), [self.rstk, lfk], [cumk])
                    c3 = cum[:, 0:T].rearrange("p (c t) -> p c t", t=64)
                    if d == 1:
                        self.cp("dve", ref[:, 8:8 + nch], cum[:, 63:T:64], [cumk], [refk])
                        self.tt("dve", c3, bc(ref[:, 8:8 + nch].unsqueeze(2), [128, nch, 64]), c3, ALU.subtract, [refk, cumk], [cumk])
                        self.tt("dve", cum[:, 0:T], cum[:, 0:T], lf[:, 0:T], ALU.add, [cumk, lfk], [cumk])
                    tcol = 63 if d == 0 else 0
                    self.cp("dve", ref[:, 0:nch], cum[:, 32:T:64], [cumk], [refk])
                    self.cp("dve", ref[:, 8:8 + nch], cum[:, tcol:T:64], [cumk], [refk])
                    E3 = E[:, 0:T].rearrange("p (c t) -> p c t", t=64)
                    self.tt("dve", E3, c3, bc(ref[:, 0:nch].unsqueeze(2), [128, nch, 64]), ALU.subtract, [cumk, refk], [Ek])
                    self.act(ex[:, 0, 0:T], E[:, 0:T], AF.Exp, [Ek], [exk])
                    self.act(ex[:, 1, 0:T], E[:, 0:T], AF.Exp, [Ek], [exk], scale=-1.0)
                    self.act(ex[:, 2, 0:T], cum[:, 0:T], AF.Exp, [cumk], [exk])
                    self.tt("dve", E3, bc(ref[:, 8:8 + nch].unsqueeze(2), [128, nch, 64]), c3, ALU.subtract, [cumk, refk], [Ek])
                    self.act(ex[:, 3, 0:T], E[:, 0:T], AF.Exp, [Ek], [exk])
                    self.act(etot[:, h, 0:nch], ref[:, 8:8 + nch], AF.Exp, [refk], [etotk])
                    self.tt("dve", qt[:, h, 0:T], psq[:, 0:T], ex[:, 0, 0:T], ALU.mult, [psqk, exk], [qtk])
                    self.tt("dve", qp[:, h, 0:T], psq[:, 0:T], ex[:, 2, 0:T], ALU.mult, [psqk, exk], [qpk])
                    self.tt("pool", kt[:, h, 0:T], kk[:, 0:T], ex[:, 1, 0:T], ALU.mult, [kkk, exk], [ktk])
                    self.tt("pool", kh[:, h, 0:T], kk[:, 0:T], ex[:, 3, 0:T], ALU.mult, [kkk, exk], [khk])
                    yield
                chs = list(range(nch) if d == 0 else range(nch - 1, -1, -1))
                for hb in range(0, nch, 4):
                    grp = chs[hb:hb + 4]
                    for gi_, ci in enumerate(grp):
                        c0 = ci * 64
                        for k in range(8):
                            self.mm(psv[0:64, :], hw[:, k, c0:c0 + 64], Wv[:, k, :], k == 0, k == 7, [hwk, Wvk], [psvk])
                        self.cp("act", vtok[:, gi_, :], psv[0:64, :], [psvk], [vtokk])
                        for h in range(4):
                            self.tr(pskh[0:64, h * 128:(h + 1) * 128], kh[:, h, c0:c0 + 64], self.idbf[:], [khk, self.idbfk], [pskhk])
                        self.cp("act", khtok[:, gi_, :], pskh[0:64, 0:512], [pskhk], [khtokk])
                        yield
                        for h in range(4):
                            self.mm(psa[0:64, h * 64:(h + 1) * 64], kt[:, h, c0:c0 + 64], qt[:, h, c0:c0 + 64], True, True, [ktk, qtk], [psak])
                        self.ts("dve", amf[:], psa[0:64, 0:256], 1.0e30, -1.0e30, ALU.min, ALU.max, [psak], [amfk])
                        self.tt("dve", am[:, gi_, :].rearrange("p (h t) -> p h t", h=4), amf[:].rearrange("p (h t) -> p h t", h=4), bc(mk.unsqueeze(1), [64, 4, 64]), ALU.mult, [amfk, self.maskk], [amk])
                        yield
                    for gi_, ci in enumerate(grp):
                        c0 = ci * 64
                        for h in range(4):
                            self.mm(pso[0:64, h * 128:(h + 1) * 128], am[:, gi_, h * 64:(h + 1) * 64], vtok[:, gi_, h * 128:(h + 1) * 128], True, False, [amk, vtokk], [psok])
                            self.mm(pso[0:64, h * 128:(h + 1) * 128], qp[:, h, c0:c0 + 64], Sbf[:, h, :], False, True, [qpk, Sbfk], [psok])
                        for h in range(4):
                            self.mm(psS[:, h * 128:(h + 1) * 128], khtok[:, gi_, h * 128:(h + 1) * 128], vtok[:, gi_, h * 128:(h + 1) * 128], True, True, [khtokk, vtokk], [psSk])
                        yield
                        ost, ostk = ost2[ci % 2]
                        self.cp("act", ost[:], pso[0:64, :], [psok], [ostk])
                        fw.dma(od[t0 + c0:t0 + c0 + 64, :], ost[:], reads=[ostk], writes=[odk])
                        for h in range(4):
                            self.stt(S[:, h, :], S[:, h, :], etot[:, h, ci:ci + 1], psS[:, h * 128:(h + 1) * 128], ALU.mult, ALU.add, [Sk, etotk, psSk], [Sk])
                        self.cp("act", Sbf[:], S[:], [Sk], [Sbfk])
                        yield

    def p_gla_comb(self, seq, l):
        fw, nc = self.fw, self.nc
        P = Pool(fw)
        W, Wk = P.sb([128, 8, 512], BF16)
        self.load_w(P, W, Wk, self.w_in[l][:, O_HGG:O_HGG + 512], 512)
        nw, nwk = P.sb([128, 512], F32)
        fw.dma(nw[:], self.hnw_bc[l], writes=[nwk])
        T = seq.T
        hw = [P.sb([128, 8, T], BF16) for _ in range(2)]
        of = [P.sb([128, 512], F32) for _ in range(2)]
        ob = [P.sb([128, 512], F32) for _ in range(2)]
        sg, sgk = P.sb([128, 512], F32)
        junk, junkk = P.sb([128, 128], BF16)
        ss, ssk = P.sb([128, 8], F32)
        yo = [P.sb([128, 512], BF16) for _ in range(2)]
        ps = [P.ps([128, 512]) for _ in range(2)]
        hd, hk = seq.hT["cm"]
        hv = hd.rearrange("(k p) t -> p k t", p=128)
        yd, ydk = seq.y
        per = T // 128
        ntile = seq.L // 128

        def load(i):
            h_, h_k = hw[(i // per) % 2]
            if i % per == 0:
                fw.dma(h_[:], hv[:, :, 2 + i * 128:2 + i * 128 + T], reads=[hk], writes=[h_k])
            f_, f_k = of[i % 2]
            b_, b_k = ob[i % 2]
            fw.dma(f_[:], seq.od[0][0][i * 128:(i + 1) * 128, :], reads=[seq.od[0][1]], writes=[f_k])
            fw.dma(b_[:], seq.od[1][0][i * 128:(i + 1) * 128, :], reads=[seq.od[1][1]], writes=[b_k])
        load(0)
        for i in range(ntile):
            h_, h_k = hw[(i // per) % 2]
            j = i % per
            f_, f_k = of[i % 2]
            b_, b_k = ob[i % 2]
            if i + 1 < ntile:
                load(i + 1)
            p_, p_k = ps[i % 2]
            for k in range(8):
                self.mm(p_[:, :], h_[:, k, j * 128:(j + 1) * 128], W[:, k, :], k == 0, k == 7, [h_k, Wk], [p_k])
            self.act(sg[:], p_[:], AF.Silu, [p_k], [sgk])
            self.tt("pool", f_[:], f_[:], b_[:], ALU.add, [f_k, b_k], [f_k])
            for h in range(4):
                self.act(junk[:], f_[:, h * 128:(h + 1) * 128], AF.Square, [f_k], [junkk, ssk], accum_out=ss[:, h:h + 1])
            self.ts("dve", ss[:, 4:8], ss[:, 0:4], 1.0 / 128, EPS, ALU.mult, ALU.add, [ssk], [ssk])
            self.act(ss[:, 4:8], ss[:, 4:8], AF.Sqrt, [ssk], [ssk])
            fw.op("dve", lambda: nc.vector.reciprocal(out=ss[:, 4:8], in_=ss[:, 4:8]), [ssk], [ssk])
            self.tt("pool", sg[:], sg[:], nw[:], ALU.mult, [sgk, nwk], [sgk])
            o_, o_k = yo[i % 2]
            for h in range(4):
                self.stt(o_[:, h * 128:(h + 1) * 128], f_[:, h * 128:(h + 1) * 128], ss[:, 4 + h:5 + h], sg[:, h * 128:(h + 1) * 128], ALU.mult, ALU.mult, [f_k, ssk, sgk], [o_k])
            for (p0, n, dstap) in self.x_tile_src(seq, yd, i, "cm"):
                fw.dma(dstap[:, 1024:1536], o_[p0:p0 + n, :], reads=[o_k], writes=[ydk])
            yield
        P.close()

    def p_hy_filter(self, seq, l):
        fw, nc = self.fw, self.nc
        Ls = seq.L
        nb2 = 2 * Ls // 128
        NP = 2 * Ls
        P = Pool(fw)
        w1, w1k = P.sb([33, 64], F32)
        w2, w2k = P.sb([64, 64], F32)
        w3, w3k = P.sb([64, 1024], F32)
        fv, fvk = P.sb([64, 4], F32)
        fb, fbk = P.sb([64, 2], F32)
        dl, dlk = P.sb([128, 512], F32)
        tn, tnk = P.sb([128, nb2], F32)
        ones, onesk = P.sb([128, 128], F32)
        fw.dma(w1[:], self.fw1[l], writes=[w1k])
        fw.dma(w2[:], self.fw2[l], writes=[w2k])
        fw.dma(w3[:], self.fw3[l], writes=[w3k])
        fw.dma(fv[:, 0:3], self.fvec[l], writes=[fvk])
        fw.dma(dl[:], self.c_delta, writes=[dlk])
        fw.dma(tn[:], self.c_tneg[Ls], writes=[tnk])
        self.ms("dve", ones[:], 1.0, [onesk])
        self.ts("dve", fb[:], fv[:, 0:2], fv[:, 2:3], None, ALU.mult, None, [fvk], [fbk])
        CH = 512
        zt = [P.sb([33, CH], F32) for _ in range(2)]
        h1, h1k = P.sb([64, CH], F32)
        msk, mskk = P.sb([64, CH], F32)
        h2, h2k = P.sb([64, CH], F32)
        ksb, ksbk = P.sb([128, 256], F32)
        dcy, dcyk = P.sb([128, 256], F32)
        kab, kabk = P.sb([128, 256], F32)
        Kf, Kfk = P.sb([128, 256, nb2], BF16)
        nacc, nacck = P.sb([128, 512], F32)
        self.ms("dve", nacc[:], 0.0, [nacck])
        ps1, ps1k = P.ps([128, 512])
        ps2, ps2k = P.ps([128, 512])
        psk, pskk = P.ps([128, 512])
        kd, kdk = seq.kmat
        zsrc = self.c_z[Ls]
        TWO_PI = 2.0 * math.pi
        for half in range(2):
            for ci in range(NP // CH):
                z_, z_k = zt[ci % 2]
                fw.dma(z_[:], zsrc[:, ci * CH:(ci + 1) * CH], writes=[z_k])
                self.mm(ps1[0:64, :], w1[:], z_[:], True, True, [w1k, z_k], [ps1k])
                for (hh, hhk, pp, ppk, col) in ((h1, h1k, ps1, ps1k, 0), (h2, h2k, ps2, ps2k, 1)):
                    if col == 1:
                        self.mm(ps2[0:64, :], w2[:], h1[:], True, True, [w2k, h1k], [ps2k])
                    self.act(hh[:], pp[0:64, :], AF.Identity, [ppk, fvk, fbk], [hhk], scale=fv[:, 2:3], bias=fb[:, col:col + 1])
                    self.ts("dve", msk[:], hh[:], math.pi, -TWO_PI, ALU.is_gt, ALU.mult, [hhk], [mskk])
                    self.tt("dve", hh[:], hh[:], msk[:], ALU.add, [hhk, mskk], [hhk])
                    self.ts("dve", msk[:], hh[:], -math.pi, TWO_PI, ALU.is_lt, ALU.mult, [hhk], [mskk])
                    self.tt("dve", hh[:], hh[:], msk[:], ALU.add, [hhk, mskk], [hhk])
                    self.act(hh[:], hh[:], AF.Sin, [hhk], [hhk])
                for j in range(CH // 128):
                    dd = ci * (CH // 128) + j
                    side = 1 if dd < nb2 // 2 else 0
                    c0 = side * 512 + half * 256
                    self.mm(psk[:, 0:256], h2[:, j * 128:(j + 1) * 128], w3[:, c0:c0 + 256], True, True, [h2k, w3k], [pskk])
                    self.act(dcy[:], dl[:, half * 256:(half + 1) * 256], AF.Exp, [dlk, tnk], [dcyk], scale=tn[:, dd:dd + 1])
                    self.tt("dve", ksb[:], psk[:, 0:256], dcy[:], ALU.mult, [pskk, dcyk], [ksbk])
                    self.cp("act", Kf[:, :, dd], ksb[:], [ksbk], [Kfk])
                    self.act(kab[:], ksb[:], AF.Abs, [ksbk], [kabk])
                    self.tt("pool", nacc[:, half * 256:(half + 1) * 256], nacc[:, half * 256:(half + 1) * 256], kab[:], ALU.add, [nacck, kabk], [nacck])
            for s4 in range(4):
                sg = half * 4 + s4
                fw.dma(kd[sg], Kf[:, s4 * 64:(s4 + 1) * 64, :], reads=[Kfk], writes=[kdk])
        self.mm(psk[:, :], ones[:], nacc[:], True, True, [onesk, nacck], [pskk])
        self.ts("dve", nacc[:], psk[:, :], EPS, None, ALU.add, None, [pskk], [nacck])
        fw.op("dve", lambda: nc.vector.reciprocal(out=nacc[:], in_=nacc[:]), [nacck], [nacck])
        fw.dma(seq.rn[0], nacc[:], reads=[nacck], writes=[seq.rn[1]])
        P.close()

    def p_hy_front(self, seq, l):
        fw, nc = self.fw, self.nc
        P = Pool(fw)
        W, Wk = P.sb([128, 8, 1536], BF16)
        Wg, Wgk = P.sb([128, 8, 512], BF16)
        self.load_w(P, W, Wk, self.w_in[l][:, O_HY:O_HY + 1536], 1536)
        self.load_w(P, Wg, Wgk, self.w_in[l][:, O_HYG:O_HYG + 512], 512)
        cw, cwk = P.sb([128, 12, 3], F32)
        cb, cbk = P.sb([128, 12], F32)
        hb, hbk = P.sb([128, 512], F32)
        fw.dma(cw[:], self.hcw_fm[l], writes=[cwk])
        fw.dma(cb[:], self.hcb_fm[l], writes=[cbk])
        fw.dma(hb[:], self.hyb_bc[l], writes=[hbk])
        T = seq.T
        nt = T // 128
        hw = [P.sb([128, 8, T + 2], BF16) for _ in range(2)]
        ps = [P.ps([128, 1024]) for _ in range(2)]
        cv, cvk = P.sb([128, 12, T], F32)
        wf, wfk = P.sb([128, 4, T], F32)
        wb = [P.sb([128, 4, T], BF16) for _ in range(2)]
        psg, psgk = P.ps([128, 512])
        psT, psTk = P.ps([128, 1024])
        sg, sgk = P.sb([128, 512], F32)
        o1 = [P.sb([128, 512], F32) for _ in range(2)]
        o2 = [P.sb([128, 512], F32) for _ in range(2)]
        hd, hk = seq.hT["rm"]
        hv = hd.rearrange("(k p) t -> p k t", p=128)
        wd, wdk = seq.wT
        wdv = wd.rearrange("(b p) t -> p b t", p=128)
        def load(si):
            h_, h_k = hw[si % 2]
            fw.dma(h_[:], hv[:, :, 1 + si * T:1 + si * T + T + 2], reads=[hk], writes=[h_k])
        load(0)
        for si in range(seq.L // T):
            t0 = si * T
            h_, h_k = hw[si % 2]
            if si + 1 < seq.L // T:
                load(si + 1)
            for b in range(12):
                p_, p_k = ps[b % 2]
                for k in range(8):
                    self.mm(p_[:, 0:T], W[:, k, b * 128:(b + 1) * 128], h_[:, k, 0:T], k == 0, k == 7, [Wk, h_k], [p_k])
                for k in range(8):
                    self.mm(p_[:, T:T + 2], W[:, k, b * 128:(b + 1) * 128], h_[:, k, T:T + 2], k == 0, k == 7, [Wk, h_k], [p_k])
                self.act(cv[:, b, :], p_[:, 0:T], AF.Identity, [p_k, cwk, cbk], [cvk], scale=cw[:, b, 0:1], bias=cb[:, b:b + 1])
                for k in range(1, 3):
                    self.stt(cv[:, b, :], p_[:, k:k + T], cw[:, b, k:k + 1], cv[:, b, :], ALU.mult, ALU.add, [p_k, cwk, cvk], [cvk])
            self.tt("dve", wf[:], cv[:, 8:12, :], cv[:, 0:4, :], ALU.mult, [cvk], [wfk])
            w_, w_k = wb[si % 2]
            self.cp("pool", w_[:], wf[:], [wfk], [w_k])
            fw.dma(wdv[:, :, 127 + t0:127 + t0 + T], w_[:], reads=[w_k], writes=[wdk])
            for j in range(nt):
                for b in range(4):
                    self.tr(psT[:, b * 128:(b + 1) * 128], cv[:, 4 + b, j * 128:(j + 1) * 128], self.id32[:], [cvk, self.id32k], [psTk])
                    self.tr(psT[:, 512 + b * 128:512 + (b + 1) * 128], wf[:, b, j * 128:(j + 1) * 128], self.id32[:], [wfk, self.id32k], [psTk])
                for k in range(8):
                    self.mm(psg[:, :], h_[:, k, 1 + j * 128:1 + (j + 1) * 128], Wg[:, k, :], k == 0, k == 7, [h_k, Wgk], [psgk])
                self.act(sg[:], psg[:], AF.Silu, [psgk], [sgk])
                a_, a_k = o1[j % 2]
                b_, b_k = o2[j % 2]
                self.tt("dve", a_[:], psT[:, 0:512], sg[:], ALU.mult, [psTk, sgk], [a_k])
                self.tt("dve", b_[:], psT[:, 512:1024], hb[:], ALU.mult, [psTk, hbk], [b_k])
                self.tt("pool", b_[:], b_[:], a_[:], ALU.mult, [b_k, a_k], [b_k])
                r0 = t0 + j * 128
                fw.dma(seq.hx0g[0][r0:r0 + 128, :], a_[:], reads=[a_k], writes=[seq.hx0g[1]])
                fw.dma(seq.hbw[0][r0:r0 + 128, :], b_[:], reads=[b_k], writes=[seq.hbw[1]])
        P.close()

    def p_hy_conv(self, seqs, l):
        fw, nc = self.fw, self.nc
        Lm = max(s.L for s in seqs)
        nbm = Lm // 128
        P = Pool(fw)
        self.hyP = P
        km, kmk = P.sb([128, 64, 2 * nbm], BF16)
        rn, rnk = P.sb([128, 512], F32)
        wsh = [P.sb([128, Lm + 128], BF16) for _ in range(3)]
        yg, ygk = P.sb([128, nbm, 64], BF16)
        IBm = min(nbm, 8)
        hx = [P.sb([128, IBm, 64], F32) for _ in range(2)]
        hbw = [P.sb([128, IBm, 64], F32) for _ in range(2)]
        yo = [P.sb([128, IBm, 64], BF16) for _ in range(2)]
        pys = [P.ps([128, 512]) for _ in range(3)]
        yield
        gc = 0
        for seq in seqs:
            Ls = seq.L
            nb = Ls // 128
            IB = min(nb, 8)
            fw.dma(rn[:], seq.rn[0], reads=[seq.rn[1]], writes=[rnk])
            wd, wdk = seq.wT
            yd, ydk = seq.y
            WL = Ls + 256
            for sg in range(8):
                fw.dma(km[:, :, 0:2 * nb], seq.kmat[0][sg], reads=[seq.kmat[1]], writes=[kmk])
                for c in range(64):
                    ch = sg * 64 + c
                    w_, w_k = wsh[gc % 3]
                    src = bass.AP(tensor=wd.tensor, offset=ch * WL, ap=[[1, 128], [1, Ls + 128]])
                    fw.dma(w_[:, 0:Ls + 128], src, reads=[wdk], writes=[w_k])
                    pyt, pyk = pys[gc % 3]
                    gc += 1
                    py = pyt[:, 0:nb]
                    for j in range(nb + 1):
                        self.mm(py, w_[:, j * 128:(j + 1) * 128], km[:, c, nb - j:2 * nb - j], j == 0, j == nb, [w_k, kmk], [pyk])
                        if j % 13 == 12:
                            yield
                    self.act(yg[:, 0:nb, c], py, AF.Copy, [pyk, rnk], [ygk], scale=rn[:, ch:ch + 1])
                    yield
                for ib in range(nb // IB):
                    a_, a_k = hx[ib % 2]
                    b_, b_k = hbw[ib % 2]
                    o_, o_k = yo[ib % 2]
                    r0 = ib * IB * 128
                    fw.dma(a_[:, 0:IB, :], seq.hx0g[0][r0:r0 + IB * 128, sg * 64:(sg + 1) * 64].rearrange("(i p) c -> p i c", p=128), reads=[seq.hx0g[1]], writes=[a_k])
                    fw.dma(b_[:, 0:IB, :], seq.hbw[0][r0:r0 + IB * 128, sg * 64:(sg + 1) * 64].rearrange("(i p) c -> p i c", p=128), reads=[seq.hbw[1]], writes=[b_k])
                    self.tt("dve", a_[:, 0:IB, :], a_[:, 0:IB, :], yg[:, ib * IB:(ib + 1) * IB, :], ALU.mult, [a_k, ygk], [a_k])
                    self.tt("pool", o_[:, 0:IB, :], a_[:, 0:IB, :], b_[:, 0:IB, :], ALU.add, [a_k, b_k], [o_k])
                    fw.dma(yd[r0:r0 + IB * 128, 1536 + sg * 64:1536 + (sg + 1) * 64].rearrange("(i p) c -> p i c", p=128), o_[:, 0:IB, :], reads=[o_k], writes=[ydk])
                    yield

    @staticmethod
    def _drain(g):
        if g is not None:
            for _ in g:
                pass

    def _corun(self, main_gens, bg, ratio=2):
        bg_alive = bg is not None
        for g in main_gens:
            if g is None:
                continue
            for _ in g:
                if bg_alive:
                    for _r in range(ratio):
                        try:
                            next(bg)
                        except StopIteration:
                            bg_alive = False
                            break
        if bg_alive:
            self._drain(bg)

    def _close_pools(self):
        while self._pools:
            self._pools.pop().close()

    def build(self):
        self.setup()
        self._pools = []
        dp = self.depth
        sc, sl = self.seqs
        xc, xck = sc.x, sc.xk
        xl, xlk = sl.x, sl.xk
        for l in range(dp):
            last = l == dp - 1
            act_seqs = [sl] if last else [sc, sl]
            self.p_adaln(l)
            self.p_norm(sc, xc, xck, "rm")
            self.p_norm(sl, xl, xlk, "rm")
            self.p_norm(sl, xl, xlk, "cm")
            for s in act_seqs:
                self.p_hy_filter(s, l)
                self.p_hy_front(s, l)
            for s in (sc, sl):
                self.p_ssd_front(s, l)
            g1 = self.p_ssd_scan(l, 1)
            next(g1)
            g0 = self.p_ssd_scan(l, 0)
            self._corun([g0], g1, ratio=1)
            self._close_pools()
            SP_ = Pool(self.fw)
            Wq = SP_.sb([128, 8, 512], BF16)
            Wv = SP_.sb([128, 8, 512], BF16)
            self.load_w(SP_, Wq[0], Wq[1], self.w_in[l][:, O_HQ:O_HQ + 512], 512)
            self.load_w(SP_, Wv[0], Wv[1], self.w_in[l][:, O_HGI:O_HGI + 512], 512)
            self._gla_shared = (Wq, Wv)
            g1 = self.p_gla_scan(l, 1)
            next(g1)
            g0 = self.p_gla_scan(l, 0)
            self._corun([g0], g1, ratio=1)
            self._close_pools()
            SP_.close()
            for s in act_seqs:
                self._drain(self.p_ssd_comb(s, l))
                self._drain(self.p_gla_comb(s, l))
            bg = self.p_hy_conv(act_seqs, l)
            self._drain(bg)
            self.hyP.close()
            if not last:
                self.p_out(sc, l, xc, xck, sc.x1[0], sc.x1[1], False)
                self.p_out(sl, l, xl, xlk, sl.x1[0], sl.x1[1], False)
                xc, xck = sc.x1
                xl, xlk = sl.x1
            else:
                self.p_out(sl, l, xl, xlk, self.out, self.outk, True)
        self.GP.close()
        self.fw.finish()
        return self.nc


def _consts(L, LC):
    import ml_dtypes
    c = {}
    c["c_id32"] = np.eye(128, dtype=np.float32)
    c["c_idbf"] = np.eye(128, dtype=np.float32).astype(ml_dtypes.bfloat16)
    s = np.arange(128)[:, None]
    t = np.arange(128)[None, :]
    c["c_mask"] = np.stack([(s <= t), (s >= t)]).astype(np.float32)
    sel = np.zeros((2, 256), np.float32)
    sel[0, 0:128] = 1.0
    sel[1, 128:256] = 1.0
    c["c_sel"] = sel
    rst = np.ones((128, 2, 512), np.float32)
    rst[:, 0, 0::128] = 0.0
    rst[:, 1, 0::64] = 0.0
    c["c_rst"] = rst
    delta = np.abs(np.linspace(HY_MIN_DECAY, HY_MAX_DECAY, 512, dtype=np.float32))
    c["c_delta"] = np.ascontiguousarray(np.broadcast_to(delta[None, :], (128, 512))).astype(np.float32)
    for Ls in sorted(set([L, LC])):
        nb = Ls // 128
        dd = np.arange(2 * nb)[None, :]
        rp = np.arange(128)[:, None]
        lam = 128 * (dd - nb) + 127 - rp
        pos = np.abs(lam)
        invalid = pos >= Ls
        pos = np.where(invalid, 0, pos)
        tl = np.linspace(0.0, 1.0, Ls, dtype=np.float32)
        tt = tl[pos].astype(np.float32)
        tneg = np.where(invalid, -1.0e4, -tt).astype(np.float32)
        c["c_tneg%d" % Ls] = np.ascontiguousarray(tneg)
        posn = pos.T.reshape(-1)
        bands = np.linspace(1e-4, 15.0, 16, dtype=np.float32)
        ang = (np.float32(2.0 * math.pi / Ls) * posn.astype(np.float32)[:, None] * bands[None, :]).astype(np.float32)
        z = np.concatenate([tl[posn][:, None], np.cos(ang.astype(np.float64)), -np.sin(ang.astype(np.float64))], axis=1)
        c["c_z%d" % Ls] = np.ascontiguousarray(z.T).astype(np.float32)
    return c


def _layout_inputs(inp, b, L, LC, depth):
    f = lambda a: np.ascontiguousarray(np.asarray(a, dtype=np.float32))
    m = {}
    m["x"] = f(inp["x"][b])
    m["ctx"] = f(inp["ctx"][b])
    c2 = np.stack([np.asarray(inp["c"][b]).reshape(128, 8), np.asarray(inp["c_ctx"]).reshape(128, 8)], axis=-1)
    m["c2"] = f(c2)
    m["ada_w"] = f(inp["ada_w"])
    m["ada_b2"] = f(np.broadcast_to(np.asarray(inp["ada_b"])[:, None, :], (depth, 2, 3 * D)))
    m["norm_w2"] = f(np.broadcast_to(np.asarray(inp["norm_w"])[:, None, :], (depth, 2, D)))
    m["fnw_bc"] = f(np.broadcast_to(np.asarray(inp["final_norm_w"])[None, :], (128, D)))
    m["w_in"] = f(inp["w_in"])
    m["w_out"] = f(inp["w_out"])
    cw = np.asarray(inp["ssd_conv_w"])
    m["cw_fm"] = f(cw.reshape(depth, 5, 12, 128).transpose(0, 3, 2, 1))
    m["cb_fm"] = f(np.asarray(inp["ssd_conv_b"]).reshape(depth, 12, 128).transpose(0, 2, 1))
    hcw = np.asarray(inp["hy_conv_w"])
    m["hcw_fm"] = f(hcw.reshape(depth, 3, 12, 128).transpose(0, 3, 2, 1))
    m["hcb_fm"] = f(np.asarray(inp["hy_conv_b"]).reshape(depth, 12, 128).transpose(0, 2, 1))
    m["dtb_fm"] = f(np.asarray(inp["ssd_dt_bias"]).reshape(depth, 32, 1))
    m["alog_fm"] = f(np.asarray(inp["ssd_a_log"]).reshape(depth, 32, 1))
    m["d_bc"] = f(np.broadcast_to(np.asarray(inp["ssd_d"])[:, None, :], (depth, 128, 16)))
    m["snw_bc"] = f(np.broadcast_to(np.asarray(inp["ssd_norm_w"])[:, None, :], (depth, 128, D)))
    lbl = np.asarray(inp["hg_lb_logits"])
    m["lbl_fm"] = f(lbl.reshape(2, depth, 4, 128).transpose(3, 0, 1, 2))
    m["hnw_bc"] = f(np.broadcast_to(np.asarray(inp["hg_norm_w"])[:, None, :], (depth, 128, 512)))
    m["hyb_bc"] = f(np.broadcast_to(np.asarray(inp["hy_bias"])[:, None, :], (depth, 128, 512)))
    m["fw1"] = f(inp["hy_filt_w1"])
    m["fw2"] = f(inp["hy_filt_w2"])
    m["fw3"] = f(inp["hy_filt_w3"])
    m["fvec"] = f(np.stack([np.asarray(inp["hy_filt_b1"]), np.asarray(inp["hy_filt_b2"]), np.asarray(inp["hy_filt_freq"])], axis=-1))
    return m


_CACHE = {}


def run(inputs, L, LC, depth, nb, debug=()):
    key = (L, LC, depth, tuple(debug))
    if key not in _CACHE:
        bld = Builder(L, LC, depth, debug)
        bld.build()
        _CACHE[key] = bld
    bld = _CACHE[key]
    consts = _consts(L, LC)
    in_maps = []
    for b in range(nb):
        m = _layout_inputs(inputs, b, L, LC, depth)
        m.update(consts)
        in_maps.append({k: m[k] for k in bld.inputs})
    res = run_bass_kernel_spmd(bld.nc, in_maps, core_ids=list(range(nb)))
    return res, bld


def kernel(**inputs):
    res, bld = run(inputs, 8192, 256, 2, 8)
    out = np.stack([np.asarray(r["out"], dtype=np.float32) for r in res.results], axis=0)
    return out
```

```python
import numpy as np
import contextlib
import concourse.bass as bass
import concourse.mybir as mybir
from concourse.bass_utils import run_bass_kernel_spmd

F32 = mybir.dt.float32
BF16 = mybir.dt.bfloat16
AF = mybir.ActivationFunctionType
ALU = mybir.AluOpType


class Tk:
    __slots__ = ("w", "r", "multi")

    def __init__(self, multi=False):
        self.w = {}
        self.r = {}
        self.multi = multi


class FW:
    EPOCH = 30000

    def __init__(self, nc):
        self.nc = nc
        self.eng = {"pe": nc.tensor, "dve": nc.vector, "act": nc.scalar, "pool": nc.gpsimd, "sp": nc.sync}
        self.stack = contextlib.ExitStack()
        self.sem = {}
        self.cnt = {}
        self.semobj = {}
        self.waited = {e: {} for e in self.eng}
        self.nsem = 0
        for e in self.eng:
            self._newsem(e)
        self.dring = []
        for i in range(12):
            s = self._alloc_sem("dma%d" % i)
            self.dring.append([s, 0])
        self.dpos = 0
        self.ninst = 0

    def _alloc_sem(self, name):
        s = self.stack.enter_context(self.nc.semaphore(name))
        self.nsem += 1
        self.semobj[id(s)] = s
        return s

    def _newsem(self, e):
        self.sem[e] = self._alloc_sem("s_%s_%d" % (e, self.nsem))
        self.cnt[e] = 0

    def _deps(self, reads, writes):
        d = {}

        def add(dd):
            for k, v in dd.items():
                if d.get(k, 0) < v:
                    d[k] = v
        for t in reads:
            add(t.w)
        for t in writes:
            if not t.multi:
                add(t.w)
            add(t.r)
        return d

    def _wait(self, e, deps, skip_own=False):
        w = self.waited[e]
        own = id(self.sem[e])
        for k, v in deps.items():
            if skip_own and k == own:
                continue
            if w.get(k, 0) >= v:
                continue
            self.eng[e].wait_ge(self.semobj[k], v)
            w[k] = v

    def _record(self, tok, reads, writes):
        k, v = tok
        for t in reads:
            if t.r.get(k, 0) < v:
                t.r[k] = v
        for t in writes:
            if t.multi:
                if t.w.get(k, 0) < v:
                    t.w[k] = v
            else:
                t.w = {k: v}
                t.r = {}

    def op(self, e, fn, reads=(), writes=()):
        deps = self._deps(reads, writes)
        self._wait(e, deps, skip_own=(e == "pe"))
        if self.cnt[e] >= self.EPOCH:
            self._newsem(e)
        inst = fn()
        inst.then_inc(self.sem[e], 1)
        self.cnt[e] += 1
        self.ninst += 1
        self._record((id(self.sem[e]), self.cnt[e]), reads, writes)
        return inst

    def dma(self, out, in_, reads=(), writes=(), q="sp", **kw):
        deps = self._deps(reads, writes)
        slot = self.dring[self.dpos % len(self.dring)]
        self.dpos += 1
        if slot[1] > 0:
            deps[id(slot[0])] = max(deps.get(id(slot[0]), 0), slot[1])
        self._wait(q, deps)
        if slot[1] >= 16 * 1800:
            slot[0] = self._alloc_sem("dmax%d" % self.nsem)
            slot[1] = 0
        inst = self.eng[q].dma_start(out=out, in_=in_, **kw)
        slot[1] += 16
        inst.then_inc(slot[0], 16)
        self.ninst += 1
        self._record((id(slot[0]), slot[1]), reads, writes)
        return inst

    def barrier(self):
        allt = {}
        for e in self.eng:
            if self.cnt[e] > 0:
                allt[id(self.sem[e])] = self.cnt[e]
        for s, v in self.dring:
            if v > 0:
                allt[id(s)] = v
        for e in self.eng:
            self._wait(e, dict(allt))

    def finish(self):
        self.barrier()
        self.stack.close()


class Pool:
    N = 0

    def __init__(self, fw):
        self.fw = fw
        self.stack = contextlib.ExitStack()
        self.n = 0

    def sb(self, shape, dt, name=None):
        Pool.N += 1
        t = self.stack.enter_context(self.fw.nc.sbuf_tensor(name or ("sb_%d" % Pool.N), list(shape), dt))
        return t, Tk()

    def ps(self, shape, dt=F32, name=None):
        Pool.N += 1
        t = self.stack.enter_context(self.fw.nc.psum_tensor(name or ("ps_%d" % Pool.N), list(shape), dt))
        return t, Tk()

    def close(self):
        self.fw.barrier()
        self.stack.close()
import math

AX = mybir.AxisListType
I32 = mybir.dt.int32
D = 1024
KC = 8
EPS = 1e-6
GW = 64
O_XBC, O_DTF, O_DTB, O_HGI, O_HFF, O_HFB, O_HQ, O_HY, O_Z, O_HGG, O_HYG = 0, 1536, 1552, 1568, 2080, 2592, 3104, 3616, 5152, 6176, 6688
IN_COLS = 7200
HY_MIN_DECAY = math.log(1e-2) / 1.5
HY_MAX_DECAY = math.log(1e-2) / 0.3


def bc(ap, shape):
    return ap.to_broadcast(list(shape))


class Seq:
    pass


class Builder:
    def __init__(self, L, LC, depth, debug=()):
        self.L, self.LC, self.depth = L, LC, depth
        self.debug = set(debug)
        self.nc = bass.Bass("TRN2", target_bir_lowering=False)
        self.fw = FW(self.nc)
        self.inputs = {}
        self.outputs = []

    def din(self, name, shape, dt=F32):
        t = self.nc.dram_tensor(name, list(shape), dt, kind="ExternalInput").ap()
        self.inputs[name] = t
        return t

    def scr(self, name, shape, dt):
        kind = "ExternalOutput" if name in self.debug else "Internal"
        t = self.nc.dram_tensor(name, list(shape), dt, kind=kind).ap()
        if name in self.debug:
            self.outputs.append(name)
        return t, Tk(multi=True)

    def tt(self, e, out, in0, in1, op, R, W):
        en = self.fw.eng[e]
        return self.fw.op(e, lambda: en.tensor_tensor(out=out, in0=in0, in1=in1, op=op), R, W)

    def ts(self, e, out, in0, s1, s2, op0, op1, R, W):
        en = self.fw.eng[e]
        if op1 is None:
            return self.fw.op(e, lambda: en.tensor_scalar(out=out, in0=in0, scalar1=s1, scalar2=None, op0=op0), R, W)
        return self.fw.op(e, lambda: en.tensor_scalar(out=out, in0=in0, scalar1=s1, scalar2=s2, op0=op0, op1=op1), R, W)

    def stt(self, out, in0, scalar, in1, op0, op1, R, W):
        nc = self.nc
        return self.fw.op("dve", lambda: nc.vector.scalar_tensor_tensor(out=out, in0=in0, scalar=scalar, in1=in1, op0=op0, op1=op1), R, W)

    def act(self, out, in_, func, R, W, bias=None, scale=None, accum_out=None):
        nc = self.nc
        kw = {}
        if bias is not None:
            kw["bias"] = bias
        if scale is not None:
            kw["scale"] = scale
        if accum_out is not None:
            kw["accum_out"] = accum_out
            npart = accum_out.shape[0]

            def fn():
                nc.scalar.activation(out=out, in_=in_, func=func, **kw)
                return nc.scalar.copy(out=self.dummy[0:npart, 0:1], in_=accum_out)
            return self.fw.op("act", fn, R, W)
        return self.fw.op("act", lambda: nc.scalar.activation(out=out, in_=in_, func=func, **kw), R, W)

    def mm(self, out, lhsT, rhs, start, stop, R, W):
        nc = self.nc
        return self.fw.op("pe", lambda: nc.tensor.matmul(out, lhsT, rhs, start=start, stop=stop), R, W)

    def tr(self, out, in_, ident, R, W):
        nc = self.nc
        return self.fw.op("pe", lambda: nc.tensor.transpose(out, in_, ident), R, W)

    def cp(self, e, out, in_, R, W):
        if e == "act":
            nc = self.nc
            return self.fw.op("act", lambda: nc.scalar.copy(out=out, in_=in_), R, W)
        en = self.fw.eng[e]
        return self.fw.op(e, lambda: en.tensor_copy(out=out, in_=in_), R, W)

    def ms(self, e, ap, val, W):
        en = self.fw.eng[e]
        return self.fw.op(e, lambda: en.memset(ap, val), (), W)

    def load_w(self, P, dst, dk, wsrc, ncols):
        TP = Pool(self.fw)
        st = [TP.sb([128, 8, 256], F32) for _ in range(2)]
        src = wsrc.rearrange("(k p) n -> p k n", p=128)
        i = 0
        for c0 in range(0, ncols, 256):
            n = min(256, ncols - c0)
            s, sk = st[i % 2]
            self.fw.dma(s[:, :, 0:n], src[:, :, c0:c0 + n], writes=[sk])
            self.cp("pool" if i % 2 else "dve", dst[:, :, c0:c0 + n], s[:, :, 0:n], [sk], [dk])
            i += 1
        TP.close()

    def setup(self):
        L, LC, dp = self.L, self.LC, self.depth
        self.x_in = self.din("x", [L, D])
        self.ctx_in = self.din("ctx", [LC, D])
        self.c2_in = self.din("c2", [128, 8, 2])
        self.ada_w = self.din("ada_w", [dp, D, 3 * D])
        self.ada_b2 = self.din("ada_b2", [dp, 2, 3 * D])
        self.norm_w2 = self.din("norm_w2", [dp, 2, D])
        self.fnw_bc = self.din("fnw_bc", [128, D])
        self.w_in = self.din("w_in", [dp, D, IN_COLS])
        self.w_out = self.din("w_out", [dp, 2 * D, D])
        self.cw_fm = self.din("cw_fm", [dp, 128, 12, 5])
        self.cb_fm = self.din("cb_fm", [dp, 128, 12])
        self.hcw_fm = self.din("hcw_fm", [dp, 128, 12, 3])
        self.hcb_fm = self.din("hcb_fm", [dp, 128, 12])
        self.dtb_fm = self.din("dtb_fm", [dp, 32, 1])
        self.alog_fm = self.din("alog_fm", [dp, 32, 1])
        self.d_bc = self.din("d_bc", [dp, 128, 16])
        self.snw_bc = self.din("snw_bc", [dp, 128, D])
        self.lbl_fm = self.din("lbl_fm", [128, 2, dp, 4])
        self.hnw_bc = self.din("hnw_bc", [dp, 128, 512])
        self.hyb_bc = self.din("hyb_bc", [dp, 128, 512])
        self.fw1 = self.din("fw1", [dp, 33, 64])
        self.fw2 = self.din("fw2", [dp, 64, 64])
        self.fw3 = self.din("fw3", [dp, 64, 1024])
        self.fvec = self.din("fvec", [dp, 64, 3])
        self.c_id32 = self.din("c_id32", [128, 128])
        self.c_idbf = self.din("c_idbf", [128, 128], BF16)
        self.c_mask = self.din("c_mask", [2, 128, 128])
        self.c_sel = self.din("c_sel", [2, 256])
        self.c_rst = self.din("c_rst", [128, 2, 512])
        self.c_delta = self.din("c_delta", [128, 512])
        self.c_z = {}
        self.c_tneg = {}
        for Ls in sorted(set([L, LC])):
            self.c_z[Ls] = self.din("c_z%d" % Ls, [33, 2 * Ls])
            self.c_tneg[Ls] = self.din("c_tneg%d" % Ls, [128, 2 * Ls // 128])
        self.out = self.nc.dram_tensor("out", [L, D], F32, kind="ExternalOutput").ap()
        self.outk = Tk(multi=True)
        self.ink = Tk(multi=True)
        self.GP = Pool(self.fw)
        G = self.GP
        self.id32, self.id32k = G.sb([128, 128], F32)
        self.idbf, self.idbfk = G.sb([128, 128], BF16)
        self.mask, self.maskk = G.sb([128, 2, 128], F32)
        self.rst, self.rstk = G.sb([128, 2, 512], F32)
        self.gsh, self.gshk = G.sb([128, 8, 2, 2], F32)
        self.gate, self.gatek = G.sb([128, 2, D], F32)
        self.zero, self.zerok = G.sb([128, 160], BF16)
        self.dummy, self.dummyk = G.sb([128, 4], F32)
        fw = self.fw
        fw.dma(self.id32[:], self.c_id32, writes=[self.id32k])
        fw.dma(self.idbf[:], self.c_idbf, writes=[self.idbfk])
        fw.dma(self.mask[:], self.c_mask.rearrange("a s t -> s a t"), writes=[self.maskk])
        fw.dma(self.rst[:], self.c_rst, writes=[self.rstk])
        self.ms("dve", self.zero[:], 0.0, [self.zerok])
        self.seqs = []
        for nm, Ls, xin in (("c", LC, self.ctx_in), ("l", L, self.x_in)):
            s = Seq()
            s.name, s.L, s.is_lat = nm, Ls, nm == "l"
            s.T = min(512, Ls)
            s.x, s.xk = xin, self.ink
            s.hT = {"rm": self.scr("hT_" + nm, [D, Ls + 4], BF16)}
            if s.is_lat:
                s.hT["cm"] = self.scr("hTc_" + nm, [D, Ls + 4], BF16)
            else:
                s.hT["cm"] = s.hT["rm"]
            s.xbcT = self.scr("xbcT_" + nm, [1536, Ls], BF16)
            s.dta = self.scr("dta_" + nm, [32, 2, Ls], F32)
            s.ysd = [self.scr("ysd%d_%s" % (d, nm), [Ls, D], F32) for d in range(2)]
            s.od = [self.scr("od%d_%s" % (d, nm), [Ls, 512], F32) for d in range(2)]
            s.y = self.scr("y_" + nm, [Ls, 2 * D], BF16)
            s.wT = self.scr("wT_" + nm, [512, Ls + 256], BF16)
            s.hx0g = self.scr("hx0g_" + nm, [Ls, 512], F32)
            s.hbw = self.scr("hbw_" + nm, [Ls, 512], F32)
            s.x1 = self.scr("x1_" + nm, [Ls, D], F32)
            s.kmat = self.scr("kmat_" + nm, [8, 128, 64, 2 * Ls // 128], BF16)
            s.rn = self.scr("rn_" + nm, [128, 512], F32)
            self.seqs.append(s)
        self.cumsc2 = [[self.scr("cumsc%d_%d" % (d, j), [16, 512], F32) for j in range(2)] for d in range(2)]
        for s in self.seqs:
            for key in (("rm", "cm") if s.is_lat else ("rm",)):
                hd, hk = s.hT[key]
                v = hd.rearrange("(k p) t -> p k t", p=128)
                fw.dma(v[:, :, 0:2], self.zero[:, 0:16].rearrange("p (k t) -> p k t", k=8), reads=[self.zerok], writes=[hk])
                fw.dma(v[:, :, s.L + 2:s.L + 4], self.zero[:, 0:16].rearrange("p (k t) -> p k t", k=8), reads=[self.zerok], writes=[hk])
            wd, wk = s.wT
            for c0 in range(0, 512, 128):
                fw.dma(wd[c0:c0 + 128, 0:127], self.zero[:, 0:127], reads=[self.zerok], writes=[wk])
                fw.dma(wd[c0:c0 + 128, 127 + s.L:s.L + 256], self.zero[:, 0:129], reads=[self.zerok], writes=[wk])

    def p_adaln(self, l):
        fw, nc = self.fw, self.nc
        P = Pool(fw)
        c2, c2k = P.sb([128, 8, 2], F32)
        wst = [P.sb([128, 8, 512], F32) for _ in range(2)]
        mod, modk = P.sb([2, 3 * D], F32)
        adab, adabk = P.sb([2, 3 * D], F32)
        nw2, nw2k = P.sb([2, D], F32)
        grow, growk = P.sb([2, 2, D], F32)
        sel, selk = P.sb([2, 256], F32)
        ps, psk = P.ps([128, 512])
        pst, pstk = P.ps([128, 512])
        fw.dma(c2[:], self.c2_in, writes=[c2k])
        fw.dma(adab[:], self.ada_b2[l], writes=[adabk])
        fw.dma(nw2[:], self.norm_w2[l], writes=[nw2k])
        fw.dma(sel[:], self.c_sel, writes=[selk])
        self.act(c2[:], c2[:], AF.Silu, [c2k], [c2k])
        wv = self.ada_w[l].rearrange("(p k) n -> p k n", k=8)
        for cb in range(6):
            w, wk = wst[cb % 2]
            fw.dma(w[:], wv[:, :, cb * 512:(cb + 1) * 512], writes=[wk])
            for k in range(8):
                self.mm(ps[0:2, :], c2[:, k, :], w[:, k, :], k == 0, k == 7, [c2k, wk], [psk])
            self.tt("dve", mod[:, cb * 512:(cb + 1) * 512], ps[0:2, :], adab[:, cb * 512:(cb + 1) * 512], ALU.add, [psk, adabk], [modk])
        self.stt(grow[:, 0, :], mod[:, D:2 * D], 1.0, nw2[:], ALU.add, ALU.mult, [modk, nw2k], [growk])
        self.cp("dve", grow[:, 1, :], mod[:, 0:D], [modk], [growk])
        for k in range(8):
            for w_ in range(2):
                o = (k * 2 + w_) * 2
                self.tr(pst[:, o:o + 2], grow[0:2, w_, k * 128:(k + 1) * 128], self.id32[0:2, 0:2], [growk, self.id32k], [pstk])
        self.cp("dve", self.gsh[:].rearrange("p k a b -> p (k a b)"), pst[:, 0:32], [pstk], [self.gshk])
        for r in range(2):
            for h in range(2):
                self.mm(ps[:, :], sel[0:2, r * 128:(r + 1) * 128], mod[0:2, 2 * D + h * 512:2 * D + (h + 1) * 512], True, True, [selk, modk], [psk])
                self.cp("dve", self.gate[:, r, h * 512:(h + 1) * 512], ps[:, :], [psk], [self.gatek])
        P.close()

    def x_tile_src(self, seq, xap, i, order):
        if order == "rm" or not seq.is_lat:
            return [(0, 128, xap[i * 128:(i + 1) * 128, :])]
        rows = seq.L // GW
        v = xap.rearrange("(r c) d -> c r d", c=GW)
        if rows >= 128:
            assert rows == 128
            return [(0, 128, v[i])]
        ncol = 128 // rows
        return [(j * rows, rows, v[i * ncol + j]) for j in range(ncol)]

    def p_norm(self, seq, xap, xk, order):
        fw, nc = self.fw, self.nc
        row = 0 if seq.is_lat else 1
        hd, hk = seq.hT[order]
        hv = hd.rearrange("(k p) t -> p k t", p=128)
        P = Pool(fw)
        xt = [P.sb([128, D], F32) for _ in range(2)]
        junk, junkk = P.sb([128, D], BF16)
        ss = [P.sb([128, 2], F32) for _ in range(2)]
        xn = [P.sb([128, D], F32) for _ in range(2)]
        tmp, tmpk = P.sb([128, 8, 128], F32)
        hs = [P.sb([128, 8, 512], BF16) for _ in range(2)]
        ps = [P.ps([128, 1024]) for _ in range(2)]
        nt = seq.L // 128
        T = seq.T
        per = T // 128
        def load(i):
            x_, x_k = xt[i % 2]
            for (p0, n, src) in self.x_tile_src(seq, xap, i, order):
                fw.dma(x_[p0:p0 + n, :], src, reads=[xk], writes=[x_k])
        load(0)
        for i in range(nt):
            x_, x_k = xt[i % 2]
            if i + 1 < nt:
                load(i + 1)
            s_, s_k = ss[i % 2]
            self.act(junk[:], x_[:], AF.Square, [x_k], [junkk, s_k], accum_out=s_[:, 0:1])
            self.ts("dve", s_[:, 1:2], s_[:, 0:1], 1.0 / D, EPS, ALU.mult, ALU.add, [s_k], [s_k])
            self.act(s_[:, 1:2], s_[:, 1:2], AF.Sqrt, [s_k], [s_k])
            fw.op("dve", lambda: nc.vector.reciprocal(out=s_[:, 1:2], in_=s_[:, 1:2]), [s_k], [s_k])
            n_, n_k = xn[i % 2]
            self.act(n_[:], x_[:], AF.Copy, [x_k, s_k], [n_k], scale=s_[:, 1:2])
            p_, p_k = ps[i % 2]
            for k in range(8):
                self.tr(p_[:, k * 128:(k + 1) * 128], n_[:, k * 128:(k + 1) * 128], self.id32[:], [n_k, self.id32k], [p_k])
            h_, h_k = hs[(i // per) % 2]
            j = i % per
            self.tt("dve", tmp[:], p_[:].rearrange("p (k t) -> p k t", k=8), bc(self.gsh[:, :, 0, row:row + 1], [128, 8, 128]), ALU.mult, [p_k, self.gshk], [tmpk])
            self.tt("pool", h_[:, :, j * 128:(j + 1) * 128], tmp[:], bc(self.gsh[:, :, 1, row:row + 1], [128, 8, 128]), ALU.add, [tmpk, self.gshk], [h_k])
            if j == per - 1:
                t0 = (i // per) * T
                fw.dma(hv[:, :, 2 + t0:2 + t0 + T], h_[:, :, 0:T], reads=[h_k], writes=[hk])
        P.close()

    def p_ssd_front(self, seq, l):
        fw, nc = self.fw, self.nc
        P = Pool(fw)
        W, Wk = P.sb([128, 8, 1568], BF16)
        self.load_w(P, W, Wk, self.w_in[l][:, O_XBC:O_XBC + 1568], 1568)
        cw, cwk = P.sb([128, 12, 5], F32)
        cb, cbk = P.sb([128, 12], F32)
        dtb, dtbk = P.sb([32, 1], F32)
        aneg, anegk = P.sb([32, 1], F32)
        fw.dma(cw[:], self.cw_fm[l], writes=[cwk])
        fw.dma(cb[:], self.cb_fm[l], writes=[cbk])
        fw.dma(dtb[:], self.dtb_fm[l], writes=[dtbk])
        fw.dma(aneg[:], self.alog_fm[l], writes=[anegk])
        self.act(aneg[:], aneg[:], AF.Exp, [anegk], [anegk])
        self.ts("dve", aneg[:], aneg[:], -1.0, None, ALU.mult, None, [anegk], [anegk])
        T = seq.T
        hw = [P.sb([128, 8, T + 4], BF16) for _ in range(2)]
        ps = [P.ps([128, 1024]) for _ in range(2)]
        acc = [P.sb([128, T], F32) for _ in range(2)]
        xst = [P.sb([128, 12, T], BF16) for _ in range(2)]
        psd, psdk = P.ps([128, 512])
        dst = [P.sb([32, 2, T], F32) for _ in range(2)]
        hd, hk = seq.hT["rm"]
        hv = hd.rearrange("(k p) t -> p k t", p=128)
        xd, xdk = seq.xbcT
        xdv = xd.rearrange("(b p) t -> p b t", p=128)
        dd, ddk = seq.dta
        nsup = seq.L // T
        def load(si):
            h_, h_k = hw[si % 2]
            fw.dma(h_[:], hv[:, :, si * T:si * T + T + 4], reads=[hk], writes=[h_k])
        load(0)
        for si in range(nsup):
            t0 = si * T
            h_, h_k = hw[si % 2]
            if si + 1 < nsup:
                load(si + 1)
            x_, x_k = xst[si % 2]
            for b in range(12):
                p_, p_k = ps[b % 2]
                for k in range(8):
                    self.mm(p_[:, 0:T], W[:, k, b * 128:(b + 1) * 128], h_[:, k, 0:T], k == 0, k == 7, [Wk, h_k], [p_k])
                for k in range(8):
                    self.mm(p_[:, T:T + 4], W[:, k, b * 128:(b + 1) * 128], h_[:, k, T:T + 4], k == 0, k == 7, [Wk, h_k], [p_k])
                a_, a_k = acc[b % 2]
                self.act(a_[:], p_[:, 0:T], AF.Identity, [p_k, cwk, cbk], [a_k], scale=cw[:, b, 0:1], bias=cb[:, b:b + 1])
                for k in range(1, 5):
                    self.stt(a_[:], p_[:, k:k + T], cw[:, b, k:k + 1], a_[:], ALU.mult, ALU.add, [p_k, cwk, a_k], [a_k])
                self.act(x_[:, b, :], a_[:], AF.Silu, [a_k], [x_k])
            fw.dma(xdv[:, :, t0:t0 + T], x_[:], reads=[x_k], writes=[xdk])
            for k in range(8):
                self.mm(psd[0:32, 0:T], W[:, k, 1536:1568], h_[:, k, 2:2 + T], k == 0, k == 7, [Wk, h_k], [psdk])
            d_, d_k = dst[si % 2]
            self.act(d_[:, 0, :], psd[0:32, 0:T], AF.Exp, [psdk, dtbk], [d_k], bias=dtb[:, 0:1])
            self.act(d_[:, 0, :], d_[:, 0, :], AF.Ln, [d_k], [d_k], bias=1.0)
            self.ts("dve", d_[:, 1, :], d_[:, 0, :], aneg[:, 0:1], None, ALU.mult, None, [d_k, anegk], [d_k])
            fw.dma(dd[:, :, t0:t0 + T], d_[:], reads=[d_k], writes=[ddk])
        P.close()

    def p_ssd_scan(self, l, d):
        fw, nc = self.fw, self.nc
        P = Pool(fw)
        self._pools.append(P)
        TS = 256
        S, Sk = P.sb([128, 16, 64], F32)
        Sbf, Sbfk = P.sb([128, 1024], BF16)
        self.ms("dve", S[:], 0.0, [Sk])
        self.ms("pool", Sbf[:], 0.0, [Sbfk])
        drow, drowk = P.sb([128, 16], F32)
        fw.dma(drow[:], self.d_bc[l], writes=[drowk])
        xs2 = [P.sb([128, 12, TS], BF16) for _ in range(2)]
        da2 = [P.sb([16, 2, TS], F32) for _ in range(2)]
        cumT2 = [P.sb([16, TS], F32) for _ in range(2)]
        R4 = [P.sb([128, 16, 128], F32) for _ in range(4)]
        ctok, ctokk = P.sb([128, 48], F32)
        Dm, Dmk = P.sb([128, 16, 128], F32)
        dec, deck = P.sb([128, 16, 128], BF16)
        sm, smk = P.sb([128, 2, 128], F32)
        MT, MTk = P.sb([128, 16, 128], BF16)
        Btok, Btokk = P.sb([128, 256], BF16)
        xdt, xdtk = P.sb([128, 16, 64], BF16)
        xsD, xsDk = P.sb([128, 16, 64], BF16)
        xdtw, xdtwk = P.sb([128, 16, 64], BF16)
        yi, yik = P.sb([128, 8, 64], F32)
        yo2 = [P.sb([128, 1024], F32) for _ in range(2)]
        etot, etotk = P.sb([128, 16], F32)
        tot, totk = P.sb([16, 8], F32)
        pm, pmk = P.ps([128, 512])
        psx, psxk = P.ps([128, 1024], BF16)
        psy, psyk = P.ps([128, 512])
        psi, psik = P.ps([128, 512])
        pst = pm[:, 0:32]
        pss = pm[:, 32:288]
        yield
        mk = self.mask[:, d, :]
        gc = 0
        plan = []
        for seq in self.seqs:
            T = min(TS, seq.L)
            nsup = seq.L // T
            sups = list(range(nsup) if d == 0 else range(nsup - 1, -1, -1))
            plan.extend((seq, si) for si in sups)
        Rb = {}

        def prologue(pi):
            seq, si = plan[pi]
            T = min(TS, seq.L)
            nch = T // 128
            t0 = si * T
            xd, xdk = seq.xbcT
            xdv = xd.rearrange("(b p) t -> p b t", p=128)
            dd, ddk = seq.dta
            xs, xsk = xs2[pi % 2]
            da, dak = da2[pi % 2]
            cumT, cumTk = cumT2[pi % 2]
            cd, cdk = self.cumsc2[d][pi % 2]
            fw.dma(xs[:, :, 0:T], xdv[:, :, t0:t0 + T], reads=[xdk], writes=[xsk])
            fw.dma(da[:, :, 0:T], dd[d * 16:(d + 1) * 16, :, t0:t0 + T], reads=[ddk], writes=[dak])
            fw.op("dve", lambda: nc.vector.tensor_tensor_scan(out=cumT[:, 0:T], data0=self.rst[0:16, 0, 0:T], data1=da[:, 1, 0:T], initial=0.0, op0=ALU.mult, op1=ALU.add), [self.rstk, dak], [cumTk])
            if d == 1:
                c3 = cumT[:, 0:T].rearrange("p (c t) -> p c t", t=128)
                self.cp("dve", tot[:, 0:nch], cumT[:, 127:T:128], [cumTk], [totk])
                self.tt("dve", c3, bc(tot[:, 0:nch].unsqueeze(2), [16, nch, 128]), c3, ALU.subtract, [cumTk, totk], [cumTk])
                self.tt("dve", cumT[:, 0:T], cumT[:, 0:T], da[:, 1, 0:T], ALU.add, [cumTk, dak], [cumTk])
            fw.dma(cd[:, 0:T], cumT[:, 0:T], reads=[cumTk], writes=[cdk])
            for ci in range(nch):
                R, Rk = R4[(pi % 2) * 2 + ci]
                rsrc = bass.AP(tensor=cd.tensor, offset=ci * 128, ap=[[0, 128], [512, 16], [1, 128]])
                fw.dma(R[:], rsrc, reads=[cdk], writes=[Rk])
        prologue(0)
        for pi in range(len(plan)):
            seq, si = plan[pi]
            if True:
                T = min(TS, seq.L)
                nch = T // 128
                yd, ydk = seq.ysd[d]
                t0 = si * T
                xs, xsk = xs2[pi % 2]
                da, dak = da2[pi % 2]
                cumT, cumTk = cumT2[pi % 2]
                if pi + 1 < len(plan):
                    prologue(pi + 1)
                yield
                chs = range(nch) if d == 0 else range(nch - 1, -1, -1)
                for ci in chs:
                    c0 = ci * 128
                    R, Rk = R4[(pi % 2) * 2 + ci]
                    yo, yok = yo2[gc % 2]
                    gc += 1
                    self.tr(pst[:, 0:16], cumT[:, c0:c0 + 128], self.id32[0:16, 0:16], [cumTk, self.id32k], [pmk])
                    self.tr(pst[:, 16:32], da[:, 0, c0:c0 + 128], self.id32[0:16, 0:16], [dak, self.id32k], [pmk])
                    for b in range(8):
                        self.tr(psx[:, b * 128:(b + 1) * 128], xs[:, b, c0:c0 + 128], self.idbf[:], [xsk, self.idbfk], [psxk])
                    yield
                    self.cp("act", ctok[:, 0:32], pst[:, 0:32], [pmk], [ctokk])
                    self.act(ctok[:, 32:48], ctok[:, 0:16], AF.Exp, [ctokk], [ctokk])
                    for g in range(2):
                        self.mm(pss[:, g * 128:(g + 1) * 128], xs[:, 8 + g, c0:c0 + 128], xs[:, 10 + g, c0:c0 + 128], True, True, [xsk], [pmk])
                    yield
                    self.tt("dve", Dm[:], R[:], bc(ctok[:, 0:16].unsqueeze(2), [128, 16, 128]), ALU.subtract, [Rk, ctokk], [Dmk])
                    self.ts("dve", Dm[:], Dm[:], 0.0, None, ALU.min, None, [Dmk], [Dmk])
                    self.tt("dve", sm[:], pss.rearrange("p (g t) -> p g t", g=2), bc(mk.unsqueeze(1), [128, 2, 128]), ALU.mult, [pmk, self.maskk], [smk])
                    px3 = psx[:].rearrange("p (e q) -> p e q", e=16)
                    self.tt("dve", xdt[:], px3, bc(ctok[:, 16:32].unsqueeze(2), [128, 16, 64]), ALU.mult, [psxk, ctokk], [xdtk])
                    if d == 0:
                        self.tt("dve", xsD[:], px3, bc(drow[:].unsqueeze(2), [128, 16, 64]), ALU.mult, [psxk, drowk], [xsDk])
                    yield
                    self.act(dec[:], Dm[:], AF.Exp, [Dmk], [deck])
                    for g in range(2):
                        self.tr(psx[:, g * 128:(g + 1) * 128], xs[:, 8 + g, c0:c0 + 128], self.idbf[:], [xsk, self.idbfk], [psxk])
                    self.cp("act", Btok[:], psx[:, 0:256], [psxk], [Btokk])
                    yield
                    for g in range(2):
                        self.tt("pool" if g else "dve", MT[:, g * 8:(g + 1) * 8, :], dec[:, g * 8:(g + 1) * 8, :], bc(sm[:, g:g + 1, :], [128, 8, 128]), ALU.mult, [deck, smk], [MTk])
                    col = 127 if d == 0 else 0
                    self.tt("pool", xdtw[:], xdt[:], bc(dec[:, :, col:col + 1], [128, 16, 64]), ALU.mult, [xdtk, deck], [xdtwk])
                    self.act(etot[:], R[:, :, col], AF.Exp, [Rk], [etotk])
                    yield
                    for g in range(2):
                        if d == 0:
                            self.mm(psy[:, :], self.idbf[:], xsD[:, g * 8:(g + 1) * 8, :].rearrange("p e q -> p (e q)"), True, False, [self.idbfk, xsDk], [psyk])
                        for e in range(8):
                            ge = g * 8 + e
                            self.mm(psy[:, e * 64:(e + 1) * 64], MT[:, ge, :], xdt[:, ge, :], (d == 1) and e == 0, e == 7, [MTk, xdtk], [psyk])
                        self.mm(psi[:, :], xs[:, 10 + g, c0:c0 + 128], Sbf[:, g * 512:(g + 1) * 512], True, True, [xsk, Sbfk], [psik])
                        yield
                        self.tt("dve", yi[:], psi[:].rearrange("p (e q) -> p e q", e=8), bc(ctok[:, 32 + g * 8:40 + g * 8].unsqueeze(2), [128, 8, 64]), ALU.mult, [psik, ctokk], [yik])
                        self.tt("dve", yo[:, g * 512:(g + 1) * 512], psy[:], yi[:].rearrange("p e q -> p (e q)"), ALU.add, [psyk, yik], [yok])
                        yield
                    fw.dma(yd[t0 + c0:t0 + c0 + 128, :], yo[:], reads=[yok], writes=[ydk])
                    for g in range(2):
                        self.mm(psi[:, :], Btok[:, g * 128:(g + 1) * 128], xdtw[:, g * 8:(g + 1) * 8, :].rearrange("p e q -> p (e q)"), True, True, [Btokk, xdtwk], [psik])
                        self.tt("pool", S[:, g * 8:(g + 1) * 8, :], S[:, g * 8:(g + 1) * 8, :], bc(etot[:, g * 8:(g + 1) * 8].unsqueeze(2), [128, 8, 64]), ALU.mult, [Sk, etotk], [Sk])
                        yield
                        self.tt("dve", S[:, g * 8:(g + 1) * 8, :], S[:, g * 8:(g + 1) * 8, :], psi[:].rearrange("p (e q) -> p e q", e=8), ALU.add, [Sk, psik], [Sk])
                    self.cp("act", Sbf[:], S[:].rearrange("p e q -> p (e q)"), [Sk], [Sbfk])
                    yield

    def p_ssd_comb(self, seq, l):
        fw, nc = self.fw, self.nc
        P = Pool(fw)
        W, Wk = P.sb([128, 8, 1024], BF16)
        self.load_w(P, W, Wk, self.w_in[l][:, O_Z:O_Z + 1024], 1024)
        nw, nwk = P.sb([128, 1024], F32)
        fw.dma(nw[:], self.snw_bc[l], writes=[nwk])
        T = seq.T
        hw = [P.sb([128, 8, T], BF16) for _ in range(2)]
        yf = [P.sb([128, 1024], F32) for _ in range(2)]
        yb = [P.sb([128, 1024], F32) for _ in range(2)]
        sz, szk = P.sb([128, 1024], F32)
        junk, junkk = P.sb([128, 512], BF16)
        ss, ssk = P.sb([128, 4], F32)
        yo = [P.sb([128, 1024], BF16) for _ in range(2)]
        ps = [P.ps([128, 1024]) for _ in range(2)]
        hd, hk = seq.hT["rm"]
        hv = hd.rearrange("(k p) t -> p k t", p=128)
        yd, ydk = seq.y
        per = T // 128
        ntile = seq.L // 128

        def load(i):
            h_, h_k = hw[(i // per) % 2]
            if i % per == 0:
                fw.dma(h_[:], hv[:, :, 2 + i * 128:2 + i * 128 + T], reads=[hk], writes=[h_k])
            f_, f_k = yf[i % 2]
            b_, b_k = yb[i % 2]
            fw.dma(f_[:], seq.ysd[0][0][i * 128:(i + 1) * 128, :], reads=[seq.ysd[0][1]], writes=[f_k])
            fw.dma(b_[:], seq.ysd[1][0][i * 128:(i + 1) * 128, :], reads=[seq.ysd[1][1]], writes=[b_k])
        load(0)
        for i in range(ntile):
            h_, h_k = hw[(i // per) % 2]
            j = i % per
            f_, f_k = yf[i % 2]
            b_, b_k = yb[i % 2]
            if i + 1 < ntile:
                load(i + 1)
            p_, p_k = ps[i % 2]
            for h in range(2):
                for k in range(8):
                    self.mm(p_[:, h * 512:(h + 1) * 512], h_[:, k, j * 128:(j + 1) * 128], W[:, k, h * 512:(h + 1) * 512], k == 0, k == 7, [h_k, Wk], [p_k])
            self.act(sz[:], p_[:], AF.Silu, [p_k], [szk])
            self.tt("pool", f_[:], f_[:], b_[:], ALU.add, [f_k, b_k], [f_k])
            self.tt("dve", f_[:], f_[:], sz[:], ALU.mult, [f_k, szk], [f_k])
            for g in range(2):
                self.act(junk[:], f_[:, g * 512:(g + 1) * 512], AF.Square, [f_k], [junkk, ssk], accum_out=ss[:, g:g + 1])
            self.ts("dve", ss[:, 2:4], ss[:, 0:2], 1.0 / 512, EPS, ALU.mult, ALU.add, [ssk], [ssk])
            self.act(ss[:, 2:4], ss[:, 2:4], AF.Sqrt, [ssk], [ssk])
            fw.op("dve", lambda: nc.vector.reciprocal(out=ss[:, 2:4], in_=ss[:, 2:4]), [ssk], [ssk])
            o_, o_k = yo[i % 2]
            for g in range(2):
                self.stt(o_[:, g * 512:(g + 1) * 512], f_[:, g * 512:(g + 1) * 512], ss[:, 2 + g:3 + g], nw[:, g * 512:(g + 1) * 512], ALU.mult, ALU.mult, [f_k, ssk, nwk], [o_k])
            fw.dma(yd[i * 128:(i + 1) * 128, 0:1024], o_[:], reads=[o_k], writes=[ydk])
            yield
        P.close()

    def p_out(self, seq, l, xap, xk, dst, dstk, final):
        fw, nc = self.fw, self.nc
        row = 0 if seq.is_lat else 1
        P = Pool(fw)
        W, Wk = P.sb([128, 16, 1024], BF16)
        st = [P.sb([128, 16, 128], F32) for _ in range(2)]
        wsrc = self.w_out[l].rearrange("(k p) n -> p k n", p=128)
        for i, c0 in enumerate(range(0, 1024, 128)):
            s, sk = st[i % 2]
            fw.dma(s[:], wsrc[:, :, c0:c0 + 128], writes=[sk])
            self.cp("pool" if i % 2 else "dve", W[:, :, c0:c0 + 128], s[:], [sk], [Wk])
        fnw, fnwk = P.sb([128, 1024], F32)
        fw.dma(fnw[:], self.fnw_bc, writes=[fnwk])
        yt = [P.sb([128, 2048], BF16) for _ in range(2)]
        yT, yTk = P.sb([128, 16, 128], BF16)
        xt = [P.sb([128, 1024], F32) for _ in range(2)]
        xo = [P.sb([128, 1024], F32) for _ in range(2)]
        junk, junkk = P.sb([128, 1024], BF16)
        ss, ssk = P.sb([128, 2], F32)
        pT, pTk = P.ps([128, 2048], BF16)
        ps = [P.ps([128, 1024]) for _ in range(2)]
        yd, ydk = seq.y
        ntile = seq.L // 128

        def load(i):
            y_, y_k = yt[i % 2]
            x_, x_k = xt[i % 2]
            fw.dma(y_[:], yd[i * 128:(i + 1) * 128, :], reads=[ydk], writes=[y_k])
            fw.dma(x_[:], xap[i * 128:(i + 1) * 128, :], reads=[xk], writes=[x_k])
        load(0)
        for i in range(ntile):
            y_, y_k = yt[i % 2]
            x_, x_k = xt[i % 2]
            if i + 1 < ntile:
                load(i + 1)
            for k in range(16):
                self.tr(pT[:, k * 128:(k + 1) * 128], y_[:, k * 128:(k + 1) * 128], self.idbf[:], [y_k, self.idbfk], [pTk])
            self.cp("act", yT[:].rearrange("p k t -> p (k t)"), pT[:], [pTk], [yTk])
            p_, p_k = ps[i % 2]
            for h in range(2):
                for k in range(16):
                    self.mm(p_[:, h * 512:(h + 1) * 512], yT[:, k, :], W[:, k, h * 512:(h + 1) * 512], k == 0, k == 15, [yTk, Wk], [p_k])
            o_, o_k = xo[i % 2]
            self.tt("dve", o_[:], p_[:], self.gate[:, row, :], ALU.mult, [p_k, self.gatek], [o_k])
            self.tt("pool", o_[:], o_[:], x_[:], ALU.add, [o_k, x_k], [o_k])
            if final:
                self.act(junk[:], o_[:], AF.Square, [o_k], [junkk, ssk], accum_out=ss[:, 0:1])
                self.ts("dve", ss[:, 1:2], ss[:, 0:1], 1.0 / D, EPS, ALU.mult, ALU.add, [ssk], [ssk])
                self.act(ss[:, 1:2], ss[:, 1:2], AF.Sqrt, [ssk], [ssk])
                fw.op("dve", lambda: nc.vector.reciprocal(out=ss[:, 1:2], in_=ss[:, 1:2]), [ssk], [ssk])
                self.stt(o_[:], o_[:], ss[:, 1:2], fnw[:], ALU.mult, ALU.mult, [o_k, ssk, fnwk], [o_k])
            fw.dma(dst[i * 128:(i + 1) * 128, :], o_[:], reads=[o_k], writes=[dstk])
        P.close()

    def p_gla_scan(self, l, d):
        fw, nc = self.fw, self.nc
        P = Pool(fw)
        self._pools.append(P)
        Wf, Wfk = P.sb([128, 8, 512], BF16)
        (Wq, Wqk), (Wv, Wvk) = self._gla_shared
        self.load_w(P, Wf, Wfk, self.w_in[l][:, (O_HFF if d == 0 else O_HFB):(O_HFF if d == 0 else O_HFB) + 512], 512)
        lbl, lblk = P.sb([128, 2, self.depth, 4], F32)
        lb, lbk = P.sb([128, 8], F32)
        fw.dma(lbl[:], self.lbl_fm, writes=[lblk])
        if l == 0:
            self.ms("dve", lb[:, 0:4], 0.0, [lbk])
        else:
            self.tt("dve", lb[:, 0:4], lbl[:, d, 1, :], lbl[:, d, 0, :], ALU.subtract, [lblk], [lbk])
            self.act(lb[:, 0:4], lb[:, 0:4], AF.Sigmoid, [lbk], [lbk])
        self.ts("dve", lb[:, 4:8], lb[:, 0:4], -1.0, 1.0, ALU.mult, ALU.add, [lbk], [lbk])
        S, Sk = P.sb([128, 4, 128], F32)
        Sbf, Sbfk = P.sb([128, 4, 128], BF16)
        self.ms("dve", S[:], 0.0, [Sk])
        self.ms("pool", Sbf[:], 0.0, [Sbfk])
        hw2 = [P.sb([128, 8, 512], BF16) for _ in range(1)]
        f_, fk = P.sb([128, 512], F32)
        lf, lfk = P.sb([128, 512], F32)
        kk, kkk = P.sb([128, 512], F32)
        cum, cumk = P.sb([128, 512], F32)
        E, Ek = P.sb([128, 512], F32)
        ex, exk = P.sb([128, 4, 512], F32)
        ref, refk = P.sb([128, 16], F32)
        etot, etotk = P.sb([128, 4, 8], F32)
        qt, qtk = P.sb([128, 4, 512], BF16)
        qp, qpk = P.sb([128, 4, 512], BF16)
        kt, ktk = P.sb([128, 4, 512], BF16)
        kh, khk = P.sb([128, 4, 512], BF16)
        vtok, vtokk = P.sb([64, 4, 512], BF16)
        khtok, khtokk = P.sb([64, 4, 512], BF16)
        am, amk = P.sb([64, 4, 256], BF16)
        amf, amfk = P.sb([64, 256], F32)
        ost2 = [P.sb([64, 512], F32) for _ in range(2)]
        psf, psfk = P.ps([128, 512])
        psq, psqk = P.ps([128, 512])
        psv, psvk = psf, psfk
        pskh, pskhk = P.ps([128, 1024], BF16)
        psa, psak = psq, psqk
        pso, psok = P.ps([128, 512])
        yield
        psS, psSk = psq, psqk
        mk = self.mask[0:64, d, 0:64]
        gi = 0
        for seq in self.seqs:
            T = seq.T
            nch = T // 64
            nsup = seq.L // T
            hd, hk = seq.hT["cm"]
            hv = hd.rearrange("(k p) t -> p k t", p=128)
            od, odk = seq.od[d]
            sups = range(nsup) if d == 0 else range(nsup - 1, -1, -1)
            for si in sups:
                t0 = si * T
                hw, hwk = hw2[0]
                gi += 1
                fw.dma(hw[:, :, 0:T], hv[:, :, 2 + t0:2 + t0 + T], reads=[hk], writes=[hwk])
                for h in range(4):
                    for k in range(8):
                        self.mm(psf[:, 0:T], Wf[:, k, h * 128:(h + 1) * 128], hw[:, k, 0:T], k == 0, k == 7, [Wfk, hwk], [psfk])
                    for k in range(8):
                        self.mm(psq[:, 0:T], Wq[:, k, h * 128:(h + 1) * 128], hw[:, k, 0:T], k == 0, k == 7, [Wqk, hwk], [psqk])
                    self.act(f_[:, 0:T], psf[:, 0:T], AF.Sigmoid, [psfk], [fk])
                    self.ts("dve", f_[:, 0:T], f_[:, 0:T], lb[:, 4 + h:5 + h], lb[:, h:h + 1], ALU.mult, ALU.add, [fk, lbk], [fk])
                    self.act(lf[:, 0:T], f_[:, 0:T], AF.Ln, [fk], [lfk])
                    self.ts("pool", kk[:, 0:T], f_[:, 0:T], -1.0, 1.0, ALU.mult, ALU.add, [fk], [kkk])
                    fw.op("dve", lambda: nc.vector.tensor_tensor_scan(out=cum[:, 0:T], data0=self.rst[:, 1, 0:T], data1=lf[:, 0:T], initial=0.0, op0=ALU.mult, op1=ALU.add), [self.rstk, lfk], [cumk])
                    c3 = cum[:, 0:T].rearrange("p (c t) -> p c t", t=64)
                    if d == 1:
                        self.cp("dve", ref[:, 8:8 + nch], cum[:, 63:T:64], [cumk], [refk])
                        self.tt("dve", c3, bc(ref[:, 8:8 + nch].unsqueeze(2), [128, nch, 64]), c3, ALU.subtract, [refk, cumk], [cumk])
                        self.tt("dve", cum[:, 0:T], cum[:, 0:T], lf[:, 0:T], ALU.add, [cumk, lfk], [cumk])
                    tcol = 63 if d == 0 else 0
                    self.cp("dve", ref[:, 0:nch], cum[:, 32:T:64], [cumk], [refk])
                    self.cp("dve", ref[:, 8:8 + nch], cum[:, tcol:T:64], [cumk], [refk])
                    E3 = E[:, 0:T].rearrange("p (c t) -> p c t", t=64)
                    self.tt("dve", E3, c3, bc(ref[:, 0:nch].unsqueeze(2), [128, nch, 64]), ALU.subtract, [cumk, refk], [Ek])
                    self.act(ex[:, 0, 0:T], E[:, 0:T], AF.Exp, [Ek], [exk])
                    self.act(ex[:, 1, 0:T], E[:, 0:T], AF.Exp, [Ek], [exk], scale=-1.0)
                    self.act(ex[:, 2, 0:T], cum[:, 0:T], AF.Exp, [cumk], [exk])
                    self.tt("dve", E3, bc(ref[:, 8:8 + nch].unsqueeze(2), [128, nch, 64]), c3, ALU.subtract, [cumk, refk], [Ek])
                    self.act(ex[:, 3, 0:T], E[:, 0:T], AF.Exp, [Ek], [exk])
                    self.act(etot[:, h, 0:nch], ref[:, 8:8 + nch], AF.Exp, [refk], [etotk])
                    self.tt("dve", qt[:, h, 0:T], psq[:, 0:T], ex[:, 0, 0:T], ALU.mult, [psqk, exk], [qtk])
                    self.tt("dve", qp[:, h, 0:T], psq[:, 0:T], ex[:, 2, 0:T], ALU.mult, [psqk, exk], [qpk])
                    self.tt("pool", kt[:, h, 0:T], kk[:, 0:T], ex[:, 1, 0:T], ALU.mult, [kkk, exk], [ktk])
                    self.tt("pool", kh[:, h, 0:T], kk[:, 0:T], ex[:, 3, 0:T], ALU.mult, [kkk, exk], [khk])
                    yield
                chs = list(range(nch) if d == 0 else range(nch - 1, -1, -1))
                for hb in range(0, nch, 4):
                    grp = chs[hb:hb + 4]
                    for gi_, ci in enumerate(grp):
                        c0 = ci * 64
                        for k in range(8):
                            self.mm(psv[0:64, :], hw[:, k, c0:c0 + 64], Wv[:, k, :], k == 0, k == 7, [hwk, Wvk], [psvk])
                        self.cp("act", vtok[:, gi_, :], psv[0:64, :], [psvk], [vtokk])
                        for h in range(4):
                            self.tr(pskh[0:64, h * 128:(h + 1) * 128], kh[:, h, c0:c0 + 64], self.idbf[:], [khk, self.idbfk], [pskhk])
                        self.cp("act", khtok[:, gi_, :], pskh[0:64, 0:512], [pskhk], [khtokk])
                        yield
                        for h in range(4):
                            self.mm(psa[0:64, h * 64:(h + 1) * 64], kt[:, h, c0:c0 + 64], qt[:, h, c0:c0 + 64], True, True, [ktk, qtk], [psak])
                        self.ts("dve", amf[:], psa[0:64, 0:256], 1.0e30, -1.0e30, ALU.min, ALU.max, [psak], [amfk])
                        self.tt("dve", am[:, gi_, :].rearrange("p (h t) -> p h t", h=4), amf[:].rearrange("p (h t) -> p h t", h=4), bc(mk.unsqueeze(1), [64, 4, 64]), ALU.mult, [amfk, self.maskk], [amk])
                        yield
                    for gi_, ci in enumerate(grp):
                        c0 = ci * 64
                        for h in range(4):
                            self.mm(pso[0:64, h * 128:(h + 1) * 128], am[:, gi_, h * 64:(h + 1) * 64], vtok[:, gi_, h * 128:(h + 1) * 128], True, False, [amk, vtokk], [psok])
                            self.mm(pso[0:64, h * 128:(h + 1) * 128], qp[:, h, c0:c0 + 64], Sbf[:, h, :], False, True, [qpk, Sbfk], [psok])
                        for h in range(4):
                            self.mm(psS[:, h * 128:(h + 1) * 128], khtok[:, gi_, h * 128:(h + 1) * 128], vtok[:, gi_, h * 128:(h + 1) * 128], True, True, [khtokk, vtokk], [psSk])
                        yield
                        ost, ostk = ost2[ci % 2]
                        self.cp("act", ost[:], pso[0:64, :], [psok], [ostk])
                        fw.dma(od[t0 + c0:t0 + c0 + 64, :], ost[:], reads=[ostk], writes=[odk])
                        for h in range(4):
                            self.stt(S[:, h, :], S[:, h, :], etot[:, h, ci:ci + 1], psS[:, h * 128:(h + 1) * 128], ALU.mult, ALU.add, [Sk, etotk, psSk], [Sk])
                        self.cp("act", Sbf[:], S[:], [Sk], [Sbfk])
                        yield

    def p_gla_comb(self, seq, l):
        fw, nc = self.fw, self.nc
        P = Pool(fw)
        W, Wk = P.sb([128, 8, 512], BF16)
        self.load_w(P, W, Wk, self.w_in[l][:, O_HGG:O_HGG + 512], 512)
        nw, nwk = P.sb([128, 512], F32)
        fw.dma(nw[:], self.hnw_bc[l], writes=[nwk])
        T = seq.T
        hw = [P.sb([128, 8, T], BF16) for _ in range(2)]
        of = [P.sb([128, 512], F32) for _ in range(2)]
        ob = [P.sb([128, 512], F32) for _ in range(2)]
        sg, sgk = P.sb([128, 512], F32)
        junk, junkk = P.sb([128, 128], BF16)
        ss, ssk = P.sb([128, 8], F32)
        yo = [P.sb([128, 512], BF16) for _ in range(2)]
        ps = [P.ps([128, 512]) for _ in range(2)]
        hd, hk = seq.hT["cm"]
        hv = hd.rearrange("(k p) t -> p k t", p=128)
        yd, ydk = seq.y
        per = T // 128
        ntile = seq.L // 128

        def load(i):
            h_, h_k = hw[(i // per) % 2]
            if i % per == 0:
                fw.dma(h_[:], hv[:, :, 2 + i * 128:2 + i * 128 + T], reads=[hk], writes=[h_k])
            f_, f_k = of[i % 2]
            b_, b_k = ob[i % 2]
            fw.dma(f_[:], seq.od[0][0][i * 128:(i + 1) * 128, :], reads=[seq.od[0][1]], writes=[f_k])
            fw.dma(b_[:], seq.od[1][0][i * 128:(i + 1) * 128, :], reads=[seq.od[1][1]], writes=[b_k])
        load(0)
        for i in range(ntile):
            h_, h_k = hw[(i // per) % 2]
            j = i % per
            f_, f_k = of[i % 2]
            b_, b_k = ob[i % 2]
            if i + 1 < ntile:
                load(i + 1)
            p_, p_k = ps[i % 2]
            for k in range(8):
                self.mm(p_[:, :], h_[:, k, j * 128:(j + 1) * 128], W[:, k, :], k == 0, k == 7, [h_k, Wk], [p_k])
            self.act(sg[:], p_[:], AF.Silu, [p_k], [sgk])
            self.tt("pool", f_[:], f_[:], b_[:], ALU.add, [f_k, b_k], [f_k])
            for h in range(4):
                self.act(junk[:], f_[:, h * 128:(h + 1) * 128], AF.Square, [f_k], [junkk, ssk], accum_out=ss[:, h:h + 1])
            self.ts("dve", ss[:, 4:8], ss[:, 0:4], 1.0 / 128, EPS, ALU.mult, ALU.add, [ssk], [ssk])
            self.act(ss[:, 4:8], ss[:, 4:8], AF.Sqrt, [ssk], [ssk])
            fw.op("dve", lambda: nc.vector.reciprocal(out=ss[:, 4:8], in_=ss[:, 4:8]), [ssk], [ssk])
            self.tt("pool", sg[:], sg[:], nw[:], ALU.mult, [sgk, nwk], [sgk])
            o_, o_k = yo[i % 2]
            for h in range(4):
                self.stt(o_[:, h * 128:(h + 1) * 128], f_[:, h * 128:(h + 1) * 128], ss[:, 4 + h:5 + h], sg[:, h * 128:(h + 1) * 128], ALU.mult, ALU.mult, [f_k, ssk, sgk], [o_k])
            for (p0, n, dstap) in self.x_tile_src(seq, yd, i, "cm"):
                fw.dma(dstap[:, 1024:1536], o_[p0:p0 + n, :], reads=[o_k], writes=[ydk])
            yield
        P.close()

    def p_hy_filter(self, seq, l):
        fw, nc = self.fw, self.nc
        Ls = seq.L
        nb2 = 2 * Ls // 128
        NP = 2 * Ls
        P = Pool(fw)
        w1, w1k = P.sb([33, 64], F32)
        w2, w2k = P.sb([64, 64], F32)
        w3, w3k = P.sb([64, 1024], F32)
        fv, fvk = P.sb([64, 4], F32)
        fb, fbk = P.sb([64, 2], F32)
        dl, dlk = P.sb([128, 512], F32)
        tn, tnk = P.sb([128, nb2], F32)
        ones, onesk = P.sb([128, 128], F32)
        fw.dma(w1[:], self.fw1[l], writes=[w1k])
        fw.dma(w2[:], self.fw2[l], writes=[w2k])
        fw.dma(w3[:], self.fw3[l], writes=[w3k])
        fw.dma(fv[:, 0:3], self.fvec[l], writes=[fvk])
        fw.dma(dl[:], self.c_delta, writes=[dlk])
        fw.dma(tn[:], self.c_tneg[Ls], writes=[tnk])
        self.ms("dve", ones[:], 1.0, [onesk])
        self.ts("dve", fb[:], fv[:, 0:2], fv[:, 2:3], None, ALU.mult, None, [fvk], [fbk])
        CH = 512
        zt = [P.sb([33, CH], F32) for _ in range(2)]
        h1, h1k = P.sb([64, CH], F32)
        msk, mskk = P.sb([64, CH], F32)
        h2, h2k = P.sb([64, CH], F32)
        ksb, ksbk = P.sb([128, 256], F32)
        dcy, dcyk = P.sb([128, 256], F32)
        kab, kabk = P.sb([128, 256], F32)
        Kf, Kfk = P.sb([128, 256, nb2], BF16)
        nacc, nacck = P.sb([128, 512], F32)
        self.ms("dve", nacc[:], 0.0, [nacck])
        ps1, ps1k = P.ps([128, 512])
        ps2, ps2k = P.ps([128, 512])
        psk, pskk = P.ps([128, 512])
        kd, kdk = seq.kmat
        zsrc = self.c_z[Ls]
        TWO_PI = 2.0 * math.pi
        for half in range(2):
            for ci in range(NP // CH):
                z_, z_k = zt[ci % 2]
                fw.dma(z_[:], zsrc[:, ci * CH:(ci + 1) * CH], writes=[z_k])
                self.mm(ps1[0:64, :], w1[:], z_[:], True, True, [w1k, z_k], [ps1k])
                for (hh, hhk, pp, ppk, col) in ((h1, h1k, ps1, ps1k, 0), (h2, h2k, ps2, ps2k, 1)):
                    if col == 1:
                        self.mm(ps2[0:64, :], w2[:], h1[:], True, True, [w2k, h1k], [ps2k])
                    self.act(hh[:], pp[0:64, :], AF.Identity, [ppk, fvk, fbk], [hhk], scale=fv[:, 2:3], bias=fb[:, col:col + 1])
                    self.ts("dve", msk[:], hh[:], math.pi, -TWO_PI, ALU.is_gt, ALU.mult, [hhk], [mskk])
                    self.tt("dve", hh[:], hh[:], msk[:], ALU.add, [hhk, mskk], [hhk])
                    self.ts("dve", msk[:], hh[:], -math.pi, TWO_PI, ALU.is_lt, ALU.mult, [hhk], [mskk])
                    self.tt("dve", hh[:], hh[:], msk[:], ALU.add, [hhk, mskk], [hhk])
                    self.act(hh[:], hh[:], AF.Sin, [hhk], [hhk])
                for j in range(CH // 128):
                    dd = ci * (CH // 128) + j
                    side = 1 if dd < nb2 // 2 else 0
                    c0 = side * 512 + half * 256
                    self.mm(psk[:, 0:256], h2[:, j * 128:(j + 1) * 128], w3[:, c0:c0 + 256], True, True, [h2k, w3k], [pskk])
                    self.act(dcy[:], dl[:, half * 256:(half + 1) * 256], AF.Exp, [dlk, tnk], [dcyk], scale=tn[:, dd:dd + 1])
                    self.tt("dve", ksb[:], psk[:, 0:256], dcy[:], ALU.mult, [pskk, dcyk], [ksbk])
                    self.cp("act", Kf[:, :, dd], ksb[:], [ksbk], [Kfk])
                    self.act(kab[:], ksb[:], AF.Abs, [ksbk], [kabk])
                    self.tt("pool", nacc[:, half * 256:(half + 1) * 256], nacc[:, half * 256:(half + 1) * 256], kab[:], ALU.add, [nacck, kabk], [nacck])
            for s4 in range(4):
                sg = half * 4 + s4
                fw.dma(kd[sg], Kf[:, s4 * 64:(s4 + 1) * 64, :], reads=[Kfk], writes=[kdk])
        self.mm(psk[:, :], ones[:], nacc[:], True, True, [onesk, nacck], [pskk])
        self.ts("dve", nacc[:], psk[:, :], EPS, None, ALU.add, None, [pskk], [nacck])
        fw.op("dve", lambda: nc.vector.reciprocal(out=nacc[:], in_=nacc[:]), [nacck], [nacck])
        fw.dma(seq.rn[0], nacc[:], reads=[nacck], writes=[seq.rn[1]])
        P.close()

    def p_hy_front(self, seq, l):
        fw, nc = self.fw, self.nc
        P = Pool(fw)
        W, Wk = P.sb([128, 8, 1536], BF16)
        Wg, Wgk = P.sb([128, 8, 512], BF16)
        self.load_w(P, W, Wk, self.w_in[l][:, O_HY:O_HY + 1536], 1536)
        self.load_w(P, Wg, Wgk, self.w_in[l][:, O_HYG:O_HYG + 512], 512)
        cw, cwk = P.sb([128, 12, 3], F32)
        cb, cbk = P.sb([128, 12], F32)
        hb, hbk = P.sb([128, 512], F32)
        fw.dma(cw[:], self.hcw_fm[l], writes=[cwk])
        fw.dma(cb[:], self.hcb_fm[l], writes=[cbk])
        fw.dma(hb[:], self.hyb_bc[l], writes=[hbk])
        T = seq.T
        nt = T // 128
        hw = [P.sb([128, 8, T + 2], BF16) for _ in range(2)]
        ps = [P.ps([128, 1024]) for _ in range(2)]
        cv, cvk = P.sb([128, 12, T], F32)
        wf, wfk = P.sb([128, 4, T], F32)
        wb = [P.sb([128, 4, T], BF16) for _ in range(2)]
        psg, psgk = P.ps([128, 512])
        psT, psTk = P.ps([128, 1024])
        sg, sgk = P.sb([128, 512], F32)
        o1 = [P.sb([128, 512], F32) for _ in range(2)]
        o2 = [P.sb([128, 512], F32) for _ in range(2)]
        hd, hk = seq.hT["rm"]
        hv = hd.rearrange("(k p) t -> p k t", p=128)
        wd, wdk = seq.wT
        wdv = wd.rearrange("(b p) t -> p b t", p=128)
        def load(si):
            h_, h_k = hw[si % 2]
            fw.dma(h_[:], hv[:, :, 1 + si * T:1 + si * T + T + 2], reads=[hk], writes=[h_k])
        load(0)
        for si in range(seq.L // T):
            t0 = si * T
            h_, h_k = hw[si % 2]
            if si + 1 < seq.L // T:
                load(si + 1)
            for b in range(12):
                p_, p_k = ps[b % 2]
                for k in range(8):
                    self.mm(p_[:, 0:T], W[:, k, b * 128:(b + 1) * 128], h_[:, k, 0:T], k == 0, k == 7, [Wk, h_k], [p_k])
                for k in range(8):
                    self.mm(p_[:, T:T + 2], W[:, k, b * 128:(b + 1) * 128], h_[:, k, T:T + 2], k == 0, k == 7, [Wk, h_k], [p_k])
                self.act(cv[:, b, :], p_[:, 0:T], AF.Identity, [p_k, cwk, cbk], [cvk], scale=cw[:, b, 0:1], bias=cb[:, b:b + 1])
                for k in range(1, 3):
                    self.stt(cv[:, b, :], p_[:, k:k + T], cw[:, b, k:k + 1], cv[:, b, :], ALU.mult, ALU.add, [p_k, cwk, cvk], [cvk])
            self.tt("dve", wf[:], cv[:, 8:12, :], cv[:, 0:4, :], ALU.mult, [cvk], [wfk])
            w_, w_k = wb[si % 2]
            self.cp("pool", w_[:], wf[:], [wfk], [w_k])
            fw.dma(wdv[:, :, 127 + t0:127 + t0 + T], w_[:], reads=[w_k], writes=[wdk])
            for j in range(nt):
                for b in range(4):
                    self.tr(psT[:, b * 128:(b + 1) * 128], cv[:, 4 + b, j * 128:(j + 1) * 128], self.id32[:], [cvk, self.id32k], [psTk])
                    self.tr(psT[:, 512 + b * 128:512 + (b + 1) * 128], wf[:, b, j * 128:(j + 1) * 128], self.id32[:], [wfk, self.id32k], [psTk])
                for k in range(8):
                    self.mm(psg[:, :], h_[:, k, 1 + j * 128:1 + (j + 1) * 128], Wg[:, k, :], k == 0, k == 7, [h_k, Wgk], [psgk])
                self.act(sg[:], psg[:], AF.Silu, [psgk], [sgk])
                a_, a_k = o1[j % 2]
                b_, b_k = o2[j % 2]
                self.tt("dve", a_[:], psT[:, 0:512], sg[:], ALU.mult, [psTk, sgk], [a_k])
                self.tt("dve", b_[:], psT[:, 512:1024], hb[:], ALU.mult, [psTk, hbk], [b_k])
                self.tt("pool", b_[:], b_[:], a_[:], ALU.mult, [b_k, a_k], [b_k])
                r0 = t0 + j * 128
                fw.dma(seq.hx0g[0][r0:r0 + 128, :], a_[:], reads=[a_k], writes=[seq.hx0g[1]])
                fw.dma(seq.hbw[0][r0:r0 + 128, :], b_[:], reads=[b_k], writes=[seq.hbw[1]])
        P.close()

    def p_hy_conv(self, seqs, l):
        fw, nc = self.fw, self.nc
        Lm = max(s.L for s in seqs)
        nbm = Lm // 128
        P = Pool(fw)
        self.hyP = P
        kms = {}
        for s_ in seqs:
            if s_.L not in kms:
                kms[s_.L] = P.sb([128, 64, 2 * s_.L // 128], BF16)
        rn, rnk = P.sb([128, 512], F32)
        wsh = [P.sb([128, Lm + 128], BF16) for _ in range(3)]
        yg, ygk = P.sb([128, nbm, 64], BF16)
        IBm = min(nbm, 8)
        hx = [P.sb([128, IBm, 64], F32) for _ in range(2)]
        hbw = [P.sb([128, IBm, 64], F32) for _ in range(2)]
        yo = [P.sb([128, IBm, 64], BF16) for _ in range(2)]
        pys = [P.ps([128, 512]) for _ in range(3)]
        yield
        gc = 0
        for seq in seqs:
            Ls = seq.L
            nb = Ls // 128
            IB = min(nb, 8)
            fw.dma(rn[:], seq.rn[0], reads=[seq.rn[1]], writes=[rnk])
            km, kmk = kms[Ls]
            wd, wdk = seq.wT
            yd, ydk = seq.y
            WL = Ls + 256
            for sg in range(8):
                fw.dma(km[:], seq.kmat[0][sg], reads=[seq.kmat[1]], writes=[kmk])
                for c in range(64):
                    ch = sg * 64 + c
                    w_, w_k = wsh[gc % 3]
                    src = bass.AP(tensor=wd.tensor, offset=ch * WL, ap=[[1, 128], [1, Ls + 128]])
                    fw.dma(w_[:, 0:Ls + 128], src, reads=[wdk], writes=[w_k])
                    pyt, pyk = pys[gc % 3]
                    gc += 1
                    py = pyt[:, 0:nb]
                    for j in range(nb + 1):
                        self.mm(py, w_[:, j * 128:(j + 1) * 128], km[:, c, nb - j:2 * nb - j], j == 0, j == nb, [w_k, kmk], [pyk])
                        if j % 13 == 12:
                            yield
                    self.act(yg[:, 0:nb, c], py, AF.Copy, [pyk, rnk], [ygk], scale=rn[:, ch:ch + 1])
                    yield
                for ib in range(nb // IB):
                    a_, a_k = hx[ib % 2]
                    b_, b_k = hbw[ib % 2]
                    o_, o_k = yo[ib % 2]
                    r0 = ib * IB * 128
                    fw.dma(a_[:, 0:IB, :], seq.hx0g[0][r0:r0 + IB * 128, sg * 64:(sg + 1) * 64].rearrange("(i p) c -> p i c", p=128), reads=[seq.hx0g[1]], writes=[a_k])
                    fw.dma(b_[:, 0:IB, :], seq.hbw[0][r0:r0 + IB * 128, sg * 64:(sg + 1) * 64].rearrange("(i p) c -> p i c", p=128), reads=[seq.hbw[1]], writes=[b_k])
                    self.tt("dve", a_[:, 0:IB, :], a_[:, 0:IB, :], yg[:, ib * IB:(ib + 1) * IB, :], ALU.mult, [a_k, ygk], [a_k])
                    self.tt("pool", o_[:, 0:IB, :], a_[:, 0:IB, :], b_[:, 0:IB, :], ALU.add, [a_k, b_k], [o_k])
                    fw.dma(yd[r0:r0 + IB * 128, 1536 + sg * 64:1536 + (sg + 1) * 64].rearrange("(i p) c -> p i c", p=128), o_[:, 0:IB, :], reads=[o_k], writes=[ydk])
                    yield

    @staticmethod
    def _drain(g):
        if g is not None:
            for _ in g:
                pass

    def _corun(self, main_gens, bg, ratio=2):
        bg_alive = bg is not None
        for g in main_gens:
            if g is None:
                continue
            for _ in g:
                if bg_alive:
                    for _r in range(ratio):
                        try:
                            next(bg)
                        except StopIteration:
                            bg_alive = False
                            break
        if bg_alive:
            self._drain(bg)

    def _close_pools(self):
        while self._pools:
            self._pools.pop().close()

    def build(self):
        self.setup()
        self._pools = []
        dp = self.depth
        sc, sl = self.seqs
        xc, xck = sc.x, sc.xk
        xl, xlk = sl.x, sl.xk
        for l in range(dp):
            last = l == dp - 1
            act_seqs = [sl] if last else [sc, sl]
            self.p_adaln(l)
            self.p_norm(sc, xc, xck, "rm")
            self.p_norm(sl, xl, xlk, "rm")
            self.p_norm(sl, xl, xlk, "cm")
            for s in act_seqs:
                self.p_hy_filter(s, l)
                self.p_hy_front(s, l)
            for s in (sc, sl):
                self.p_ssd_front(s, l)
            g1 = self.p_ssd_scan(l, 1)
            next(g1)
            g0 = self.p_ssd_scan(l, 0)
            self._corun([g0], g1, ratio=1)
            self._close_pools()
            SP_ = Pool(self.fw)
            Wq = SP_.sb([128, 8, 512], BF16)
            Wv = SP_.sb([128, 8, 512], BF16)
            self.load_w(SP_, Wq[0], Wq[1], self.w_in[l][:, O_HQ:O_HQ + 512], 512)
            self.load_w(SP_, Wv[0], Wv[1], self.w_in[l][:, O_HGI:O_HGI + 512], 512)
            self._gla_shared = (Wq, Wv)
            g1 = self.p_gla_scan(l, 1)
            next(g1)
            g0 = self.p_gla_scan(l, 0)
            self._corun([g0], g1, ratio=1)
            self._close_pools()
            SP_.close()
            for s in act_seqs:
                self._drain(self.p_ssd_comb(s, l))
                self._drain(self.p_gla_comb(s, l))
            bg = self.p_hy_conv(act_seqs, l)
            self._drain(bg)
            self.hyP.close()
            if not last:
                self.p_out(sc, l, xc, xck, sc.x1[0], sc.x1[1], False)
                self.p_out(sl, l, xl, xlk, sl.x1[0], sl.x1[1], False)
                xc, xck = sc.x1
                xl, xlk = sl.x1
            else:
                self.p_out(sl, l, xl, xlk, self.out, self.outk, True)
        self.GP.close()
        self.fw.finish()
        return self.nc


def _consts(L, LC):
    import ml_dtypes
    c = {}
    c["c_id32"] = np.eye(128, dtype=np.float32)
    c["c_idbf"] = np.eye(128, dtype=np.float32).astype(ml_dtypes.bfloat16)
    s = np.arange(128)[:, None]
    t = np.arange(128)[None, :]
    c["c_mask"] = np.stack([(s <= t), (s >= t)]).astype(np.float32)
    sel = np.zeros((2, 256), np.float32)
    sel[0, 0:128] = 1.0
    sel[1, 128:256] = 1.0
    c["c_sel"] = sel
    rst = np.ones((128, 2, 512), np.float32)
    rst[:, 0, 0::128] = 0.0
    rst[:, 1, 0::64] = 0.0
    c["c_rst"] = rst
    delta = np.abs(np.linspace(HY_MIN_DECAY, HY_MAX_DECAY, 512, dtype=np.float32))
    c["c_delta"] = np.ascontiguousarray(np.broadcast_to(delta[None, :], (128, 512))).astype(np.float32)
    for Ls in sorted(set([L, LC])):
        nb = Ls // 128
        dd = np.arange(2 * nb)[None, :]
        rp = np.arange(128)[:, None]
        lam = 128 * (dd - nb) + 127 - rp
        pos = np.abs(lam)
        invalid = pos >= Ls
        pos = np.where(invalid, 0, pos)
        tl = np.linspace(0.0, 1.0, Ls, dtype=np.float32)
        tt = tl[pos].astype(np.float32)
        tneg = np.where(invalid, -1.0e4, -tt).astype(np.float32)
        c["c_tneg%d" % Ls] = np.ascontiguousarray(tneg)
        posn = pos.T.reshape(-1)
        bands = np.linspace(1e-4, 15.0, 16, dtype=np.float32)
        ang = (np.float32(2.0 * math.pi / Ls) * posn.astype(np.float32)[:, None] * bands[None, :]).astype(np.float32)
        z = np.concatenate([tl[posn][:, None], np.cos(ang.astype(np.float64)), -np.sin(ang.astype(np.float64))], axis=1)
        c["c_z%d" % Ls] = np.ascontiguousarray(z.T).astype(np.float32)
    return c


def _layout_inputs(inp, b, L, LC, depth):
    f = lambda a: np.ascontiguousarray(np.asarray(a, dtype=np.float32))
    m = {}
    m["x"] = f(inp["x"][b])
    m["ctx"] = f(inp["ctx"][b])
    c2 = np.stack([np.asarray(inp["c"][b]).reshape(128, 8), np.asarray(inp["c_ctx"]).reshape(128, 8)], axis=-1)
    m["c2"] = f(c2)
    m["ada_w"] = f(inp["ada_w"])
    m["ada_b2"] = f(np.broadcast_to(np.asarray(inp["ada_b"])[:, None, :], (depth, 2, 3 * D)))
    m["norm_w2"] = f(np.broadcast_to(np.asarray(inp["norm_w"])[:, None, :], (depth, 2, D)))
    m["fnw_bc"] = f(np.broadcast_to(np.asarray(inp["final_norm_w"])[None, :], (128, D)))
    m["w_in"] = f(inp["w_in"])
    m["w_out"] = f(inp["w_out"])
    cw = np.asarray(inp["ssd_conv_w"])
    m["cw_fm"] = f(cw.reshape(depth, 5, 12, 128).transpose(0, 3, 2, 1))
    m["cb_fm"] = f(np.asarray(inp["ssd_conv_b"]).reshape(depth, 12, 128).transpose(0, 2, 1))
    hcw = np.asarray(inp["hy_conv_w"])
    m["hcw_fm"] = f(hcw.reshape(depth, 3, 12, 128).transpose(0, 3, 2, 1))
    m["hcb_fm"] = f(np.asarray(inp["hy_conv_b"]).reshape(depth, 12, 128).transpose(0, 2, 1))
    m["dtb_fm"] = f(np.asarray(inp["ssd_dt_bias"]).reshape(depth, 32, 1))
    m["alog_fm"] = f(np.asarray(inp["ssd_a_log"]).reshape(depth, 32, 1))
    m["d_bc"] = f(np.broadcast_to(np.asarray(inp["ssd_d"])[:, None, :], (depth, 128, 16)))
    m["snw_bc"] = f(np.broadcast_to(np.asarray(inp["ssd_norm_w"])[:, None, :], (depth, 128, D)))
    lbl = np.asarray(inp["hg_lb_logits"])
    m["lbl_fm"] = f(lbl.reshape(2, depth, 4, 128).transpose(3, 0, 1, 2))
    m["hnw_bc"] = f(np.broadcast_to(np.asarray(inp["hg_norm_w"])[:, None, :], (depth, 128, 512)))
    m["hyb_bc"] = f(np.broadcast_to(np.asarray(inp["hy_bias"])[:, None, :], (depth, 128, 512)))
    m["fw1"] = f(inp["hy_filt_w1"])
    m["fw2"] = f(inp["hy_filt_w2"])
    m["fw3"] = f(inp["hy_filt_w3"])
    m["fvec"] = f(np.stack([np.asarray(inp["hy_filt_b1"]), np.asarray(inp["hy_filt_b2"]), np.asarray(inp["hy_filt_freq"])], axis=-1))
    return m


_CACHE = {}


def run(inputs, L, LC, depth, nb, debug=()):
    key = (L, LC, depth, tuple(debug))
    if key not in _CACHE:
        bld = Builder(L, LC, depth, debug)
        bld.build()
        _CACHE[key] = bld
    bld = _CACHE[key]
    consts = _consts(L, LC)
    in_maps = []
    for b in range(nb):
        m = _layout_inputs(inputs, b, L, LC, depth)
        m.update(consts)
        in_maps.append({k: m[k] for k in bld.inputs})
    res = run_bass_kernel_spmd(bld.nc, in_maps, core_ids=list(range(nb)))
    return res, bld


def kernel(**inputs):
    res, bld = run(inputs, 8192, 256, 2, 8)
    out = np.stack([np.asarray(r["out"], dtype=np.float32) for r in res.results], axis=0)
    return out
```
